# Optimizing a Trainium2 kernel written in Bass

```python
import math
import jax
import jax.numpy as jnp
from jax import lax
import numpy as np

D_MODEL = 1024
BATCH = 4
SEQ = 8192
DEPTH = 2

HEAD_DIM = 64
EPS = 1e-6
NEG_INF = -1e30
GRID_W = 64
A_HEADS = 8
A_KV_HEADS = 2
A_GROUP = A_HEADS // A_KV_HEADS
Q_BLOCK = 128
ROPE_THETA = 10000.0
B_GROUPS = 3
B_HEADS_PER_GROUP = 4
B_HEADS = B_GROUPS * B_HEADS_PER_GROUP
B_WINDOWS = (128, 512, 2048)
B_DILATIONS = (1, 4, 16)
N_BUCKETS = 32
MAX_DISTANCE = 1024
C_HEADS = 8
C_HEAD_DIM = 64
C_INNER = C_HEADS * C_HEAD_DIM
C_GROUPS = 2
C_STATE = 128
C_CONV = 5
C_CHUNK = 128
A_Q = A_HEADS * HEAD_DIM
A_KV = A_KV_HEADS * HEAD_DIM
A_OUT = A_Q
B_QKV = B_HEADS * HEAD_DIM
B_OUT = B_HEADS_PER_GROUP * HEAD_DIM
C_BC = C_GROUPS * C_STATE
C_XBC = C_INNER + 2 * C_BC
N_BRANCH = 3
SPLIT_WIDTHS = (A_Q, A_KV, A_KV, A_OUT,
                B_QKV, B_QKV, B_QKV, B_OUT,
                C_INNER, C_INNER, C_BC, C_BC, C_HEADS, C_HEADS,
                N_BRANCH * D_MODEL)
D_IN_PROJ = sum(SPLIT_WIDTHS)
SPLIT_POINTS = tuple(int(v) for v in np.cumsum(SPLIT_WIDTHS)[:-1])

kernel_name = "hybrid_gated_parallel_encoder"


def rms_norm(x, w):
    xf = x.astype(jnp.float32)
    y = xf * lax.rsqrt(jnp.mean(xf * xf, axis=-1, keepdims=True) + EPS)
    return y.astype(x.dtype) * w.astype(x.dtype)


def axial_rope(x, row_idx, col_idx):
    half = x.shape[-1] // 2
    quarter = half // 2
    freqs = ROPE_THETA ** (-jnp.arange(quarter, dtype=jnp.float32) / quarter)

    def rotate(seg, pos):
        ang = pos.astype(jnp.float32)[:, None] * freqs
        cos = jnp.cos(ang)[:, None, :].astype(seg.dtype)
        sin = jnp.sin(ang)[:, None, :].astype(seg.dtype)
        a, b = seg[..., :quarter], seg[..., quarter:]
        return jnp.concatenate([a * cos - b * sin, b * cos + a * sin], axis=-1)

    return jnp.concatenate([rotate(x[..., :half], row_idx), rotate(x[..., half:], col_idx)], axis=-1)


def t5_bucket(rel):
    nb = N_BUCKETS // 2
    max_exact = nb // 2
    ret = jnp.where(rel > 0, nb, 0)
    n = jnp.abs(rel)
    nf = jnp.maximum(n, 1).astype(jnp.float32)
    large = max_exact + (jnp.log(nf / max_exact) / math.log(MAX_DISTANCE / max_exact)
                         * (nb - max_exact)).astype(jnp.int32)
    large = jnp.minimum(large, nb - 1)
    return ret + jnp.where(n < max_exact, n, large)


def mixer_a(q, k, v, q_norm, k_norm, row_idx, col_idx):
    Bsz, S, _ = q.shape
    q = axial_rope(rms_norm(q.reshape(Bsz, S, A_HEADS, HEAD_DIM), q_norm), row_idx, col_idx)
    k = axial_rope(rms_norm(k.reshape(Bsz, S, A_KV_HEADS, HEAD_DIM), k_norm), row_idx, col_idx)
    v = v.reshape(Bsz, S, A_KV_HEADS, HEAD_DIM)
    n_qb = S // Q_BLOCK
    qb = q.reshape(Bsz, n_qb, Q_BLOCK, A_KV_HEADS, A_GROUP, HEAD_DIM).transpose(1, 0, 3, 4, 2, 5)
    k = k.transpose(0, 2, 1, 3)
    v = v.transpose(0, 2, 1, 3)
    scale = HEAD_DIM ** -0.5

    def block(qblk):
        s = jnp.einsum('bkgqe,bkse->bkgqs', qblk, k, preferred_element_type=jnp.float32) * scale
        p = jax.nn.softmax(s, axis=-1)
        return jnp.einsum('bkgqs,bkse->bkgqe', p.astype(v.dtype), v)

    o = lax.map(block, qb)
    return o.transpose(1, 0, 4, 2, 3, 5).reshape(Bsz, S, A_OUT)


def dilated_group_attention(q, k, v, bias_table, dilation, side):
    Bsz, S, H, E = q.shape
    M = S // dilation
    nb = -(-M // side)
    Mp = nb * side

    def phases(t):
        return t.reshape(Bsz, M, dilation, H, E).transpose(0, 2, 3, 1, 4)

    qd = jnp.pad(phases(q), ((0, 0), (0, 0), (0, 0), (0, Mp - M), (0, 0)))
    qd = qd.reshape(Bsz, dilation, H, nb, side, E)

    def band(t):
        t = jnp.pad(phases(t), ((0, 0), (0, 0), (0, 0), (side, side + Mp - M), (0, 0)))
        t = t.reshape(Bsz, dilation, H, nb + 2, side, E)
        return jnp.concatenate([t[:, :, :, :-2], t[:, :, :, 1:-1], t[:, :, :, 2:]], axis=4)

    kw, vw = band(k), band(v)
    qi = jnp.arange(side)[:, None]
    kk = jnp.arange(3 * side)[None, :]
    rel = kk - side - qi
    m_k = jnp.arange(nb)[:, None, None] * side + kk[None] - side
    valid = (jnp.abs(rel) <= side)[None] & (m_k >= 0) & (m_k < M)
    bias = bias_table[t5_bucket(rel * dilation)].transpose(2, 0, 1)
    s = jnp.einsum('bdhnqe,bdhnke->bdhnqk', qd, kw, preferred_element_type=jnp.float32) * (E ** -0.5)
    s = s + bias[None, None, :, None].astype(jnp.float32)
    s = jnp.where(valid, s, NEG_INF)
    mx = jnp.max(s, axis=-1, keepdims=True)
    e = jnp.exp(s - mx)
    den = jnp.sum(e, axis=-1, keepdims=True)
    p = e / den
    lse = (mx + jnp.log(den))[..., 0]
    o = jnp.einsum('bdhnqk,bdhnke->bdhnqe', p.astype(v.dtype), vw)
    o = o.reshape(Bsz, dilation, H, Mp, E)[:, :, :, :M].transpose(0, 3, 1, 2, 4).reshape(Bsz, S, H, E)
    lse = lse.reshape(Bsz, dilation, H, Mp)[..., :M].transpose(0, 3, 1, 2).reshape(Bsz, S, H)
    return o, lse


def mixer_b(q, k, v, q_norm, k_norm, rel_bias):
    Bsz, S, _ = q.shape
    shp = (Bsz, S, B_GROUPS, B_HEADS_PER_GROUP, HEAD_DIM)
    q = rms_norm(q.reshape(shp), q_norm)
    k = rms_norm(k.reshape(shp), k_norm)
    v = v.reshape(shp)
    outs, lses = [], []
    for g in range(B_GROUPS):
        d = B_DILATIONS[g]
        side = B_WINDOWS[g] // (2 * d)
        tbl = rel_bias[:, g * B_HEADS_PER_GROUP:(g + 1) * B_HEADS_PER_GROUP]
        o, l = dilated_group_attention(q[:, :, g], k[:, :, g], v[:, :, g], tbl, d, side)
        outs.append(o)
        lses.append(l)
    w = jax.nn.softmax(jnp.stack(lses, axis=0), axis=0)
    o = jnp.sum(w[..., None].astype(outs[0].dtype) * jnp.stack(outs, axis=0), axis=0)
    return o.reshape(Bsz, S, B_OUT)


def segsum(x):
    T = x.shape[-1]
    xr = jnp.broadcast_to(x[..., None], x.shape + (T,))
    strict = jnp.tril(jnp.ones((T, T), dtype=bool), -1)
    cs = jnp.cumsum(jnp.where(strict, xr, 0.0), axis=-2)
    return jnp.where(jnp.tril(jnp.ones((T, T), dtype=bool)), cs, -jnp.inf)


def ssd_chunked(xdt, a, Bm, Cm):
    b, l, h, p = xdt.shape
    n = Bm.shape[-1]
    c = l // C_CHUNK
    x = xdt.astype(jnp.float32).reshape(b, c, C_CHUNK, h, p)
    Bc = Bm.astype(jnp.float32).reshape(b, c, C_CHUNK, h, n)
    Cc = Cm.astype(jnp.float32).reshape(b, c, C_CHUNK, h, n)
    A = a.astype(jnp.float32).reshape(b, c, C_CHUNK, h).transpose(0, 3, 1, 2)
    A_cs = jnp.cumsum(A, axis=-1)
    Lm = jnp.exp(segsum(A))
    CB = jnp.einsum('bclhn,bcshn->bhcls', Cc, Bc)
    y_diag = jnp.einsum('bhcls,bcshp->bclhp', CB * Lm, x)
    decay_states = jnp.exp(A_cs[..., -1:] - A_cs)
    states = jnp.einsum('bclhn,bhcl,bclhp->bchpn', Bc, decay_states, x)
    states = jnp.concatenate([jnp.zeros_like(states[:, :1]), states], axis=1)
    decay_chunk = jnp.exp(segsum(jnp.pad(A_cs[..., -1], ((0, 0), (0, 0), (1, 0)))))
    states = jnp.einsum('bhzc,bchpn->bzhpn', decay_chunk, states)[:, :-1]
    y_off = jnp.einsum('bclhn,bchpn,bhcl->bclhp', Cc, states, jnp.exp(A_cs))
    return (y_diag + y_off).reshape(b, l, h, p)


def mixer_c(xc, zc, bc, cc, dtf, dtb, conv_w, conv_b, a_log, dt_bias, d_skip, norm_w):
    Bsz, S, _ = xc.shape
    xbc = jnp.concatenate([xc, bc, cc], axis=-1)
    xbc = lax.conv_general_dilated(xbc, conv_w[:, None, :].astype(xbc.dtype), (1,),
                                   [(C_CONV // 2, C_CONV // 2)],
                                   dimension_numbers=('NWC', 'WIO', 'NWC'),
                                   feature_group_count=C_XBC)
    xbc = jax.nn.silu(xbc + conv_b.astype(xbc.dtype))
    xs, Bm, Cm = jnp.split(xbc, [C_INNER, C_INNER + C_BC], axis=-1)
    xs = xs.reshape(Bsz, S, C_HEADS, C_HEAD_DIM)
    rep = C_HEADS // C_GROUPS
    Bm = jnp.repeat(Bm.reshape(Bsz, S, C_GROUPS, C_STATE), rep, axis=2)
    Cm = jnp.repeat(Cm.reshape(Bsz, S, C_GROUPS, C_STATE), rep, axis=2)
    A = -jnp.exp(a_log.astype(jnp.float32))
    dt_f = jax.nn.softplus(dtf.astype(jnp.float32) + dt_bias[0].astype(jnp.float32))
    dt_b = jax.nn.softplus(dtb.astype(jnp.float32) + dt_bias[1].astype(jnp.float32))
    y_f = ssd_chunked(xs * dt_f[..., None], dt_f * A[0], Bm, Cm)
    flip = lambda t: jnp.flip(t, axis=1)
    y_b = flip(ssd_chunked(flip(xs * dt_b[..., None]), flip(dt_b * A[1]), flip(Bm), flip(Cm)))
    y = y_f + y_b + d_skip.astype(jnp.float32)[:, None] * xs
    y = y.reshape(Bsz, S, C_INNER).astype(xc.dtype)
    return rms_norm(y * jax.nn.silu(zc), norm_w)


def setup_inputs(seed: int = 0) -> dict:
    key = jax.random.key(seed)
    ks = jax.random.split(key, 24)
    f32 = jnp.float32

    def nrm(k, shape, s):
        return jax.random.normal(k, shape, f32) * s

    x = nrm(ks[0], (BATCH, SEQ, D_MODEL), 1.0)
    c = nrm(ks[1], (BATCH, D_MODEL), 1.0)
    norm_w = 1.0 + nrm(ks[2], (DEPTH, D_MODEL), 0.1)
    w_ada = nrm(ks[3], (DEPTH, D_MODEL, 3 * D_MODEL), 0.5 * D_MODEL ** -0.5)
    b_ada = nrm(ks[4], (DEPTH, 3 * D_MODEL), 0.02)
    w_in = nrm(ks[5], (DEPTH, D_MODEL, D_IN_PROJ), D_MODEL ** -0.5)
    b_gate = nrm(ks[6], (DEPTH, N_BRANCH * D_MODEL), 0.1)
    q_norm_a = 1.0 + nrm(ks[7], (DEPTH, HEAD_DIM), 0.1)
    k_norm_a = 1.0 + nrm(ks[8], (DEPTH, HEAD_DIM), 0.1)
    q_norm_b = 1.0 + nrm(ks[9], (DEPTH, HEAD_DIM), 0.1)
    k_norm_b = 1.0 + nrm(ks[10], (DEPTH, HEAD_DIM), 0.1)
    rel_bias = nrm(ks[11], (N_BUCKETS, B_HEADS), 0.5)
    conv_w = nrm(ks[12], (DEPTH, C_CONV, C_XBC), C_CONV ** -0.5)
    conv_b = nrm(ks[13], (DEPTH, C_XBC), 0.02)
    a_log = jnp.log(jax.random.uniform(ks[14], (DEPTH, 2, C_HEADS), f32, 1.0, 16.0))
    dt0 = jnp.exp(jax.random.uniform(ks[15], (DEPTH, 2, C_HEADS), f32, math.log(1e-3), math.log(1e-1)))
    dt_bias = dt0 + jnp.log(-jnp.expm1(-dt0))
    d_skip = 1.0 + nrm(ks[16], (DEPTH, C_HEADS), 0.1)
    ssm_norm_w = 1.0 + nrm(ks[17], (DEPTH, C_INNER), 0.1)
    w_proj_a = nrm(ks[18], (DEPTH, A_OUT, D_MODEL), A_OUT ** -0.5)
    w_proj_b = nrm(ks[19], (DEPTH, B_OUT, D_MODEL), B_OUT ** -0.5)
    w_proj_c = nrm(ks[20], (DEPTH, C_INNER, D_MODEL), C_INNER ** -0.5)
    w_out = nrm(ks[21], (DEPTH, D_MODEL, D_MODEL), D_MODEL ** -0.5)
    return {"x": x, "c": c, "norm_w": norm_w, "w_ada": w_ada, "b_ada": b_ada,
            "w_in": w_in, "b_gate": b_gate, "q_norm_a": q_norm_a, "k_norm_a": k_norm_a,
            "q_norm_b": q_norm_b, "k_norm_b": k_norm_b, "rel_bias": rel_bias,
            "conv_w": conv_w, "conv_b": conv_b, "a_log": a_log, "dt_bias": dt_bias,
            "d_skip": d_skip, "ssm_norm_w": ssm_norm_w, "w_proj_a": w_proj_a,
            "w_proj_b": w_proj_b, "w_proj_c": w_proj_c, "w_out": w_out}


def reference(x, c, norm_w, w_ada, b_ada, w_in, b_gate, q_norm_a, k_norm_a, q_norm_b,
              k_norm_b, rel_bias, conv_w, conv_b, a_log, dt_bias, d_skip, ssm_norm_w,
              w_proj_a, w_proj_b, w_proj_c, w_out):
    Bsz, S, _ = x.shape
    rows = S // GRID_W
    row_idx = jnp.repeat(jnp.arange(rows), GRID_W)
    col_idx = jnp.tile(jnp.arange(GRID_W), rows)
    c_act = jax.nn.silu(c)
    for l in range(DEPTH):
        mod = c_act @ w_ada[l] + b_ada[l]
        shift, scale, gate = jnp.split(mod, 3, axis=-1)
        h = rms_norm(x, norm_w[l]) * (1.0 + scale[:, None, :]) + shift[:, None, :]
        proj = h @ w_in[l]
        (qa, ka, va, ga, qb, kb, vb, gb, xc, zc, bc, cc, dtf, dtb,
         mg) = jnp.split(proj, SPLIT_POINTS, axis=-1)
        ya = mixer_a(qa, ka, va, q_norm_a[l], k_norm_a[l], row_idx, col_idx) * jax.nn.silu(ga)
        yb = mixer_b(qb, kb, vb, q_norm_b[l], k_norm_b[l], rel_bias) * jax.nn.silu(gb)
        yc = mixer_c(xc, zc, bc, cc, dtf, dtb, conv_w[l], conv_b[l], a_log[l], dt_bias[l],
                     d_skip[l], ssm_norm_w[l])
        g_a, g_b, g_c = jnp.split(jax.nn.sigmoid(mg + b_gate[l]), N_BRANCH, axis=-1)
        merged = (g_a * (ya @ w_proj_a[l]) + g_b * (yb @ w_proj_b[l])
                  + g_c * (yc @ w_proj_c[l]))
        x = x + gate[:, None, :] * (merged @ w_out[l])
    return x
```

```python
from contextlib import ExitStack
import math
import numpy as np
import ml_dtypes
import concourse.bass as bass
import concourse.mybir as mybir
from concourse.bass_utils import run_bass_kernel_spmd

F32 = mybir.dt.float32
BF16 = mybir.dt.bfloat16
ALU = mybir.AluOpType
AF = mybir.ActivationFunctionType
AX = mybir.AxisListType

D = 1024
DEPTH = 2
EPS = 1e-6
NCOL1 = 5392
D_IN = 8464
GRID_W = 64
B_DIL = (1, 4, 16)


class Trk:
    __slots__ = ("w", "r")

    def __init__(self):
        self.w = []
        self.r = []


class PTrk(Trk):
    __slots__ = ()


class KB:
    def __init__(self, nc, es):
        self.nc = nc
        self.es = es
        self.eng = {"pe": nc.tensor, "act": nc.scalar, "dve": nc.vector, "pool": nc.gpsimd, "sp": nc.sync}
        self.semh = {}
        self.cnt = {}
        self.seen = {e: {} for e in self.eng}
        for e in ("pe", "act", "dve", "pool"):
            self.semh[e] = es.enter_context(nc.semaphore("s_" + e))
            self.cnt[e] = 0
        self.dq = {}
        for q, n in (("sp", 12), ("pool", 8), ("act", 4)):
            names = []
            for i in range(n):
                nm = "d_%s%d" % (q, i)
                self.semh[nm] = es.enter_context(nc.semaphore(nm))
                self.cnt[nm] = 0
                names.append(nm)
            self.dq[q] = [names, 0]
        self.uid = 0

    def sb(self, shape, dtype, name=None):
        self.uid += 1
        return self.es.enter_context(self.nc.sbuf_tensor("%s_%d" % (name or "t", self.uid), list(shape), dtype))

    def ps(self, shape, dtype, name=None):
        self.uid += 1
        esz = 4 if dtype == F32 else 2
        full = self.es.enter_context(self.nc.psum_tensor("%s_%d" % (name or "p", self.uid), [128, 2048 // esz], dtype))
        n = 1
        for d_ in shape[1:]:
            n *= d_
        assert n * esz <= 2048 and shape[0] == 128
        v = full[:, 0:n]
        if len(shape) == 3:
            v = v.rearrange("p (a b) -> p a b", b=shape[2])
        return v

    def _wait(self, e, tickets):
        need = {}
        for (s, v) in tickets:
            if v > need.get(s, 0):
                need[s] = v
        seen = self.seen[e]
        for s, v in need.items():
            if seen.get(s, 0) >= v:
                continue
            if s == e and e == "pe":
                continue
            self.eng[e].wait_ge(self.semh[s], v)
            seen[s] = v

    @staticmethod
    def _addr(lst, tk):
        for i, (s, v) in enumerate(lst):
            if s == tk[0]:
                if tk[1] > v:
                    lst[i] = tk
                return
        lst.append(tk)

    def _deps(self, reads, writes):
        tickets = []
        for t in reads:
            tickets += t.w
            if isinstance(t, PTrk):
                tickets += t.r
        for t in writes:
            tickets += t.w
            tickets += t.r
        return tickets

    def _mark(self, tk, reads, writes):
        for t in reads:
            if isinstance(t, PTrk):
                t.r = [tk]
            else:
                self._addr(t.r, tk)
        for t in writes:
            t.w = [tk]
            t.r = []

    def op(self, e, fn, reads=(), writes=()):
        self._wait(e, self._deps(reads, writes))
        ins = fn(self.eng[e])
        self.cnt[e] += 1
        ins.then_inc(self.semh[e], 1)
        tk = (e, self.cnt[e])
        self._mark(tk, reads, writes)
        return tk

    def mm_group(self, fns, reads=(), writes=()):
        self._wait("pe", self._deps(reads, writes))
        ins = None
        for fn in fns:
            ins = fn(self.eng["pe"])
        self.cnt["pe"] += 1
        ins.then_inc(self.semh["pe"], 1)
        tk = ("pe", self.cnt["pe"])
        self._mark(tk, reads, writes)
        return tk

    def dma(self, q, out, in_, reads=(), writes=(), **kw):
        names, i = self.dq[q]
        nm = names[i % len(names)]
        self.dq[q][1] = i + 1
        tickets = self._deps(reads, writes)
        tickets.append((nm, self.cnt[nm]))
        self._wait(q, tickets)
        self.eng[q].dma_start(out=out, in_=in_, **kw).then_inc(self.semh[nm], 16)
        self.cnt[nm] += 16
        tk = (nm, self.cnt[nm])
        self._mark(tk, reads, writes)
        return tk

    def barrier(self):
        allt = [(s, c) for s, c in self.cnt.items() if c > 0]
        for e in self.eng:
            self._wait(e, allt)


def _bc(ap, shape):
    return ap.to_broadcast(list(shape))


class Prog:
    def __init__(self, S=8192, layers=2, phases=("0", "1", "2a", "2b", "2c", "3"), debug=False, ext_scratch=True):
        self.ext_scratch = ext_scratch
        self.S = S
        self.L = layers
        self.phases = phases
        self.debug = debug
        self.nc = bass.Bass("TRN2", target_bir_lowering=False)
        self.build()

    def build(self):
        nc = self.nc
        S = self.S
        L = self.L
        dt = nc.dram_tensor

        def din(name, shape, dtype=F32):
            return dt(name, list(shape), dtype, kind="ExternalInput").ap()

        def dscr(name, shape, dtype):
            return dt(name, list(shape), dtype, kind="ExternalOutput" if (self.debug or self.ext_scratch) else "Internal").ap()

        self.x_in = din("x", [S, D])
        self.c_in = din("c", [128, 8])
        self.norm_w = din("norm_w", [L, D])
        self.w_ada = din("w_ada", [L, D, 3 * D])
        self.b_ada = din("b_ada", [L, 3 * D])
        self.w_in = din("w_in", [L, D, D_IN])
        self.b_gate = din("b_gate", [L, 3 * D])
        self.qk_w = din("qk_w", [L, 4, 64])
        self.rope_c = din("rope_c", [S, 64])
        self.rope_s = din("rope_s", [S, 64])
        self.dt_bias = din("dt_bias", [L, 16])
        self.w_out = din("w_out", [L, D, D])
        self.w_pa = din("w_proj_a", [L, 512, D])
        self.w_pb = din("w_proj_b", [L, 256, D])
        self.w_pc = din("w_proj_c", [L, 512, D])
        self.out = dt("out", [S, D], F32, kind="ExternalOutput").ap()
        self.s_qa = dscr("s_qa", [S, 512], BF16)
        self.s_ka = dscr("s_ka", [S, 128], BF16)
        self.s_va = dscr("s_va", [S, 128], BF16)
        self.s_gaT = dscr("s_gaT", [512, S], BF16)
        self.s_yaT = dscr("s_yaT", [512, S], BF16)
        self.s_qb = dscr("s_qb", [S, 768], BF16)
        self.s_kb = dscr("s_kb", [S, 768], BF16)
        self.s_vb = dscr("s_vb", [S, 768], BF16)
        self.s_gb = dscr("s_gb", [S, 256], BF16)
        self.s_zc = dscr("s_zc", [S, 512], BF16)
        self.s_xbc = dscr("s_xbc", [1024, S], BF16)
        self.s_dt = dscr("s_dt", [S, 16], F32)
        self.s_ybT = dscr("s_ybT", [256, S], BF16)
        self.s_nb = [dscr("s_nb%d" % g, [S, 260], F32) for g in range(3)]
        self.bias_tab = din("bias_tab", [128, 36, 128])
        self.mask_tab = din("mask_tab", [128, 3, 128])
        self.s_ycT = dscr("s_ycT", [512, S], BF16)
        self.s_xcv = dscr("s_xcv", [1024, S], BF16)
        self.s_yb32 = dscr("s_yb32", [S, 512], F32)
        self.conv_w = din("conv_w", [L, 128, 8, 5])
        self.conv_b = din("conv_b", [L, 128, 8])
        self.a_log = din("a_log", [L, 16])
        self.d_skip = din("d_skip", [L, 8])
        self.ssm_nw = din("ssm_norm_w", [L, 512])
        if self.debug:
            self.dbg_mod = dt("dbg_mod", [128, 3 * D], F32, kind="ExternalOutput").ap()

        with ExitStack() as es_top:
            kb = KB(nc, es_top)
            self.kb = kb
            self.ident_bf = kb.sb([128, 128], BF16, "identb")
            self.ident_f = kb.sb([128, 128], F32, "identf")
            self.ones_f = kb.sb([128, 128], F32, "onesf")
            self.t_const = Trk()
            kb.op("pool", lambda e: e.memset(self.ones_f[:, :], 1.0), writes=[self.t_const])
            kb.op("pool", lambda e: e.memset(self.ident_f[:, :], 0.0), writes=[self.t_const])
            kb.op("pool", lambda e: e.affine_select(out=self.ident_f[:, :], in_=self.ident_f[:, :],
                                                    pattern=[[-1, 128]], compare_op=ALU.not_equal, fill=1.0,
                                                    base=0, channel_multiplier=1),
                  reads=[self.t_const], writes=[self.t_const])
            kb.op("dve", lambda e: e.tensor_copy(out=self.ident_bf[:, :], in_=self.ident_f[:, :]),
                  reads=[self.t_const], writes=[self.t_const])
            self.A_t = kb.sb([128, D], F32, "A_t")
            self.Sh_t = kb.sb([128, D], F32, "Sh_t")
            self.G_t = kb.sb([128, D], F32, "G_t")
            self.t_mod = Trk()

            for l in range(L):
                x_src = self.x_in if l == 0 else self.out
                if "0" in self.phases:
                    with ExitStack() as es:
                        kb.es = es
                        self.phase0(l)
                        kb.barrier()
                if "1" in self.phases:
                    with ExitStack() as es:
                        kb.es = es
                        self.phase1(l, x_src)
                        kb.barrier()
                for ph, fn in (("2a", self.phase2a), ("2b", self.phase2b), ("2c", self.phase2c)):
                    if ph in self.phases:
                        with ExitStack() as es:
                            kb.es = es
                            fn(l)
                            kb.barrier()
                if "3" in self.phases:
                    with ExitStack() as es:
                        kb.es = es
                        self.phase3(l, x_src)
                        kb.barrier()
            kb.es = es_top
            kb.barrier()

    def phase0(self, l):
        kb, nc = self.kb, self.nc
        c_sb = kb.sb([128, 8], F32, "c_sb")
        c_act = kb.sb([128, 8], F32, "c_act")
        cl = kb.sb([128, 8, 128], F32, "cl")
        wa = kb.sb([128, 8, 3 * D], F32, "wa")
        bb = kb.sb([128, 3 * D], F32, "bb")
        nw = kb.sb([128, D], F32, "nw")
        mod = kb.sb([128, 3 * D], F32, "mod")
        t_c, t_wa, t_bb, t_cl, t_mod = Trk(), [Trk() for _ in range(8)], Trk(), Trk(), Trk()
        kb.dma("sp", c_sb[:, :], self.c_in[:, :], writes=[t_c])
        wv = self.w_ada[l].rearrange("(k p) n -> p k n", p=128)
        for k in range(8):
            kb.dma("sp", wa[:, k, :], wv[:, k, :], writes=[t_wa[k]])
        kb.dma("sp", bb[:, :], self.b_ada[l:l + 1, :].partition_broadcast(128), writes=[t_bb])
        kb.dma("sp", nw[:, :], self.norm_w[l:l + 1, :].partition_broadcast(128), writes=[t_bb])
        kb.op("act", lambda e: e.activation(out=c_act[:, :], in_=c_sb[:, :], func=AF.Silu), reads=[t_c], writes=[t_c])
        for k in range(8):
            kb.op("dve", lambda e: e.tensor_scalar(out=cl[:, k, :], in0=self.ones_f[:, :], scalar1=c_act[:, k:k + 1],
                                                   scalar2=None, op0=ALU.mult),
                  reads=[t_c, self.t_const], writes=[t_cl])
        pm = [kb.ps([128, 512], F32, "pm") for _ in range(2)]
        t_pm = [PTrk(), PTrk()]
        for n in range(6):
            b = n % 2
            fns = []
            for k in range(8):
                fns.append(lambda e, k=k: e.matmul(pm[b][:, :], lhsT=cl[:, k, :], rhs=wa[:, k, n * 512:(n + 1) * 512],
                                                   start=(k == 0), stop=(k == 7)))
            kb.mm_group(fns, reads=[t_cl] + t_wa, writes=[t_pm[b]])
            kb.op("dve", lambda e: e.tensor_tensor(out=mod[:, n * 512:(n + 1) * 512], in0=pm[b][:, :],
                                                   in1=bb[:, n * 512:(n + 1) * 512], op=ALU.add),
                  reads=[t_pm[b], t_bb], writes=[t_mod])
        kb.op("dve", lambda e: e.tensor_copy(out=self.Sh_t[:, :], in_=mod[:, 0:D]), reads=[t_mod], writes=[self.t_mod])
        kb.op("dve", lambda e: e.scalar_tensor_tensor(out=self.A_t[:, :], in0=mod[:, D:2 * D], scalar=1.0, in1=nw[:, :],
                                                      op0=ALU.add, op1=ALU.mult),
              reads=[t_mod, t_bb], writes=[self.t_mod])
        kb.op("dve", lambda e: e.tensor_copy(out=self.G_t[:, :], in_=mod[:, 2 * D:3 * D]), reads=[t_mod], writes=[self.t_mod])
        if self.debug and l == 0:
            kb.dma("pool", self.dbg_mod[:, :], mod[:, :], reads=[t_mod])

    def phase1(self, l, x_src):
        kb, nc = self.kb, self.nc
        S = self.S
        NT = S // 512
        W = kb.sb([128, 8, NCOL1], BF16, "W1")
        t_W = [Trk() for _ in range(8)]
        wv = self.w_in[l].rearrange("(k p) n -> p k n", p=128)
        for k in range(8):
            kb.dma("pool", W[:, k, :], wv[:, k, 0:NCOL1], writes=[t_W[k]])
        qkw = kb.sb([128, 4, 64], F32, "qkw")
        dtb = kb.sb([128, 16], F32, "dtb")
        t_small = Trk()
        kb.dma("sp", qkw[:, :, :].rearrange("p a e -> p (a e)"),
               self.qk_w[l:l + 1].rearrange("o a e -> o (a e)").partition_broadcast(128), writes=[t_small])
        kb.dma("sp", dtb[:, :], self.dt_bias[l:l + 1, :].partition_broadcast(128), writes=[t_small])

        NB = 2
        xt = [kb.sb([128, D], F32, "xt") for _ in range(NB)]
        t_xt = [Trk() for _ in range(NB)]
        junk = kb.sb([128, D], BF16, "junk")
        t_junk = Trk()
        h32 = kb.sb([128, D], F32, "h32")
        hb = kb.sb([128, D], BF16, "hb")
        t_h = Trk()
        st = [kb.sb([128, 8], F32, "st") for _ in range(NB)]
        t_st = [Trk() for _ in range(NB)]
        hT = [kb.sb([128, 8, 512], BF16, "hT") for _ in range(2)]
        t_hT = [Trk(), Trk()]
        ptp = [kb.ps([128, 4, 128], BF16, "ptp") for _ in range(2)]
        t_ptp = [PTrk(), PTrk()]
        pmm = [kb.ps([128, 512], F32, "pmm") for _ in range(4)]
        t_pmm = [PTrk() for _ in range(4)]
        mmi = [0]
        NO = 2
        o_qa = [kb.sb([128, 512], BF16, "o_qa") for _ in range(NO)]
        o_ka = [kb.sb([128, 128], BF16, "o_ka") for _ in range(NO)]
        o_va = [kb.sb([128, 128], BF16, "o_va") for _ in range(NO)]
        o_ga = [kb.sb([128, 512], BF16, "o_ga") for _ in range(NO)]
        o_qb = [kb.sb([128, 768], BF16, "o_qb") for _ in range(NO)]
        o_kb = [kb.sb([128, 768], BF16, "o_kb") for _ in range(NO)]
        o_vb = [kb.sb([128, 768], BF16, "o_vb") for _ in range(NO)]
        o_gb = [kb.sb([128, 256], BF16, "o_gb") for _ in range(NO)]
        o_zc = [kb.sb([128, 512], BF16, "o_zc") for _ in range(NO)]
        o_dt = [kb.sb([128, 16], F32, "o_dt") for _ in range(NO)]
        o_xbc = [kb.sb([128, 512], BF16, "o_xbc") for _ in range(NO)]
        t_o = {n: [Trk() for _ in range(NO)] for n in ("qa", "ka", "va", "ga", "qb", "kb", "vb", "gb", "zc", "dt", "xbc")}
        sq = kb.sb([128, 768], F32, "sq")
        t_sq = Trk()
        ssh = kb.sb([128, 12], F32, "ssh")
        rsh = kb.sb([128, 12], F32, "rsh")
        t_ssh = Trk()
        qn = kb.sb([128, 768], F32, "qn")
        qr = kb.sb([128, 512], F32, "qr")
        t_qn = Trk()
        t_qr = Trk()
        rc = [kb.sb([128, 64], F32, "rc") for _ in range(NB)]
        rs = [kb.sb([128, 64], F32, "rs") for _ in range(NB)]
        t_rope = [Trk() for _ in range(NB)]
        dtt = kb.sb([128, 16], F32, "dtt")
        t_dtt = Trk()

        def next_pmm():
            i = mmi[0] % 4
            mmi[0] += 1
            return pmm[i], t_pmm[i]

        def mm_tok(hTt, t_hTt, j, c0, c1):
            p, tp = next_pmm()
            n = c1 - c0
            fns = [lambda e, k=k: e.matmul(p[:, 0:n], lhsT=hTt[:, k, j * 128:(j + 1) * 128], rhs=W[:, k, c0:c1],
                                           start=(k == 0), stop=(k == 7)) for k in range(8)]
            kb.mm_group(fns, reads=[t_hTt] + t_W, writes=[tp])
            return p, tp

        def qk_norm(p, tp, nh, widx, dst, t_dst, rope, g, ob):
            n = nh * 64
            kb.op("act", lambda e: e.activation(out=sq[:, 0:n], in_=p[:, 0:n], func=AF.Square), reads=[tp], writes=[t_sq])
            kb.op("dve", lambda e: e.tensor_reduce(out=ssh[:, 0:nh], in_=sq[:, 0:n].rearrange("p (h e) -> p h e", e=64),
                                                   axis=AX.X, op=ALU.add), reads=[t_sq], writes=[t_ssh])
            kb.op("act", lambda e: e.activation(out=ssh[:, 0:nh], in_=ssh[:, 0:nh], func=AF.Sqrt, scale=1.0 / 64, bias=self.eps_t[:, 0:1]),
                  reads=[t_ssh, self.t_const], writes=[t_ssh])
            kb.op("dve", lambda e: e.reciprocal(out=rsh[:, 0:nh], in_=ssh[:, 0:nh]), reads=[t_ssh], writes=[t_ssh])
            p3 = p[:, 0:n].rearrange("p (h e) -> p h e", e=64)
            q3 = qn[:, 0:n].rearrange("p (h e) -> p h e", e=64)
            kb.op("dve", lambda e: e.tensor_tensor(out=q3, in0=p3, in1=_bc(rsh[:, 0:nh].unsqueeze(2), [128, nh, 64]), op=ALU.mult),
                  reads=[tp, t_ssh], writes=[t_qn])
            wb = _bc(qkw[:, widx:widx + 1, :], [128, nh, 64])
            if not rope:
                d3 = dst[:, 0:n].rearrange("p (h e) -> p h e", e=64)
                kb.op("dve", lambda e: e.tensor_tensor(out=d3, in0=q3, in1=wb, op=ALU.mult),
                      reads=[t_qn, t_small], writes=[t_dst])
                return
            kb.op("dve", lambda e: e.tensor_tensor(out=q3, in0=q3, in1=wb, op=ALU.mult), reads=[t_qn, t_small], writes=[t_qn])
            r3 = qr[:, 0:n].rearrange("p (h e) -> p h e", e=64)
            kb.op("dve", lambda e: e.tensor_tensor(out=r3, in0=q3, in1=_bc(rc[g][:, :].unsqueeze(1), [128, nh, 64]), op=ALU.mult),
                  reads=[t_qn, t_rope[g]], writes=[t_qr])
            q5 = qn[:, 0:n].rearrange("p (h a b i) -> p h a b i", a=2, b=2, i=16)
            s5 = rs[g][:, :].rearrange("p (a b i) -> p a b i", a=2, b=2, i=16)
            sw = kb.sb_sw
            w5 = sw[:, 0:n].rearrange("p (h a b i) -> p h a b i", a=2, b=2, i=16)
            for bsel in range(2):
                kb.op("dve", lambda e, bsel=bsel: e.tensor_tensor(
                    out=w5[:, :, :, bsel, :], in0=q5[:, :, :, 1 - bsel, :],
                    in1=_bc(s5[:, :, bsel, :].unsqueeze(1), [128, nh, 2, 16]), op=ALU.mult),
                    reads=[t_qn, t_rope[g]], writes=[self.t_sw])
            kb.op("dve", lambda e: e.tensor_tensor(out=dst[:, 0:n], in0=qr[:, 0:n], in1=sw[:, 0:n], op=ALU.add),
                  reads=[t_qr, self.t_sw], writes=[t_dst])

        kb.sb_sw = kb.sb([128, 512], F32, "sw")
        self.t_sw = Trk()
        self.eps_t = kb.sb([128, 1], F32, "eps_t")
        kb.op("pool", lambda e: e.memset(self.eps_t[:, :], EPS), writes=[self.t_const])

        gi = 0
        for T in range(NT):
            hTt, t_hTt = hT[T % 2], t_hT[T % 2]
            for j in range(4):
                g = gi % NB
                gi += 1
                t0 = T * 512 + j * 128
                kb.dma("sp", xt[g][:, :], x_src[t0:t0 + 128, :], writes=[t_xt[g]])
                kb.dma("sp", rc[g][:, :], self.rope_c[t0:t0 + 128, :], writes=[t_rope[g]])
                kb.dma("sp", rs[g][:, :], self.rope_s[t0:t0 + 128, :], writes=[t_rope[g]])
                kb.op("act", lambda e: e.activation(out=junk[:, :], in_=xt[g][:, :], func=AF.Square, accum_out=st[g][:, 0:1]),
                      reads=[t_xt[g]], writes=[t_junk, t_st[g]])
                kb.op("act", lambda e: e.activation(out=st[g][:, 1:2], in_=st[g][:, 0:1], func=AF.Sqrt, scale=1.0 / D, bias=self.eps_t[:, 0:1]),
                      reads=[t_st[g], self.t_const], writes=[t_st[g]])
                kb.op("dve", lambda e: e.reciprocal(out=st[g][:, 2:3], in_=st[g][:, 1:2]), reads=[t_st[g]], writes=[t_st[g]])
                kb.op("dve", lambda e: e.scalar_tensor_tensor(out=h32[:, :], in0=xt[g][:, :], scalar=st[g][:, 2:3], in1=self.A_t[:, :],
                                                              op0=ALU.mult, op1=ALU.mult),
                      reads=[t_xt[g], t_st[g], self.t_mod], writes=[t_h])
                kb.op("dve", lambda e: e.tensor_tensor(out=hb[:, :], in0=h32[:, :], in1=self.Sh_t[:, :], op=ALU.add),
                      reads=[t_h, self.t_mod], writes=[t_h])
                for half in range(2):
                    pp, tpp = ptp[half], t_ptp[half]
                    fns = [lambda e, q=q: e.transpose(pp[:, q, :], hb[:, (half * 4 + q) * 128:(half * 4 + q + 1) * 128], self.ident_bf[:, :])
                           for q in range(4)]
                    kb.mm_group(fns, reads=[t_h, self.t_const], writes=[tpp])
                    kb.op("act", lambda e: e.activation(out=hTt[:, half * 4:half * 4 + 4, j * 128:(j + 1) * 128], in_=pp[:, :, :], func=AF.Copy),
                          reads=[tpp], writes=[t_hTt])
                ob = g % NO
                p, tp = mm_tok(hTt, t_hTt, j, 0, 512)
                qk_norm(p, tp, 8, 0, o_qa[ob], t_o["qa"][ob], True, g, ob)
                kb.dma("pool", self.s_qa[t0:t0 + 128, :], o_qa[ob][:, :], reads=[t_o["qa"][ob]])
                p, tp = mm_tok(hTt, t_hTt, j, 512, 768)
                qk_norm(p, tp, 2, 1, o_ka[ob], t_o["ka"][ob], True, g, ob)
                kb.dma("pool", self.s_ka[t0:t0 + 128, :], o_ka[ob][:, :], reads=[t_o["ka"][ob]])
                kb.op("act", lambda e: e.activation(out=o_va[ob][:, :], in_=p[:, 128:256], func=AF.Copy), reads=[tp], writes=[t_o["va"][ob]])
                kb.dma("pool", self.s_va[t0:t0 + 128, :], o_va[ob][:, :], reads=[t_o["va"][ob]])
                for (c0, c1, o0) in ((1280, 1792, 0), (1792, 2048, 512)):
                    p, tp = mm_tok(hTt, t_hTt, j, c0, c1)
                    nh = (c1 - c0) // 64
                    qk_norm(p, tp, nh, 2, o_qb[ob][:, o0:o0 + nh * 64], t_o["qb"][ob], False, g, ob)
                kb.dma("pool", self.s_qb[t0:t0 + 128, :], o_qb[ob][:, :], reads=[t_o["qb"][ob]])
                for (c0, c1, o0) in ((2048, 2560, 0), (2560, 2816, 512)):
                    p, tp = mm_tok(hTt, t_hTt, j, c0, c1)
                    nh = (c1 - c0) // 64
                    qk_norm(p, tp, nh, 3, o_kb[ob][:, o0:o0 + nh * 64], t_o["kb"][ob], False, g, ob)
                kb.dma("pool", self.s_kb[t0:t0 + 128, :], o_kb[ob][:, :], reads=[t_o["kb"][ob]])
                for (c0, c1, o0) in ((2816, 3328, 0), (3328, 3584, 512)):
                    p, tp = mm_tok(hTt, t_hTt, j, c0, c1)
                    n = c1 - c0
                    kb.op("act", lambda e: e.activation(out=o_vb[ob][:, o0:o0 + n], in_=p[:, 0:n], func=AF.Copy), reads=[tp], writes=[t_o["vb"][ob]])
                kb.dma("pool", self.s_vb[t0:t0 + 128, :], o_vb[ob][:, :], reads=[t_o["vb"][ob]])
                p, tp = mm_tok(hTt, t_hTt, j, 3584, 3840)
                kb.op("act", lambda e: e.activation(out=o_gb[ob][:, :], in_=p[:, 0:256], func=AF.Silu), reads=[tp], writes=[t_o["gb"][ob]])
                kb.dma("pool", self.s_gb[t0:t0 + 128, :], o_gb[ob][:, :], reads=[t_o["gb"][ob]])
                p, tp = mm_tok(hTt, t_hTt, j, 4352, 4864)
                kb.op("act", lambda e: e.activation(out=o_zc[ob][:, :], in_=p[:, 0:512], func=AF.Silu), reads=[tp], writes=[t_o["zc"][ob]])
                kb.dma("pool", self.s_zc[t0:t0 + 128, :], o_zc[ob][:, :], reads=[t_o["zc"][ob]])
                p, tp = mm_tok(hTt, t_hTt, j, 5376, 5392)
                kb.op("dve", lambda e: e.tensor_tensor(out=dtt[:, :], in0=p[:, 0:16], in1=dtb[:, :], op=ALU.add), reads=[tp, t_small], writes=[t_dtt])
                kb.op("act", lambda e: e.activation(out=dtt[:, :], in_=dtt[:, :], func=AF.Exp), reads=[t_dtt], writes=[t_dtt])
                kb.op("act", lambda e: e.activation(out=o_dt[ob][:, :], in_=dtt[:, :], func=AF.Ln, bias=self.ones_f[:, 0:1], scale=1.0),
                      reads=[t_dtt, self.t_const], writes=[t_o["dt"][ob]])
                kb.dma("pool", self.s_dt[t0:t0 + 128, :], o_dt[ob][:, :], reads=[t_o["dt"][ob]])
            for cb in range(12):
                if cb < 4:
                    c0 = 3840 + cb * 128
                elif cb < 8:
                    c0 = 4864 + (cb - 4) * 128
                else:
                    c0 = 768 + (cb - 8) * 128
                p, tp = next_pmm()
                fns = [lambda e, k=k: e.matmul(p[:, :], lhsT=W[:, k, c0:c0 + 128], rhs=hTt[:, k, :], start=(k == 0), stop=(k == 7))
                       for k in range(8)]
                kb.mm_group(fns, reads=[t_hTt] + t_W, writes=[tp])
                ob = cb % NO
                if cb < 8:
                    kb.op("act", lambda e: e.activation(out=o_xbc[ob][:, :], in_=p[:, :], func=AF.Copy), reads=[tp], writes=[t_o["xbc"][ob]])
                    kb.dma("pool", self.s_xbc[cb * 128:(cb + 1) * 128, T * 512:(T + 1) * 512], o_xbc[ob][:, :], reads=[t_o["xbc"][ob]])
                else:
                    kb.op("act", lambda e: e.activation(out=o_xbc[ob][:, :], in_=p[:, :], func=AF.Silu), reads=[tp], writes=[t_o["xbc"][ob]])
                    kb.dma("pool", self.s_gaT[(cb - 8) * 128:(cb - 7) * 128, T * 512:(T + 1) * 512], o_xbc[ob][:, :], reads=[t_o["xbc"][ob]])


    def phase2a(self, l):
        kb, nc = self.kb, self.nc
        S = self.S
        NKT = S // 128
        NQB = S // 512
        kT2 = [kb.sb([128, S], BF16, "kT2") for _ in range(2)]
        v1 = kb.sb([128, NKT, 2, 65], BF16, "v1")
        kall = kb.sb([128, NKT, 128], BF16, "kall")
        kd = [kb.sb([128, 2, 2, 64], BF16, "kd") for _ in range(2)]
        t_kT, t_v1, t_kall = Trk(), Trk(), Trk()
        t_kd = [Trk(), Trk()]
        ptp = [kb.ps([128, 4, 128], BF16, "ptp") for _ in range(1)]
        t_ptp = [PTrk()]
        kb.op("pool", lambda e: e.memset(v1[:, :, :, :], 1.0), writes=[t_v1])
        CH = 8
        for kv in range(2):
            src = self.s_va[:, kv * 64:(kv + 1) * 64].rearrange("(n p) e -> p n e", p=128)
            for n0 in range(0, NKT, CH):
                kb.dma("sp", v1[:, n0:n0 + CH, kv, 0:64], src[:, n0:n0 + CH, :], writes=[t_v1])
        srck = self.s_ka.rearrange("(n p) c -> p n c", p=128)
        for n0 in range(0, NKT, CH):
            kb.dma("sp", kall[:, n0:n0 + CH, :], srck[:, n0:n0 + CH, :], writes=[t_kall])
        for n in range(NKT):
            b = n % 2
            kb.op("dve", lambda e: e.tensor_copy(out=kd[b][:, :, :, :],
                                                 in_=_bc(kall[:, n, :].rearrange("p (k e) -> p k e", e=64).unsqueeze(2), [128, 2, 2, 64])),
                  reads=[t_kall], writes=[t_kd[b]])
            fns = [lambda e, kv=kv: e.transpose(ptp[0][:, kv, :], kd[b][:, kv, :, :].rearrange("p a e -> p (a e)"), self.ident_bf[:, :])
                   for kv in range(2)]
            kb.mm_group(fns, reads=[t_kd[b], self.t_const], writes=[t_ptp[0]])
            for kv in range(2):
                kb.op("act", lambda e, kv=kv: e.activation(out=kT2[kv][:, n * 128:(n + 1) * 128], in_=ptp[0][:, kv, :], func=AF.Copy),
                      reads=[t_ptp[0]], writes=[t_kT])
        qsb = [kb.sb([128, 4, 512], BF16, "qsb") for _ in range(2)]
        t_qsb = [Trk(), Trk()]
        qT = [kb.sb([128, 4, 512], BF16, "qT") for _ in range(2)]
        t_qT = [Trk(), Trk()]
        gT = [kb.sb([64, 8, 512], BF16, "gT") for _ in range(2)]
        t_gT = [Trk(), Trk()]
        NP = 3
        psT = [kb.ps([128, 512], F32, "psT") for _ in range(NP)]
        t_psT = [PTrk() for _ in range(NP)]
        pT = [kb.sb([128, 512], BF16, "pT") for _ in range(NP)]
        t_pT = [Trk() for _ in range(NP)]
        pacc = [kb.ps([128, 512], F32, "pacc") for _ in range(2)]
        t_acc = [PTrk(), PTrk()]
        pbc = kb.ps([128, 512], F32, "pbc")
        t_bc = PTrk()
        rec = kb.sb([128, 512], F32, "rec")
        t_rec = Trk()
        t1 = kb.sb([64, 512], F32, "t1")
        t_t1 = Trk()
        yT = [kb.sb([64, 512], BF16, "yT") for _ in range(2)]
        t_yT = [Trk(), Trk()]

        def load_q(qb):
            g = qb % 2
            kb.dma("sp", qsb[g][:, :, :], self.s_qa[qb * 512:(qb + 1) * 512, :].rearrange("(j p) c -> p j c", p=128), writes=[t_qsb[g]])
            kb.dma("sp", gT[g][:, :, :], self.s_gaT[:, qb * 512:(qb + 1) * 512].rearrange("(h e) t -> e h t", e=64), writes=[t_gT[g]])

        def prep_q(qb):
            g = qb % 2
            for j in range(4):
                fns = [lambda e, hp=hp: e.transpose(ptp[0][:, hp, :], qsb[g][:, j, hp * 128:(hp + 1) * 128], self.ident_bf[:, :])
                       for hp in range(4)]
                kb.mm_group(fns, reads=[t_qsb[g], self.t_const], writes=[t_ptp[0]])
                kb.op("dve", lambda e: e.tensor_copy(out=qT[g][:, :, j * 128:(j + 1) * 128], in_=ptp[0][:, :, :]),
                      reads=[t_ptp[0]], writes=[t_qT[g]])

        seq = [(qb, h, kt) for qb in range(NQB) for h in range(8) for kt in range(NKT)]

        def qk(i):
            qb, h, kt = seq[i]
            g = qb % 2
            hp, half, kv = h // 2, h % 2, h // 4
            pr = slice(half * 64, half * 64 + 64)
            b = i % NP
            kb.op("pe", lambda e: e.matmul(psT[b][:, :], lhsT=kT2[kv][pr, kt * 128:(kt + 1) * 128], rhs=qT[g][pr, hp, :], start=True, stop=True),
                  reads=[t_kT, t_qT[g]], writes=[t_psT[b]])

        pending = []

        def epilogue_a(qb, h):
            a = h % 2
            kb.op("dve", lambda e: e.reciprocal(out=rec[64:65, :], in_=pacc[a][64:65, :]), reads=[t_acc[a]], writes=[t_rec])
            kb.op("dve", lambda e: e.tensor_tensor(out=t1[:, :], in0=pacc[a][0:64, :], in1=gT[qb % 2][:, h, :], op=ALU.mult),
                  reads=[t_acc[a], t_gT[qb % 2]], writes=[t_t1])

        def epilogue_b(qb, h):
            a = h % 2
            kb.op("pe", lambda e: e.matmul(pbc[0:64, :], lhsT=self.ones_f[64:65, 0:64], rhs=rec[64:65, :], start=True, stop=True),
                  reads=[t_rec, self.t_const], writes=[t_bc])
            kb.op("dve", lambda e: e.tensor_tensor(out=yT[a][:, :], in0=t1[:, :], in1=pbc[0:64, :], op=ALU.mult),
                  reads=[t_t1, t_bc], writes=[t_yT[a]])
            kb.dma("pool", self.s_yaT[h * 64:(h + 1) * 64, qb * 512:(qb + 1) * 512], yT[a][:, :], reads=[t_yT[a]])

        load_q(0)
        if NQB > 1:
            load_q(1)
        prep_q(0)
        N = len(seq)
        qk(0)
        if N > 1:
            qk(1)
        for i in range(N):
            qb, h, kt = seq[i]
            if i + 2 < N:
                qb2, h2, kt2 = seq[i + 2]
                if qb2 != qb and h2 == 0 and kt2 == 0:
                    prep_q(qb2)
                qk(i + 2)
            b = i % NP
            kv = h // 4
            a = h % 2
            kb.op("act", lambda e: e.activation(out=pT[b][:, :], in_=psT[b][:, :], func=AF.Exp, scale=0.125),
                  reads=[t_psT[b]], writes=[t_pT[b]])
            kb.op("pe", lambda e: e.matmul(pacc[a][0:65, :], lhsT=v1[:, kt, kv, :], rhs=pT[b][:, :], start=(kt == 0), stop=(kt == NKT - 1)),
                  reads=[t_v1, t_pT[b]], writes=[t_acc[a]])
            if kt == NKT - 1:
                epilogue_a(qb, h)
                pending.append((i + 3, qb, h))
                if h == 7 and qb + 2 < NQB:
                    load_q(qb + 2)
            while pending and (pending[0][0] <= i or i == N - 1):
                _, pq, ph = pending.pop(0)
                epilogue_b(pq, ph)


    def phase2b(self, l):
        kb, nc = self.kb, self.nc
        S = self.S
        E = kb.sb([128, 36, 128], F32, "E2b")
        msk = kb.sb([128, 3, 128], F32, "msk2b")
        t_E, t_msk = Trk(), Trk()
        for c in range(4):
            kb.dma("sp", E[:, c * 9:(c + 1) * 9, :], self.bias_tab[:, c * 9:(c + 1) * 9, :], writes=[t_E])
        kb.dma("sp", msk[:, :, :], self.mask_tab[:, :, :], writes=[t_msk])
        kb.op("act", lambda e: e.activation(out=E[:, :, :], in_=E[:, :, :], func=AF.Exp), reads=[t_E], writes=[t_E])
        for i in range(12):
            kb.op("dve", lambda e: e.tensor_tensor(out=E[:, i * 3:(i + 1) * 3, :], in0=E[:, i * 3:(i + 1) * 3, :], in1=msk[:, :, :], op=ALU.mult),
                  reads=[t_E, t_msk], writes=[t_E])
        MMAX = S
        NTMAX = MMAX // 128
        qT = kb.sb([128, 2, MMAX], BF16, "qT2b")
        kT = kb.sb([128, 2, MMAX], BF16, "kT2b")
        v1 = kb.sb([128, NTMAX, 4, 65], BF16, "v12b")
        t_qT, t_kT, t_v1 = Trk(), Trk(), Trk()
        CH = 4
        qch = [kb.sb([128, CH, 256], BF16, "qch") for _ in range(2)]
        kch = [kb.sb([128, CH, 256], BF16, "kch") for _ in range(2)]
        t_qch, t_kch = [Trk(), Trk()], [Trk(), Trk()]
        ptp = [kb.ps([128, 4, 128], BF16, "ptp2b") for _ in range(2)]
        t_ptp = [PTrk(), PTrk()]
        psT = [[kb.ps([128, 2, 128], F32, "psT2b") for _ in range(2)] for _ in range(2)]
        t_psT = [[PTrk(), PTrk()], [PTrk(), PTrk()]]
        pacc = [kb.ps([128, 4, 65], F32, "pacc2b") for _ in range(2)]
        t_acc = [PTrk(), PTrk()]
        pe32 = [kb.sb([128, 4, 128], F32, "pe32") for _ in range(2)]
        t_pe32 = [Trk(), Trk()]
        pT = [kb.sb([128, 4, 128], BF16, "pT2b") for _ in range(2)]
        t_pT = [Trk(), Trk()]
        osb = [kb.sb([128, 4, 65], F32, "osb2b") for _ in range(2)]
        t_osb = [Trk(), Trk()]
        kb.op("pool", lambda e: e.memset(v1[:, :, :, :], 1.0), writes=[t_v1])
        ci = 0
        si = 0
        ti = 0
        for g, d in enumerate(B_DIL):
            M = S // d
            NT = M // 128
            cols = slice(g * 256, (g + 1) * 256)
            for r in range(d):
                qv = self.s_qb.rearrange("(m d) c -> d m c", d=d)[r][:, cols].rearrange("(n p) c -> p n c", p=128)
                kv_ = self.s_kb.rearrange("(m d) c -> d m c", d=d)[r][:, cols].rearrange("(n p) c -> p n c", p=128)
                vv = self.s_vb.rearrange("(m d) c -> d m c", d=d)[r][:, cols].rearrange("(n p) (h e) -> p n h e", p=128, e=64)
                nv = self.s_nb[g].rearrange("(m d) c -> d m c", d=d)[r].rearrange("(n p) c -> p n c", p=128)
                for n0 in range(0, NT, CH):
                    n1 = min(NT, n0 + CH)
                    for hs in range(4):
                        kb.dma("sp", v1[:, n0:n1, hs, 0:64], vv[:, n0:n1, hs, :], writes=[t_v1])
                    b = ci % 2
                    ci += 1
                    kb.dma("sp", qch[b][:, 0:n1 - n0, :], qv[:, n0:n1, :], writes=[t_qch[b]])
                    kb.dma("sp", kch[b][:, 0:n1 - n0, :], kv_[:, n0:n1, :], writes=[t_kch[b]])
                    for n in range(n0, n1):
                        pb = n % 2
                        fns = [lambda e, pr=pr: e.transpose(ptp[pb][:, pr, :], qch[b][:, n - n0, pr * 128:(pr + 1) * 128], self.ident_bf[:, :])
                               for pr in range(2)]
                        fns += [lambda e, pr=pr: e.transpose(ptp[pb][:, 2 + pr, :], kch[b][:, n - n0, pr * 128:(pr + 1) * 128], self.ident_bf[:, :])
                                for pr in range(2)]
                        kb.mm_group(fns, reads=[t_qch[b], t_kch[b], self.t_const], writes=[t_ptp[pb]])
                        kb.op("act", lambda e: e.activation(out=qT[:, :, n * 128:(n + 1) * 128], in_=ptp[pb][:, 0:2, :], func=AF.Copy),
                              reads=[t_ptp[pb]], writes=[t_qT])
                        kb.op("dve", lambda e: e.tensor_copy(out=kT[:, :, n * 128:(n + 1) * 128], in_=ptp[pb][:, 2:4, :]),
                              reads=[t_ptp[pb]], writes=[t_kT])
                for mt in range(NT):
                    a = ti % 2
                    ti += 1
                    offs = [o for o in (-1, 0, 1) if 0 <= mt + o < NT]
                    for oi, o in enumerate(offs):
                        kt = mt + o
                        b = si % 2
                        si += 1
                        fns = []
                        for hs in range(4):
                            pr, half = hs // 2, hs % 2
                            ps_ = slice(half * 64, half * 64 + 64)
                            fns.append(lambda e, hs=hs, pr=pr, ps_=ps_, half=half: e.matmul(
                                psT[b][half][:, pr, :], lhsT=kT[ps_, pr, kt * 128:(kt + 1) * 128],
                                rhs=qT[ps_, pr, mt * 128:(mt + 1) * 128], start=True, stop=True))
                        kb.mm_group(fns, reads=[t_qT, t_kT], writes=[t_psT[b][0], t_psT[b][1]])
                        for half in range(2):
                            kb.op("act", lambda e, half=half: e.activation(out=pe32[b][:, half:4:2, :], in_=psT[b][half][:, :, :], func=AF.Exp, scale=0.125),
                                  reads=[t_psT[b][half]], writes=[t_pe32[b]])
                        kb.op("dve", lambda e: e.tensor_tensor(out=pT[b][:, :, :], in0=pe32[b][:, :, :],
                                                               in1=E[:, g * 12 + (o + 1):g * 12 + 12:3, :], op=ALU.mult),
                              reads=[t_pe32[b], t_E], writes=[t_pT[b]])
                        fns = []
                        for hs in range(4):
                            fns.append(lambda e, hs=hs: e.matmul(pacc[a][:, hs, :], lhsT=pT[b][:, hs, :], rhs=v1[:, kt, hs, :],
                                                                 start=(oi == 0 and hs == 0), stop=(oi == len(offs) - 1),
                                                                 skip_group_check=True))
                        kb.mm_group(fns, reads=[t_pT[b], t_v1], writes=[t_acc[a]])
                    kb.op("act", lambda e: e.activation(out=osb[a][:, :, :], in_=pacc[a][:, :, :], func=AF.Copy), reads=[t_acc[a]], writes=[t_osb[a]])
                    kb.dma("pool", nv[:, mt, :], osb[a][:, :, :].rearrange("p h e -> p (h e)"), reads=[t_osb[a]])
        kb.barrier()
        nb3 = [kb.sb([128, 3, 260], F32, "nb3") for _ in range(2)]
        gt = [kb.sb([128, 256], BF16, "gt2b") for _ in range(2)]
        t_nb3 = [Trk(), Trk()]
        sm = kb.sb([128, 4, 65], F32, "sm2b")
        rc = kb.sb([128, 4], F32, "rc2b")
        yb32 = kb.sb([128, 4, 64], F32, "yb32")
        ybb = kb.sb([128, 256], BF16, "ybb")
        t_sm = Trk()
        t_ybb = Trk()
        ybo = [kb.sb([128, 2, 128], BF16, "ybo") for _ in range(2)]
        t_ybo = [Trk(), Trk()]
        for T in range(S // 128):
            b = T % 2
            t0 = T * 128
            for g in range(3):
                kb.dma("sp", nb3[b][:, g, :], self.s_nb[g][t0:t0 + 128, :], writes=[t_nb3[b]])
            kb.dma("sp", gt[b][:, :], self.s_gb[t0:t0 + 128, :], writes=[t_nb3[b]])
            smf = sm[:, :, :].rearrange("p h e -> p (h e)")
            kb.op("dve", lambda e: e.tensor_tensor(out=smf, in0=nb3[b][:, 0, :], in1=nb3[b][:, 1, :], op=ALU.add), reads=[t_nb3[b]], writes=[t_sm])
            kb.op("dve", lambda e: e.tensor_tensor(out=smf, in0=smf, in1=nb3[b][:, 2, :], op=ALU.add), reads=[t_nb3[b], t_sm], writes=[t_sm])
            kb.op("dve", lambda e: e.reciprocal(out=rc[:, :], in_=sm[:, :, 64]), reads=[t_sm], writes=[t_sm])
            kb.op("dve", lambda e: e.tensor_tensor(out=yb32[:, :, :], in0=sm[:, :, 0:64], in1=_bc(rc[:, :].unsqueeze(2), [128, 4, 64]), op=ALU.mult),
                  reads=[t_sm], writes=[t_sm])
            kb.op("dve", lambda e: e.tensor_tensor(out=ybb[:, :], in0=yb32[:, :, :].rearrange("p h e -> p (h e)"), in1=gt[b][:, :], op=ALU.mult),
                  reads=[t_sm, t_nb3[b]], writes=[t_ybb])
            fns = [lambda e, pr=pr: e.transpose(ptp[0][:, pr, :], ybb[:, pr * 128:(pr + 1) * 128], self.ident_bf[:, :]) for pr in range(2)]
            kb.mm_group(fns, reads=[t_ybb, self.t_const], writes=[t_ptp[0]])
            kb.op("act", lambda e: e.activation(out=ybo[b][:, :, :], in_=ptp[0][:, 0:2, :], func=AF.Copy), reads=[t_ptp[0]], writes=[t_ybo[b]])
            kb.dma("pool", self.s_ybT[:, t0:t0 + 128].rearrange("(c p) t -> p c t", p=128), ybo[b][:, :, :], reads=[t_ybo[b]])

    def phase2c(self, l):
        kb, nc = self.kb, self.nc
        S = self.S
        NC = S // 128
        UT = [kb.sb([128, 128], F32, "UTf"), kb.sb([128, 128], F32, "UTb")]
        SM = [kb.sb([128, 128], F32, "Sf"), kb.sb([128, 128], F32, "Sb")]
        t_msk = Trk()
        for (tile_, pat, cm, cmp_) in ((UT[0], 1, -1, ALU.is_ge), (UT[1], -1, 1, ALU.is_ge), (SM[0], -1, 1, ALU.is_gt), (SM[1], 1, -1, ALU.is_gt)):
            kb.op("pool", lambda e: e.affine_select(out=tile_[:, :], in_=self.ones_f[:, :], pattern=[[pat, 128]], compare_op=cmp_, fill=0.0,
                                                    base=0, channel_multiplier=cm), reads=[self.t_const], writes=[t_msk])
        cw = kb.sb([128, 8, 5], F32, "cw")
        cb_ = kb.sb([128, 8], F32, "cb")
        Abc = kb.sb([128, 16], F32, "Abc")
        dsk = kb.sb([128, 8], F32, "dsk")
        nwc = kb.sb([128, 512], F32, "nwc")
        eps_t = kb.sb([128, 1], F32, "eps2c")
        t_par = Trk()
        kb.dma("sp", cw[:, :, :], self.conv_w[l], writes=[t_par])
        kb.dma("sp", cb_[:, :], self.conv_b[l], writes=[t_par])
        kb.dma("sp", Abc[:, :], self.a_log[l:l + 1, :].partition_broadcast(128), writes=[t_par])
        kb.dma("sp", dsk[:, :], self.d_skip[l:l + 1, :].partition_broadcast(128), writes=[t_par])
        kb.dma("sp", nwc[:, :], self.ssm_nw[l:l + 1, :].partition_broadcast(128), writes=[t_par])
        kb.op("act", lambda e: e.activation(out=Abc[:, :], in_=Abc[:, :], func=AF.Exp), reads=[t_par], writes=[t_par])
        kb.op("dve", lambda e: e.tensor_scalar(out=Abc[:, :], in0=Abc[:, :], scalar1=-1.0, scalar2=None, op0=ALU.mult), reads=[t_par], writes=[t_par])
        kb.op("dve", lambda e: e.memset(eps_t[:, :], EPS), writes=[t_par])
        TC = min(2048, S)
        with ExitStack() as es2:
            old_es = kb.es
            kb.es = es2
            xin = [kb.sb([128, S + 4], BF16, "xin") for _ in range(2)]
            t_xin = [Trk(), Trk()]
            cacc = kb.sb([128, TC], F32, "cacc")
            t_cacc = Trk()
            cout = [kb.sb([128, TC], BF16, "cout") for _ in range(2)]
            t_cout = [Trk(), Trk()]
            for b in range(2):
                kb.op("dve", lambda e: e.memset(xin[b][:, 0:2], 0.0), writes=[t_xin[b]])
                kb.op("dve", lambda e: e.memset(xin[b][:, S + 2:S + 4], 0.0), writes=[t_xin[b]])
            i = 0
            for cb in range(8):
                b = cb % 2
                for c0 in range(0, S, 2048):
                    c1 = min(S, c0 + 2048)
                    kb.dma("sp", xin[b][:, 2 + c0:2 + c1], self.s_xbc[cb * 128:(cb + 1) * 128, c0:c1], writes=[t_xin[b]])
                for c0 in range(0, S, TC):
                    o = i % 2
                    i += 1
                    kb.op("dve", lambda e: e.tensor_scalar(out=cacc[:, :], in0=xin[b][:, c0:c0 + TC], scalar1=cw[:, cb, 0:1], scalar2=None, op0=ALU.mult),
                          reads=[t_xin[b], t_par], writes=[t_cacc])
                    for j in range(1, 5):
                        kb.op("dve", lambda e: e.scalar_tensor_tensor(out=cacc[:, :], in0=xin[b][:, c0 + j:c0 + j + TC], scalar=cw[:, cb, j:j + 1], in1=cacc[:, :],
                                                                      op0=ALU.mult, op1=ALU.add),
                              reads=[t_xin[b], t_par, t_cacc], writes=[t_cacc])
                    kb.op("act", lambda e: e.activation(out=cout[o][:, :], in_=cacc[:, :], func=AF.Silu, bias=cb_[:, cb:cb + 1], scale=1.0),
                          reads=[t_cacc, t_par], writes=[t_cout[o]])
                    kb.dma("pool", self.s_xcv[cb * 128:(cb + 1) * 128, c0:c0 + TC], cout[o][:, :], reads=[t_cout[o]])
            kb.es = old_es
        kb.barrier()
        xT = [kb.sb([128, 4, 128], BF16, "xT2c") for _ in range(2)]
        BT = [kb.sb([128, 2, 128], BF16, "BT2c") for _ in range(2)]
        CT = [kb.sb([128, 2, 128], BF16, "CT2c") for _ in range(2)]
        dtt = [kb.sb([128, 16], F32, "dt2c") for _ in range(2)]
        zt = [kb.sb([128, 512], BF16, "z2c") for _ in range(2)]
        ybl = [kb.sb([128, 512], F32, "ybl2c") for _ in range(2)]
        t_ld = [Trk(), Trk()]
        ptxb = kb.ps([128, 6, 128], BF16, "ptxb")
        t_ptxb = PTrk()
        pcum = kb.ps([128, 16], F32, "pcum")
        t_pcum = PTrk()
        pGT = kb.ps([128, 2, 128], F32, "pGT")
        t_pGT = PTrk()
        pseg = [kb.ps([128, 4, 128], F32, "pseg") for _ in range(2)]
        t_pseg = [PTrk(), PTrk()]
        pd = kb.ps([128, 8, 64], F32, "pd")
        po = kb.ps([128, 8, 64], F32, "po")
        pst = kb.ps([128, 8, 64], F32, "pst")
        t_pd, t_po, t_pst = PTrk(), PTrk(), PTrk()
        Btok = kb.sb([128, 2, 128], BF16, "Btok")
        t_Btok = Trk()
        xdt = kb.sb([128, 8, 64], BF16, "xdt")
        xdd = kb.sb([128, 8, 64], BF16, "xdd")
        t_xdt, t_xdd = Trk(), Trk()
        a_t = kb.sb([128, 8], F32, "a_t")
        t_a = Trk()
        ecs = kb.sb([128, 16], F32, "ecs")
        t_ecs = Trk()
        Amat = [kb.sb([128, 128], F32, "Amat") for _ in range(4)]
        t_Amat = [Trk() for _ in range(4)]
        LT = [kb.sb([128, 4, 128], F32, "LT") for _ in range(2)]
        t_LT = [Trk(), Trk()]
        GTm = kb.sb([128, 2, 128], F32, "GTm")
        t_GTm = Trk()
        MT = [kb.sb([128, 4, 128], BF16, "MT") for _ in range(2)]
        t_MT = [Trk(), Trk()]
        S32 = kb.sb([128, 8, 64], F32, "S32")
        Sst = kb.sb([128, 8, 64], BF16, "Sst")
        t_S32, t_Sst = Trk(), Trk()
        ytmp = kb.sb([128, 8, 64], F32, "ytmp")
        yacc = [kb.sb([128, 8, 64], F32, "yacc") for _ in range(2)]
        t_ytmp = Trk()
        t_yacc = [Trk(), Trk()]
        u32 = kb.sb([128, 512], F32, "u32")
        junk = kb.sb([128, 512], BF16, "junk2c")
        stt = kb.sb([128, 4], F32, "stt2c")
        t_u, t_junk, t_stt = Trk(), Trk(), Trk()
        ycb = kb.sb([128, 512], BF16, "ycb")
        t_ycb = Trk()
        yco = [kb.sb([128, 4, 128], BF16, "yco") for _ in range(2)]
        t_yco = [Trk(), Trk()]

        for dr in (1, 0):
            final = (dr == 0)
            order = list(range(NC)) if dr == 0 else list(range(NC - 1, -1, -1))
            kb.op("dve", lambda e: e.memset(S32[:, :, :], 0.0), writes=[t_S32])
            kb.op("dve", lambda e: e.memset(Sst[:, :, :], 0.0), writes=[t_Sst])
            dcol = 127 if dr == 0 else 0

            def load(ci):
                c = order[ci]
                b = ci % 2
                t0 = c * 128
                kb.dma("sp", xT[b][:, :, :], self.s_xcv[0:512, t0:t0 + 128].rearrange("(c p) t -> p c t", p=128), writes=[t_ld[b]])
                kb.dma("sp", BT[b][:, :, :], self.s_xcv[512:768, t0:t0 + 128].rearrange("(c p) t -> p c t", p=128), writes=[t_ld[b]])
                kb.dma("sp", CT[b][:, :, :], self.s_xcv[768:1024, t0:t0 + 128].rearrange("(c p) t -> p c t", p=128), writes=[t_ld[b]])
                kb.dma("sp", dtt[b][:, :], self.s_dt[t0:t0 + 128, :], writes=[t_ld[b]])
                if final:
                    kb.dma("sp", zt[b][:, :], self.s_zc[t0:t0 + 128, :], writes=[t_ld[b]])
                    kb.dma("sp", ybl[b][:, :], self.s_yb32[t0:t0 + 128, :], writes=[t_ld[b]])

            load(0)
            for ci in range(NC):
                c = order[ci]
                b = ci % 2
                t0 = c * 128
                if ci + 1 < NC:
                    load(ci + 1)
                fns = [lambda e, q=q: e.transpose(ptxb[:, q, :], xT[b][:, q, :], self.ident_bf[:, :]) for q in range(4)]
                fns += [lambda e, q=q: e.transpose(ptxb[:, 4 + q, :], BT[b][:, q, :], self.ident_bf[:, :]) for q in range(2)]
                kb.mm_group(fns, reads=[t_ld[b], self.t_const], writes=[t_ptxb])
                xtok = ptxb[:, 0:4, :].rearrange("p c (h e) -> p (c h) e", e=64)
                kb.op("dve", lambda e: e.tensor_tensor(out=xdt[:, :, :], in0=xtok, in1=_bc(dtt[b][:, dr * 8:dr * 8 + 8].unsqueeze(2), [128, 8, 64]), op=ALU.mult),
                      reads=[t_ptxb, t_ld[b]], writes=[t_xdt])
                kb.op("act", lambda e: e.activation(out=Btok[:, :, :], in_=ptxb[:, 4:6, :], func=AF.Copy), reads=[t_ptxb], writes=[t_Btok])
                kb.op("dve", lambda e: e.tensor_tensor(out=a_t[:, :], in0=dtt[b][:, dr * 8:dr * 8 + 8], in1=Abc[:, dr * 8:dr * 8 + 8], op=ALU.mult),
                      reads=[t_ld[b], t_par], writes=[t_a])
                kb.mm_group([lambda e: e.matmul(pcum[:, 0:8], lhsT=UT[dr][:, :], rhs=a_t[:, :], start=True, stop=True),
                             lambda e: e.matmul(pcum[:, 8:16], lhsT=self.ones_f[:, :], rhs=a_t[:, :], start=True, stop=True)],
                            reads=[t_a, t_msk, self.t_const], writes=[t_pcum])
                kb.op("act", lambda e: e.activation(out=ecs[:, :], in_=pcum[:, :], func=AF.Exp), reads=[t_pcum], writes=[t_ecs])
                kb.mm_group([lambda e, gq=gq: e.matmul(pGT[:, gq, :], lhsT=BT[b][:, gq, :], rhs=CT[b][:, gq, :], start=True, stop=True) for gq in range(2)],
                            reads=[t_ld[b]], writes=[t_pGT])
                kb.op("dve", lambda e: e.tensor_tensor(out=GTm[:, :, :], in0=pGT[:, :, :], in1=_bc(UT[dr][:, :].unsqueeze(1), [128, 2, 128]), op=ALU.mult),
                      reads=[t_pGT, t_msk], writes=[t_GTm])
                for hg in range(2):
                    for hh in range(4):
                        h = hg * 4 + hh
                        kb.op("dve", lambda e: e.tensor_scalar(out=Amat[hh][:, :], in0=SM[dr][:, :], scalar1=a_t[:, h:h + 1], scalar2=None, op0=ALU.mult),
                              reads=[t_a, t_msk], writes=[t_Amat[hh]])
                        kb.op("pe", lambda e: e.matmul(pseg[hg][:, hh, :], lhsT=Amat[hh][:, :], rhs=UT[dr][:, :], start=True, stop=True),
                              reads=[t_Amat[hh], t_msk], writes=[t_pseg[hg]])
                    kb.op("act", lambda e: e.activation(out=LT[hg][:, :, :], in_=pseg[hg][:, :, :], func=AF.Exp), reads=[t_pseg[hg]], writes=[t_LT[hg]])
                    kb.op("dve", lambda e: e.tensor_tensor(out=MT[hg][:, :, :], in0=LT[hg][:, :, :], in1=_bc(GTm[:, hg:hg + 1, :], [128, 4, 128]), op=ALU.mult),
                          reads=[t_LT[hg], t_GTm], writes=[t_MT[hg]])
                    kb.op("dve", lambda e: e.tensor_tensor(out=xdd[:, hg * 4:hg * 4 + 4, :], in0=xdt[:, hg * 4:hg * 4 + 4, :],
                                                           in1=_bc(LT[hg][:, :, dcol:dcol + 1], [128, 4, 64]), op=ALU.mult),
                          reads=[t_xdt, t_LT[hg]], writes=[t_xdd])
                kb.mm_group([lambda e, h=h: e.matmul(pd[:, h, :], lhsT=MT[h // 4][:, h % 4, :], rhs=xdt[:, h, :], start=True, stop=True) for h in range(8)],
                            reads=[t_MT[0], t_MT[1], t_xdt], writes=[t_pd])
                kb.mm_group([lambda e, h=h: e.matmul(po[:, h, :], lhsT=CT[b][:, h // 4, :], rhs=Sst[:, h, :], start=True, stop=True) for h in range(8)],
                            reads=[t_ld[b], t_Sst], writes=[t_po])
                kb.mm_group([lambda e, h=h: e.matmul(pst[:, h, :], lhsT=Btok[:, h // 4, :], rhs=xdd[:, h, :], start=True, stop=True) for h in range(8)],
                            reads=[t_Btok, t_xdd], writes=[t_pst])
                ya = yacc[ci % 2]
                t_ya = t_yacc[ci % 2]
                kb.op("dve", lambda e: e.tensor_tensor(out=ytmp[:, :, :], in0=po[:, :, :], in1=_bc(ecs[:, 0:8].unsqueeze(2), [128, 8, 64]), op=ALU.mult),
                      reads=[t_po, t_ecs], writes=[t_ytmp])
                kb.op("dve", lambda e: e.tensor_tensor(out=ya[:, :, :], in0=ytmp[:, :, :], in1=pd[:, :, :], op=ALU.add),
                      reads=[t_pd, t_ytmp], writes=[t_ya])
                kb.op("dve", lambda e: e.tensor_tensor(out=S32[:, :, :], in0=S32[:, :, :], in1=_bc(ecs[:, 8:16].unsqueeze(2), [128, 8, 64]), op=ALU.mult),
                      reads=[t_ecs, t_S32], writes=[t_S32])
                kb.op("dve", lambda e: e.tensor_tensor(out=S32[:, :, :], in0=S32[:, :, :], in1=pst[:, :, :], op=ALU.add),
                      reads=[t_pst, t_S32], writes=[t_S32])
                kb.op("act", lambda e: e.activation(out=Sst[:, :, :], in_=S32[:, :, :], func=AF.Copy), reads=[t_S32], writes=[t_Sst])
                yaf = ya[:, :, :].rearrange("p h e -> p (h e)")
                if not final:
                    kb.dma("pool", self.s_yb32[t0:t0 + 128, :], yaf, reads=[t_ya])
                    continue
                kb.op("dve", lambda e: e.tensor_tensor(out=ytmp[:, :, :], in0=xtok, in1=_bc(dsk[:, :].unsqueeze(2), [128, 8, 64]), op=ALU.mult),
                      reads=[t_ptxb, t_par], writes=[t_ytmp])
                kb.op("dve", lambda e: e.tensor_tensor(out=yaf, in0=yaf, in1=ytmp[:, :, :].rearrange("p h e -> p (h e)"), op=ALU.add),
                      reads=[t_ytmp, t_ya], writes=[t_ya])
                kb.op("dve", lambda e: e.tensor_tensor(out=yaf, in0=yaf, in1=ybl[b][:, :], op=ALU.add), reads=[t_ld[b], t_ya], writes=[t_ya])
                kb.op("dve", lambda e: e.tensor_tensor(out=u32[:, :], in0=yaf, in1=zt[b][:, :], op=ALU.mult), reads=[t_ld[b], t_ya], writes=[t_u])
                kb.op("act", lambda e: e.activation(out=junk[:, :], in_=u32[:, :], func=AF.Square, accum_out=stt[:, 0:1]), reads=[t_u], writes=[t_junk, t_stt])
                kb.op("act", lambda e: e.activation(out=stt[:, 1:2], in_=stt[:, 0:1], func=AF.Sqrt, scale=1.0 / 512, bias=eps_t[:, 0:1]),
                      reads=[t_stt, t_par], writes=[t_stt])
                kb.op("dve", lambda e: e.reciprocal(out=stt[:, 2:3], in_=stt[:, 1:2]), reads=[t_stt], writes=[t_stt])
                kb.op("dve", lambda e: e.scalar_tensor_tensor(out=ycb[:, :], in0=u32[:, :], scalar=stt[:, 2:3], in1=nwc[:, :], op0=ALU.mult, op1=ALU.mult),
                      reads=[t_u, t_stt, t_par], writes=[t_ycb])
                fns = [lambda e, q=q: e.transpose(ptxb[:, q, :], ycb[:, q * 128:(q + 1) * 128], self.ident_bf[:, :]) for q in range(4)]
                kb.mm_group(fns, reads=[t_ycb, self.t_const], writes=[t_ptxb])
                kb.op("act", lambda e: e.activation(out=yco[b][:, :, :], in_=ptxb[:, 0:4, :], func=AF.Copy), reads=[t_ptxb], writes=[t_yco[b]])
                kb.dma("pool", self.s_ycT[:, t0:t0 + 128].rearrange("(c p) t -> p c t", p=128), yco[b][:, :, :], reads=[t_yco[b]])
            kb.barrier()

    def phase3(self, l, x_src):
        kb, nc = self.kb, self.nc
        S = self.S
        NT = S // 128
        br = [b for b in ("2a", "2b", "2c") if b in self.phases]
        Wg = kb.sb([128, 8, 3 * D], BF16, "Wg")
        Wo = kb.sb([128, 8, D], BF16, "Wo")
        Wa = kb.sb([128, 4, D], BF16, "Wa")
        Wb = kb.sb([128, 2, D], BF16, "Wb")
        Wc = kb.sb([128, 4, D], BF16, "Wc")
        bg = kb.sb([1, 3 * D], BF16, "bg")
        ones_b = kb.sb([1, 128], BF16, "ones_b")
        t_W = Trk()
        wv = self.w_in[l].rearrange("(k p) n -> p k n", p=128)
        for k in range(8):
            kb.dma("pool", Wg[:, k, :], wv[:, k, NCOL1:D_IN], writes=[t_W])
        kb.dma("pool", Wo[:, :, :], self.w_out[l].rearrange("(k p) n -> p k n", p=128), writes=[t_W])
        kb.dma("pool", Wa[:, :, :], self.w_pa[l].rearrange("(k p) n -> p k n", p=128), writes=[t_W])
        kb.dma("pool", Wb[:, :, :], self.w_pb[l].rearrange("(k p) n -> p k n", p=128), writes=[t_W])
        kb.dma("pool", Wc[:, :, :], self.w_pc[l].rearrange("(k p) n -> p k n", p=128), writes=[t_W])
        kb.dma("pool", bg[:, :], self.b_gate[l:l + 1, :], writes=[t_W])
        kb.op("dve", lambda e: e.memset(ones_b[:, :], 1.0), writes=[t_W])
        eps_t = kb.sb([128, 1], F32, "eps3")
        kb.op("dve", lambda e: e.memset(eps_t[:, :], EPS), writes=[t_W])

        NB = 2
        xt = [kb.sb([128, D], F32, "xt3") for _ in range(NB)]
        t_xt = [Trk() for _ in range(NB)]
        yaT = [kb.sb([128, 4, 128], BF16, "yaT3") for _ in range(NB)]
        ybT = [kb.sb([128, 2, 128], BF16, "ybT3") for _ in range(NB)]
        ycT = [kb.sb([128, 4, 128], BF16, "ycT3") for _ in range(NB)]
        t_y = [Trk() for _ in range(NB)]
        junk = kb.sb([128, D], BF16, "junk3")
        t_junk = Trk()
        st = kb.sb([128, 8], F32, "st3")
        t_st = Trk()
        h32 = kb.sb([128, D], F32, "h32_3")
        hb = kb.sb([128, D], BF16, "hb3")
        t_h = Trk()
        hT = kb.sb([128, 8, 128], BF16, "hT3")
        t_hT = Trk()
        ptp = [kb.ps([128, 4, 128], BF16, "ptp3") for _ in range(2)]
        t_ptp = [PTrk(), PTrk()]
        pg = [kb.ps([128, 512], F32, "pg3") for _ in range(2)]
        t_pg = [PTrk(), PTrk()]
        pp = [kb.ps([128, 512], F32, "pp3") for _ in range(2)]
        t_pp = [PTrk(), PTrk()]
        po = [kb.ps([128, 512], F32, "po3") for _ in range(2)]
        t_po = [PTrk(), PTrk()]
        sg = [kb.sb([128, 512], F32, "sg3") for _ in range(2)]
        t_sg = [Trk(), Trk()]
        merged = kb.sb([128, D], F32, "merged3")
        tmpm = kb.sb([128, 512], F32, "tmpm3")
        t_tmpm = Trk()
        t_merged = [Trk(), Trk()]
        mb = kb.sb([128, D], BF16, "mb3")
        t_mb = Trk()
        mT = kb.sb([128, 8, 128], BF16, "mT3")
        t_mT = Trk()
        xn = [kb.sb([128, D], F32, "xn3") for _ in range(2)]
        t_xn = [Trk(), Trk()]
        ci = 0
        for T in range(NT):
            g = T % NB
            t0 = T * 128
            kb.dma("sp", xt[g][:, :], x_src[t0:t0 + 128, :], writes=[t_xt[g]])
            if "2a" in br:
                kb.dma("sp", yaT[g][:, :, :], self.s_yaT[:, t0:t0 + 128].rearrange("(c p) t -> p c t", p=128), writes=[t_y[g]])
            if "2b" in br:
                kb.dma("sp", ybT[g][:, :, :], self.s_ybT[:, t0:t0 + 128].rearrange("(c p) t -> p c t", p=128), writes=[t_y[g]])
            if "2c" in br:
                kb.dma("sp", ycT[g][:, :, :], self.s_ycT[:, t0:t0 + 128].rearrange("(c p) t -> p c t", p=128), writes=[t_y[g]])
            kb.op("act", lambda e: e.activation(out=junk[:, :], in_=xt[g][:, :], func=AF.Square, accum_out=st[:, 0:1]),
                  reads=[t_xt[g]], writes=[t_junk, t_st])
            kb.op("act", lambda e: e.activation(out=st[:, 1:2], in_=st[:, 0:1], func=AF.Sqrt, scale=1.0 / D, bias=eps_t[:, 0:1]),
                  reads=[t_st, t_W], writes=[t_st])
            kb.op("dve", lambda e: e.reciprocal(out=st[:, 2:3], in_=st[:, 1:2]), reads=[t_st], writes=[t_st])
            kb.op("dve", lambda e: e.scalar_tensor_tensor(out=h32[:, :], in0=xt[g][:, :], scalar=st[:, 2:3], in1=self.A_t[:, :],
                                                          op0=ALU.mult, op1=ALU.mult),
                  reads=[t_xt[g], t_st, self.t_mod], writes=[t_h])
            kb.op("dve", lambda e: e.tensor_tensor(out=hb[:, :], in0=h32[:, :], in1=self.Sh_t[:, :], op=ALU.add),
                  reads=[t_h, self.t_mod], writes=[t_h])
            for half in range(2):
                pq, tpq = ptp[half], t_ptp[half]
                fns = [lambda e, q=q: e.transpose(pq[:, q, :], hb[:, (half * 4 + q) * 128:(half * 4 + q + 1) * 128], self.ident_bf[:, :])
                       for q in range(4)]
                kb.mm_group(fns, reads=[t_h, self.t_const], writes=[tpq])
                kb.op("act", lambda e: e.activation(out=hT[:, half * 4:half * 4 + 4, :], in_=pq[:, :, :], func=AF.Copy),
                      reads=[tpq], writes=[t_hT])
            for half in range(2):
                cs = slice(half * 512, (half + 1) * 512)
                first = True
                for (bn, gi, Wx, yx, nk) in (("2a", 0, Wa, yaT, 4), ("2b", 1, Wb, ybT, 2), ("2c", 2, Wc, ycT, 4)):
                    if bn not in br:
                        continue
                    b = ci % 2
                    ci += 1
                    fns = [lambda e, k=k: e.matmul(pg[b][:, :], lhsT=hT[:, k, :], rhs=Wg[:, k, gi * D + half * 512:gi * D + (half + 1) * 512],
                                                   start=(k == 0), stop=False) for k in range(8)]
                    fns.append(lambda e: e.matmul(pg[b][:, :], lhsT=ones_b[0:1, :], rhs=bg[0:1, gi * D + half * 512:gi * D + (half + 1) * 512],
                                                  start=False, stop=True))
                    kb.mm_group(fns, reads=[t_hT, t_W], writes=[t_pg[b]])
                    kb.op("act", lambda e: e.activation(out=sg[b][:, :], in_=pg[b][:, :], func=AF.Sigmoid), reads=[t_pg[b]], writes=[t_sg[b]])
                    fns = [lambda e, k=k: e.matmul(pp[b][:, :], lhsT=yx[g][:, k, :], rhs=Wx[:, k, cs], start=(k == 0), stop=(k == nk - 1))
                           for k in range(nk)]
                    kb.mm_group(fns, reads=[t_y[g], t_W], writes=[t_pp[b]])
                    if first:
                        kb.op("dve", lambda e: e.tensor_tensor(out=merged[:, cs], in0=pp[b][:, :], in1=sg[b][:, :], op=ALU.mult),
                              reads=[t_pp[b], t_sg[b]], writes=[t_merged[half]])
                        first = False
                    else:
                        kb.op("dve", lambda e: e.tensor_tensor(out=tmpm[:, :], in0=pp[b][:, :], in1=sg[b][:, :], op=ALU.mult),
                              reads=[t_pp[b], t_sg[b]], writes=[t_tmpm])
                        kb.op("dve", lambda e: e.tensor_tensor(out=merged[:, cs], in0=merged[:, cs], in1=tmpm[:, :], op=ALU.add),
                              reads=[t_tmpm, t_merged[half]], writes=[t_merged[half]])
                kb.op("act", lambda e: e.activation(out=mb[:, cs], in_=merged[:, cs], func=AF.Copy), reads=[t_merged[half]], writes=[t_mb])
                pq, tpq = ptp[half], t_ptp[half]
                fns = [lambda e, q=q: e.transpose(pq[:, q, :], mb[:, (half * 4 + q) * 128:(half * 4 + q + 1) * 128], self.ident_bf[:, :])
                       for q in range(4)]
                kb.mm_group(fns, reads=[t_mb, self.t_const], writes=[tpq])
                kb.op("act", lambda e: e.activation(out=mT[:, half * 4:half * 4 + 4, :], in_=pq[:, :, :], func=AF.Copy),
                      reads=[tpq], writes=[t_mT])
            o = T % 2
            for half in range(2):
                cs = slice(half * 512, (half + 1) * 512)
                fns = [lambda e, k=k: e.matmul(po[half][:, :], lhsT=mT[:, k, :], rhs=Wo[:, k, cs], start=(k == 0), stop=(k == 7)) for k in range(8)]
                kb.mm_group(fns, reads=[t_mT, t_W], writes=[t_po[half]])
                kb.op("dve", lambda e: e.tensor_tensor(out=xn[o][:, cs], in0=po[half][:, :], in1=self.G_t[:, cs], op=ALU.mult),
                      reads=[t_po[half], self.t_mod], writes=[t_xn[o]])
            kb.op("dve", lambda e: e.tensor_tensor(out=xn[o][:, :], in0=xn[o][:, :], in1=xt[g][:, :], op=ALU.add),
                  reads=[t_xt[g], t_xn[o]], writes=[t_xn[o]])
            kb.dma("pool", self.out[t0:t0 + 128, :], xn[o][:, :], reads=[t_xn[o]])

def rope_tables(S):
    t = np.arange(S)
    row = (t // GRID_W).astype(np.float32)
    col = (t % GRID_W).astype(np.float32)
    quarter = 16
    freqs = (10000.0 ** (-np.arange(quarter, dtype=np.float32) / quarter)).astype(np.float32)
    C = np.zeros((S, 64), np.float32)
    Sg = np.zeros((S, 64), np.float32)
    for hi, pos in enumerate((row, col)):
        ang = (pos[:, None] * freqs[None, :]).astype(np.float32)
        c, s = np.cos(ang), np.sin(ang)
        C[:, hi * 32:hi * 32 + 16] = c
        C[:, hi * 32 + 16:hi * 32 + 32] = c
        Sg[:, hi * 32:hi * 32 + 16] = -s
        Sg[:, hi * 32 + 16:hi * 32 + 32] = s
    return C, Sg


def t5_bucket_np(rel):
    nb = 16
    max_exact = 8
    ret = np.where(rel > 0, nb, 0)
    n = np.abs(rel)
    nf = np.maximum(n, 1).astype(np.float32)
    large = max_exact + (np.log(nf / np.float32(max_exact)) / np.float32(math.log(1024 / max_exact))
                         * np.float32(nb - max_exact)).astype(np.int32)
    large = np.minimum(large, nb - 1)
    return ret + np.where(n < max_exact, n, large)


def b_tables(rel_bias):
    rel_bias = np.asarray(rel_bias, np.float32)
    k = np.arange(128)[:, None]
    q = np.arange(128)[None, :]
    bias = np.zeros((3, 4, 3, 128, 128), np.float32)
    mask = np.zeros((3, 128, 128), np.float32)
    for oi, o in enumerate((-1, 0, 1)):
        relp = 128 * o + k - q
        mask[oi] = (np.abs(relp) <= 64).astype(np.float32)
        for g, d in enumerate(B_DIL):
            bk = t5_bucket_np(np.clip(relp, -64, 64) * d)
            for hs in range(4):
                bias[g, hs, oi] = rel_bias[bk, g * 4 + hs]
    bias_t = np.ascontiguousarray(bias.reshape(36, 128, 128).transpose(1, 0, 2))
    mask_t = np.ascontiguousarray(mask.transpose(1, 0, 2))
    return bias_t, mask_t


def make_inputs(inp, b, S, L):
    f = lambda a: np.ascontiguousarray(np.asarray(a), dtype=np.float32)
    C, Sg = rope_tables(S)
    BT, MT = b_tables(inp["rel_bias"])
    m = {
        "x": f(inp["x"][b][:S]),
        "c": f(np.asarray(inp["c"][b]).reshape(8, 128).T),
        "norm_w": f(inp["norm_w"][:L]),
        "w_ada": f(inp["w_ada"][:L]),
        "b_ada": f(inp["b_ada"][:L]),
        "w_in": f(inp["w_in"][:L]),
        "b_gate": f(inp["b_gate"][:L]),
        "qk_w": f(np.stack([inp["q_norm_a"][:L], inp["k_norm_a"][:L], inp["q_norm_b"][:L], inp["k_norm_b"][:L]], axis=1)),
        "rope_c": C, "rope_s": Sg,
        "dt_bias": f(np.asarray(inp["dt_bias"][:L]).reshape(L, 16)),
        "bias_tab": BT, "mask_tab": MT,
        "conv_w": f(np.asarray(inp["conv_w"][:L]).reshape(L, 5, 8, 128).transpose(0, 3, 2, 1)),
        "conv_b": f(np.asarray(inp["conv_b"][:L]).reshape(L, 8, 128).transpose(0, 2, 1)),
        "a_log": f(np.asarray(inp["a_log"][:L]).reshape(L, 16)),
        "d_skip": f(inp["d_skip"][:L]), "ssm_norm_w": f(inp["ssm_norm_w"][:L]),
        "w_out": f(inp["w_out"][:L]), "w_proj_a": f(inp["w_proj_a"][:L]),
        "w_proj_b": f(inp["w_proj_b"][:L]), "w_proj_c": f(inp["w_proj_c"][:L]),
    }
    return m


_PROG = {}


def kernel(**inputs):
    S, L = 8192, DEPTH
    key = (S, L)
    if key not in _PROG:
        _PROG[key] = Prog(S, L)
    prog = _PROG[key]
    in_maps = [make_inputs(inputs, b, S, L) for b in range(4)]
    res = run_bass_kernel_spmd(prog.nc, in_maps, core_ids=list(range(4)))
    return np.stack([np.asarray(r["out"]) for r in res.results], axis=0).astype(np.float32)
```

```python
from contextlib import ExitStack
import math
import numpy as np
import ml_dtypes
import concourse.bass as bass
import concourse.mybir as mybir
from concourse.bass_utils import run_bass_kernel_spmd

F32 = mybir.dt.float32
BF16 = mybir.dt.bfloat16
ALU = mybir.AluOpType
AF = mybir.ActivationFunctionType
AX = mybir.AxisListType

D = 1024
DEPTH = 2
EPS = 1e-6
NCOL1 = 5392
D_IN = 8464
GRID_W = 64
B_DIL = (1, 4, 16)


class Trk:
    __slots__ = ("w", "r")

    def __init__(self):
        self.w = []
        self.r = []


class PTrk(Trk):
    __slots__ = ()


class KB:
    def __init__(self, nc, es):
        self.nc = nc
        self.es = es
        self.eng = {"pe": nc.tensor, "act": nc.scalar, "dve": nc.vector, "pool": nc.gpsimd, "sp": nc.sync}
        self.semh = {}
        self.cnt = {}
        self.seen = {e: {} for e in self.eng}
        for e in ("pe", "act", "dve", "pool"):
            self.semh[e] = es.enter_context(nc.semaphore("s_" + e))
            self.cnt[e] = 0
        self.dq = {}
        for q, n in (("sp", 12), ("pool", 8), ("act", 4)):
            names = []
            for i in range(n):
                nm = "d_%s%d" % (q, i)
                self.semh[nm] = es.enter_context(nc.semaphore(nm))
                self.cnt[nm] = 0
                names.append(nm)
            self.dq[q] = [names, 0]
        self.uid = 0

    def sb(self, shape, dtype, name=None):
        self.uid += 1
        return self.es.enter_context(self.nc.sbuf_tensor("%s_%d" % (name or "t", self.uid), list(shape), dtype))

    def ps(self, shape, dtype, name=None):
        self.uid += 1
        esz = 4 if dtype == F32 else 2
        full = self.es.enter_context(self.nc.psum_tensor("%s_%d" % (name or "p", self.uid), [128, 2048 // esz], dtype))
        n = 1
        for d_ in shape[1:]:
            n *= d_
        assert n * esz <= 2048 and shape[0] == 128
        v = full[:, 0:n]
        if len(shape) == 3:
            v = v.rearrange("p (a b) -> p a b", b=shape[2])
        return v

    def _wait(self, e, tickets):
        need = {}
        for (s, v) in tickets:
            if v > need.get(s, 0):
                need[s] = v
        seen = self.seen[e]
        for s, v in need.items():
            if seen.get(s, 0) >= v:
                continue
            if s == e and e == "pe":
                continue
            self.eng[e].wait_ge(self.semh[s], v)
            seen[s] = v

    @staticmethod
    def _addr(lst, tk):
        for i, (s, v) in enumerate(lst):
            if s == tk[0]:
                if tk[1] > v:
                    lst[i] = tk
                return
        lst.append(tk)

    def _deps(self, reads, writes):
        tickets = []
        for t in reads:
            tickets += t.w
            if isinstance(t, PTrk):
                tickets += t.r
        for t in writes:
            tickets += t.w
            tickets += t.r
        return tickets

    def _mark(self, tk, reads, writes):
        for t in reads:
            if isinstance(t, PTrk):
                t.r = [tk]
            else:
                self._addr(t.r, tk)
        for t in writes:
            t.w = [tk]
            t.r = []

    def op(self, e, fn, reads=(), writes=()):
        self._wait(e, self._deps(reads, writes))
        ins = fn(self.eng[e])
        self.cnt[e] += 1
        ins.then_inc(self.semh[e], 1)
        tk = (e, self.cnt[e])
        self._mark(tk, reads, writes)
        return tk

    def mm_group(self, fns, reads=(), writes=()):
        self._wait("pe", self._deps(reads, writes))
        ins = None
        for fn in fns:
            ins = fn(self.eng["pe"])
        self.cnt["pe"] += 1
        ins.then_inc(self.semh["pe"], 1)
        tk = ("pe", self.cnt["pe"])
        self._mark(tk, reads, writes)
        return tk

    def dma(self, q, out, in_, reads=(), writes=(), **kw):
        names, i = self.dq[q]
        nm = names[i % len(names)]
        self.dq[q][1] = i + 1
        tickets = self._deps(reads, writes)
        tickets.append((nm, self.cnt[nm]))
        self._wait(q, tickets)
        self.eng[q].dma_start(out=out, in_=in_, **kw).then_inc(self.semh[nm], 16)
        self.cnt[nm] += 16
        tk = (nm, self.cnt[nm])
        self._mark(tk, reads, writes)
        return tk

    def barrier(self):
        allt = [(s, c) for s, c in self.cnt.items() if c > 0]
        for e in self.eng:
            self._wait(e, allt)


def _bc(ap, shape):
    return ap.to_broadcast(list(shape))


class Prog:
    def __init__(self, S=8192, layers=2, phases=("0", "1", "2a", "2b", "2c", "3"), debug=False, ext_scratch=True):
        self.ext_scratch = ext_scratch
        self.S = S
        self.L = layers
        self.phases = phases
        self.debug = debug
        self.nc = bass.Bass("TRN2", target_bir_lowering=False)
        self.build()

    def build(self):
        nc = self.nc
        S = self.S
        L = self.L
        dt = nc.dram_tensor

        def din(name, shape, dtype=F32):
            return dt(name, list(shape), dtype, kind="ExternalInput").ap()

        def dscr(name, shape, dtype):
            return dt(name, list(shape), dtype, kind="ExternalOutput" if (self.debug or self.ext_scratch) else "Internal").ap()

        self.x_in = din("x", [S, D])
        self.c_in = din("c", [128, 8])
        self.norm_w = din("norm_w", [L, D])
        self.w_ada = din("w_ada", [L, D, 3 * D])
        self.b_ada = din("b_ada", [L, 3 * D])
        self.w_in = din("w_in", [L, D, D_IN])
        self.b_gate = din("b_gate", [L, 3 * D])
        self.qk_w = din("qk_w", [L, 4, 64])
        self.rope_c = din("rope_c", [S, 64])
        self.rope_s = din("rope_s", [S, 64])
        self.dt_bias = din("dt_bias", [L, 16])
        self.w_out = din("w_out", [L, D, D])
        self.w_pa = din("w_proj_a", [L, 512, D])
        self.w_pb = din("w_proj_b", [L, 256, D])
        self.w_pc = din("w_proj_c", [L, 512, D])
        self.out = dt("out", [S, D], F32, kind="ExternalOutput").ap()
        self.s_qa = dscr("s_qa", [S, 512], BF16)
        self.s_ka = dscr("s_ka", [S, 128], BF16)
        self.s_va = dscr("s_va", [S, 128], BF16)
        self.s_gaT = dscr("s_gaT", [512, S], BF16)
        self.s_yaT = dscr("s_yaT", [512, S], BF16)
        self.s_qb = dscr("s_qb", [S, 768], BF16)
        self.s_kb = dscr("s_kb", [S, 768], BF16)
        self.s_vb = dscr("s_vb", [S, 768], BF16)
        self.s_gb = dscr("s_gb", [S, 256], BF16)
        self.s_zc = dscr("s_zc", [S, 512], BF16)
        self.s_xbc = dscr("s_xbc", [1024, S], BF16)
        self.s_dt = dscr("s_dt", [S, 16], F32)
        self.s_ybT = dscr("s_ybT", [256, S], BF16)
        self.s_nb = [dscr("s_nb%d" % g, [S, 260], F32) for g in range(3)]
        self.bias_tab = din("bias_tab", [128, 36, 128])
        self.mask_tab = din("mask_tab", [128, 3, 128])
        self.s_ycT = dscr("s_ycT", [512, S], BF16)
        self.s_xcv = dscr("s_xcv", [1024, S], BF16)
        self.s_yb32 = dscr("s_yb32", [S, 512], F32)
        self.conv_w = din("conv_w", [L, 128, 8, 5])
        self.conv_b = din("conv_b", [L, 128, 8])
        self.a_log = din("a_log", [L, 16])
        self.d_skip = din("d_skip", [L, 8])
        self.ssm_nw = din("ssm_norm_w", [L, 512])
        if self.debug:
            self.dbg_mod = dt("dbg_mod", [128, 3 * D], F32, kind="ExternalOutput").ap()

        with ExitStack() as es_top:
            kb = KB(nc, es_top)
            self.kb = kb
            self.ident_bf = kb.sb([128, 128], BF16, "identb")
            self.ident_f = kb.sb([128, 128], F32, "identf")
            self.ones_f = kb.sb([128, 128], F32, "onesf")
            self.t_const = Trk()
            kb.op("pool", lambda e: e.memset(self.ones_f[:, :], 1.0), writes=[self.t_const])
            kb.op("pool", lambda e: e.memset(self.ident_f[:, :], 0.0), writes=[self.t_const])
            kb.op("pool", lambda e: e.affine_select(out=self.ident_f[:, :], in_=self.ident_f[:, :],
                                                    pattern=[[-1, 128]], compare_op=ALU.not_equal, fill=1.0,
                                                    base=0, channel_multiplier=1),
                  reads=[self.t_const], writes=[self.t_const])
            kb.op("dve", lambda e: e.tensor_copy(out=self.ident_bf[:, :], in_=self.ident_f[:, :]),
                  reads=[self.t_const], writes=[self.t_const])
            self.A_t = kb.sb([128, D], F32, "A_t")
            self.Sh_t = kb.sb([128, D], F32, "Sh_t")
            self.G_t = kb.sb([128, D], F32, "G_t")
            self.t_mod = Trk()

            for l in range(L):
                x_src = self.x_in if l == 0 else self.out
                if "0" in self.phases:
                    with ExitStack() as es:
                        kb.es = es
                        self.phase0(l)
                        kb.barrier()
                if "1" in self.phases:
                    with ExitStack() as es:
                        kb.es = es
                        self.phase1(l, x_src)
                        kb.barrier()
                for ph, fn in (("2a", self.phase2a), ("2b", self.phase2b), ("2c", self.phase2c)):
                    if ph in self.phases:
                        with ExitStack() as es:
                            kb.es = es
                            fn(l)
                            kb.barrier()
                if "3" in self.phases:
                    with ExitStack() as es:
                        kb.es = es
                        self.phase3(l, x_src)
                        kb.barrier()
            kb.es = es_top
            kb.barrier()

    def phase0(self, l):
        kb, nc = self.kb, self.nc
        c_sb = kb.sb([128, 8], F32, "c_sb")
        c_act = kb.sb([128, 8], F32, "c_act")
        cl = kb.sb([128, 8, 128], F32, "cl")
        wa = kb.sb([128, 8, 3 * D], F32, "wa")
        bb = kb.sb([128, 3 * D], F32, "bb")
        nw = kb.sb([128, D], F32, "nw")
        mod = kb.sb([128, 3 * D], F32, "mod")
        t_c, t_wa, t_bb, t_cl, t_mod = Trk(), [Trk() for _ in range(8)], Trk(), Trk(), Trk()
        kb.dma("sp", c_sb[:, :], self.c_in[:, :], writes=[t_c])
        wv = self.w_ada[l].rearrange("(k p) n -> p k n", p=128)
        for k in range(8):
            kb.dma("sp", wa[:, k, :], wv[:, k, :], writes=[t_wa[k]])
        kb.dma("sp", bb[:, :], self.b_ada[l:l + 1, :].partition_broadcast(128), writes=[t_bb])
        kb.dma("sp", nw[:, :], self.norm_w[l:l + 1, :].partition_broadcast(128), writes=[t_bb])
        kb.op("act", lambda e: e.activation(out=c_act[:, :], in_=c_sb[:, :], func=AF.Silu), reads=[t_c], writes=[t_c])
        for k in range(8):
            kb.op("dve", lambda e: e.tensor_scalar(out=cl[:, k, :], in0=self.ones_f[:, :], scalar1=c_act[:, k:k + 1],
                                                   scalar2=None, op0=ALU.mult),
                  reads=[t_c, self.t_const], writes=[t_cl])
        pm = [kb.ps([128, 512], F32, "pm") for _ in range(2)]
        t_pm = [PTrk(), PTrk()]
        for n in range(6):
            b = n % 2
            fns = []
            for k in range(8):
                fns.append(lambda e, k=k: e.matmul(pm[b][:, :], lhsT=cl[:, k, :], rhs=wa[:, k, n * 512:(n + 1) * 512],
                                                   start=(k == 0), stop=(k == 7)))
            kb.mm_group(fns, reads=[t_cl] + t_wa, writes=[t_pm[b]])
            kb.op("dve", lambda e: e.tensor_tensor(out=mod[:, n * 512:(n + 1) * 512], in0=pm[b][:, :],
                                                   in1=bb[:, n * 512:(n + 1) * 512], op=ALU.add),
                  reads=[t_pm[b], t_bb], writes=[t_mod])
        kb.op("dve", lambda e: e.tensor_copy(out=self.Sh_t[:, :], in_=mod[:, 0:D]), reads=[t_mod], writes=[self.t_mod])
        kb.op("dve", lambda e: e.scalar_tensor_tensor(out=self.A_t[:, :], in0=mod[:, D:2 * D], scalar=1.0, in1=nw[:, :],
                                                      op0=ALU.add, op1=ALU.mult),
              reads=[t_mod, t_bb], writes=[self.t_mod])
        kb.op("dve", lambda e: e.tensor_copy(out=self.G_t[:, :], in_=mod[:, 2 * D:3 * D]), reads=[t_mod], writes=[self.t_mod])
        if self.debug and l == 0:
            kb.dma("pool", self.dbg_mod[:, :], mod[:, :], reads=[t_mod])

    def phase1(self, l, x_src):
        kb, nc = self.kb, self.nc
        S = self.S
        NT = S // 512
        W = kb.sb([128, 8, NCOL1], BF16, "W1")
        t_W = [Trk() for _ in range(8)]
        wv = self.w_in[l].rearrange("(k p) n -> p k n", p=128)
        for k in range(8):
            kb.dma("pool", W[:, k, :], wv[:, k, 0:NCOL1], writes=[t_W[k]])
        qkw = kb.sb([128, 4, 64], F32, "qkw")
        dtb = kb.sb([128, 16], F32, "dtb")
        t_small = Trk()
        kb.dma("sp", qkw[:, :, :].rearrange("p a e -> p (a e)"),
               self.qk_w[l:l + 1].rearrange("o a e -> o (a e)").partition_broadcast(128), writes=[t_small])
        kb.dma("sp", dtb[:, :], self.dt_bias[l:l + 1, :].partition_broadcast(128), writes=[t_small])

        NB = 2
        xt = [kb.sb([128, D], F32, "xt") for _ in range(NB)]
        t_xt = [Trk() for _ in range(NB)]
        junk = kb.sb([128, D], BF16, "junk")
        t_junk = Trk()
        h32 = kb.sb([128, D], F32, "h32")
        hb = kb.sb([128, D], BF16, "hb")
        t_h = Trk()
        st = [kb.sb([128, 8], F32, "st") for _ in range(NB)]
        t_st = [Trk() for _ in range(NB)]
        hT = [kb.sb([128, 8, 512], BF16, "hT") for _ in range(2)]
        t_hT = [Trk(), Trk()]
        ptp = [kb.ps([128, 4, 128], BF16, "ptp") for _ in range(2)]
        t_ptp = [PTrk(), PTrk()]
        pmm = [kb.ps([128, 512], F32, "pmm") for _ in range(4)]
        t_pmm = [PTrk() for _ in range(4)]
        mmi = [0]
        NO = 2
        o_qa = [kb.sb([128, 512], BF16, "o_qa") for _ in range(NO)]
        o_ka = [kb.sb([128, 128], BF16, "o_ka") for _ in range(NO)]
        o_va = [kb.sb([128, 128], BF16, "o_va") for _ in range(NO)]
        o_ga = [kb.sb([128, 512], BF16, "o_ga") for _ in range(NO)]
        o_qb = [kb.sb([128, 768], BF16, "o_qb") for _ in range(NO)]
        o_kb = [kb.sb([128, 768], BF16, "o_kb") for _ in range(NO)]
        o_vb = [kb.sb([128, 768], BF16, "o_vb") for _ in range(NO)]
        o_gb = [kb.sb([128, 256], BF16, "o_gb") for _ in range(NO)]
        o_zc = [kb.sb([128, 512], BF16, "o_zc") for _ in range(NO)]
        o_dt = [kb.sb([128, 16], F32, "o_dt") for _ in range(NO)]
        o_xbc = [kb.sb([128, 512], BF16, "o_xbc") for _ in range(NO)]
        t_o = {n: [Trk() for _ in range(NO)] for n in ("qa", "ka", "va", "ga", "qb", "kb", "vb", "gb", "zc", "dt", "xbc")}
        sq = kb.sb([128, 768], F32, "sq")
        t_sq = Trk()
        ssh = kb.sb([128, 12], F32, "ssh")
        rsh = kb.sb([128, 12], F32, "rsh")
        t_ssh = Trk()
        qn = kb.sb([128, 768], F32, "qn")
        qr = kb.sb([128, 512], F32, "qr")
        t_qn = Trk()
        t_qr = Trk()
        rc = [kb.sb([128, 64], F32, "rc") for _ in range(NB)]
        rs = [kb.sb([128, 64], F32, "rs") for _ in range(NB)]
        t_rope = [Trk() for _ in range(NB)]
        dtt = kb.sb([128, 16], F32, "dtt")
        t_dtt = Trk()

        def next_pmm():
            i = mmi[0] % 4
            mmi[0] += 1
            return pmm[i], t_pmm[i]

        def mm_tok(hTt, t_hTt, j, c0, c1):
            p, tp = next_pmm()
            n = c1 - c0
            fns = [lambda e, k=k: e.matmul(p[:, 0:n], lhsT=hTt[:, k, j * 128:(j + 1) * 128], rhs=W[:, k, c0:c1],
                                           start=(k == 0), stop=(k == 7)) for k in range(8)]
            kb.mm_group(fns, reads=[t_hTt] + t_W, writes=[tp])
            return p, tp

        def qk_norm(p, tp, nh, widx, dst, t_dst, rope, g, ob):
            n = nh * 64
            kb.op("act", lambda e: e.activation(out=sq[:, 0:n], in_=p[:, 0:n], func=AF.Square), reads=[tp], writes=[t_sq])
            kb.op("dve", lambda e: e.tensor_reduce(out=ssh[:, 0:nh], in_=sq[:, 0:n].rearrange("p (h e) -> p h e", e=64),
                                                   axis=AX.X, op=ALU.add), reads=[t_sq], writes=[t_ssh])
            kb.op("act", lambda e: e.activation(out=ssh[:, 0:nh], in_=ssh[:, 0:nh], func=AF.Sqrt, scale=1.0 / 64, bias=self.eps_t[:, 0:1]),
                  reads=[t_ssh, self.t_const], writes=[t_ssh])
            kb.op("dve", lambda e: e.reciprocal(out=rsh[:, 0:nh], in_=ssh[:, 0:nh]), reads=[t_ssh], writes=[t_ssh])
            p3 = p[:, 0:n].rearrange("p (h e) -> p h e", e=64)
            q3 = qn[:, 0:n].rearrange("p (h e) -> p h e", e=64)
            kb.op("dve", lambda e: e.tensor_tensor(out=q3, in0=p3, in1=_bc(rsh[:, 0:nh].unsqueeze(2), [128, nh, 64]), op=ALU.mult),
                  reads=[tp, t_ssh], writes=[t_qn])
            wb = _bc(qkw[:, widx:widx + 1, :], [128, nh, 64])
            if not rope:
                d3 = dst[:, 0:n].rearrange("p (h e) -> p h e", e=64)
                kb.op("dve", lambda e: e.tensor_tensor(out=d3, in0=q3, in1=wb, op=ALU.mult),
                      reads=[t_qn, t_small], writes=[t_dst])
                return
            kb.op("dve", lambda e: e.tensor_tensor(out=q3, in0=q3, in1=wb, op=ALU.mult), reads=[t_qn, t_small], writes=[t_qn])
            r3 = qr[:, 0:n].rearrange("p (h e) -> p h e", e=64)
            kb.op("dve", lambda e: e.tensor_tensor(out=r3, in0=q3, in1=_bc(rc[g][:, :].unsqueeze(1), [128, nh, 64]), op=ALU.mult),
                  reads=[t_qn, t_rope[g]], writes=[t_qr])
            q5 = qn[:, 0:n].rearrange("p (h a b i) -> p h a b i", a=2, b=2, i=16)
            s5 = rs[g][:, :].rearrange("p (a b i) -> p a b i", a=2, b=2, i=16)
            sw = kb.sb_sw
            w5 = sw[:, 0:n].rearrange("p (h a b i) -> p h a b i", a=2, b=2, i=16)
            for bsel in range(2):
                kb.op("dve", lambda e, bsel=bsel: e.tensor_tensor(
                    out=w5[:, :, :, bsel, :], in0=q5[:, :, :, 1 - bsel, :],
                    in1=_bc(s5[:, :, bsel, :].unsqueeze(1), [128, nh, 2, 16]), op=ALU.mult),
                    reads=[t_qn, t_rope[g]], writes=[self.t_sw])
            kb.op("dve", lambda e: e.tensor_tensor(out=dst[:, 0:n], in0=qr[:, 0:n], in1=sw[:, 0:n], op=ALU.add),
                  reads=[t_qr, self.t_sw], writes=[t_dst])

        kb.sb_sw = kb.sb([128, 512], F32, "sw")
        self.t_sw = Trk()
        self.eps_t = kb.sb([128, 1], F32, "eps_t")
        kb.op("pool", lambda e: e.memset(self.eps_t[:, :], EPS), writes=[self.t_const])

        gi = 0
        for T in range(NT):
            hTt, t_hTt = hT[T % 2], t_hT[T % 2]
            for j in range(4):
                g = gi % NB
                gi += 1
                t0 = T * 512 + j * 128
                kb.dma("sp", xt[g][:, :], x_src[t0:t0 + 128, :], writes=[t_xt[g]])
                kb.dma("sp", rc[g][:, :], self.rope_c[t0:t0 + 128, :], writes=[t_rope[g]])
                kb.dma("sp", rs[g][:, :], self.rope_s[t0:t0 + 128, :], writes=[t_rope[g]])
                kb.op("act", lambda e: e.activation(out=junk[:, :], in_=xt[g][:, :], func=AF.Square, accum_out=st[g][:, 0:1]),
                      reads=[t_xt[g]], writes=[t_junk, t_st[g]])
                kb.op("act", lambda e: e.activation(out=st[g][:, 1:2], in_=st[g][:, 0:1], func=AF.Sqrt, scale=1.0 / D, bias=self.eps_t[:, 0:1]),
                      reads=[t_st[g], self.t_const], writes=[t_st[g]])
                kb.op("dve", lambda e: e.reciprocal(out=st[g][:, 2:3], in_=st[g][:, 1:2]), reads=[t_st[g]], writes=[t_st[g]])
                kb.op("dve", lambda e: e.scalar_tensor_tensor(out=h32[:, :], in0=xt[g][:, :], scalar=st[g][:, 2:3], in1=self.A_t[:, :],
                                                              op0=ALU.mult, op1=ALU.mult),
                      reads=[t_xt[g], t_st[g], self.t_mod], writes=[t_h])
                kb.op("dve", lambda e: e.tensor_tensor(out=hb[:, :], in0=h32[:, :], in1=self.Sh_t[:, :], op=ALU.add),
                      reads=[t_h, self.t_mod], writes=[t_h])
                for half in range(2):
                    pp, tpp = ptp[half], t_ptp[half]
                    fns = [lambda e, q=q: e.transpose(pp[:, q, :], hb[:, (half * 4 + q) * 128:(half * 4 + q + 1) * 128], self.ident_bf[:, :])
                           for q in range(4)]
                    kb.mm_group(fns, reads=[t_h, self.t_const], writes=[tpp])
                    kb.op("act", lambda e: e.activation(out=hTt[:, half * 4:half * 4 + 4, j * 128:(j + 1) * 128], in_=pp[:, :, :], func=AF.Copy),
                          reads=[tpp], writes=[t_hTt])
                ob = g % NO
                p, tp = mm_tok(hTt, t_hTt, j, 0, 512)
                qk_norm(p, tp, 8, 0, o_qa[ob], t_o["qa"][ob], True, g, ob)
                kb.dma("pool", self.s_qa[t0:t0 + 128, :], o_qa[ob][:, :], reads=[t_o["qa"][ob]])
                p, tp = mm_tok(hTt, t_hTt, j, 512, 768)
                qk_norm(p, tp, 2, 1, o_ka[ob], t_o["ka"][ob], True, g, ob)
                kb.dma("pool", self.s_ka[t0:t0 + 128, :], o_ka[ob][:, :], reads=[t_o["ka"][ob]])
                kb.op("act", lambda e: e.activation(out=o_va[ob][:, :], in_=p[:, 128:256], func=AF.Copy), reads=[tp], writes=[t_o["va"][ob]])
                kb.dma("pool", self.s_va[t0:t0 + 128, :], o_va[ob][:, :], reads=[t_o["va"][ob]])
                for (c0, c1, o0) in ((1280, 1792, 0), (1792, 2048, 512)):
                    p, tp = mm_tok(hTt, t_hTt, j, c0, c1)
                    nh = (c1 - c0) // 64
                    qk_norm(p, tp, nh, 2, o_qb[ob][:, o0:o0 + nh * 64], t_o["qb"][ob], False, g, ob)
                kb.dma("pool", self.s_qb[t0:t0 + 128, :], o_qb[ob][:, :], reads=[t_o["qb"][ob]])
                for (c0, c1, o0) in ((2048, 2560, 0), (2560, 2816, 512)):
                    p, tp = mm_tok(hTt, t_hTt, j, c0, c1)
                    nh = (c1 - c0) // 64
                    qk_norm(p, tp, nh, 3, o_kb[ob][:, o0:o0 + nh * 64], t_o["kb"][ob], False, g, ob)
                kb.dma("pool", self.s_kb[t0:t0 + 128, :], o_kb[ob][:, :], reads=[t_o["kb"][ob]])
                for (c0, c1, o0) in ((2816, 3328, 0), (3328, 3584, 512)):
                    p, tp = mm_tok(hTt, t_hTt, j, c0, c1)
                    n = c1 - c0
                    kb.op("act", lambda e: e.activation(out=o_vb[ob][:, o0:o0 + n], in_=p[:, 0:n], func=AF.Copy), reads=[tp], writes=[t_o["vb"][ob]])
                kb.dma("pool", self.s_vb[t0:t0 + 128, :], o_vb[ob][:, :], reads=[t_o["vb"][ob]])
                p, tp = mm_tok(hTt, t_hTt, j, 3584, 3840)
                kb.op("act", lambda e: e.activation(out=o_gb[ob][:, :], in_=p[:, 0:256], func=AF.Silu), reads=[tp], writes=[t_o["gb"][ob]])
                kb.dma("pool", self.s_gb[t0:t0 + 128, :], o_gb[ob][:, :], reads=[t_o["gb"][ob]])
                p, tp = mm_tok(hTt, t_hTt, j, 4352, 4864)
                kb.op("act", lambda e: e.activation(out=o_zc[ob][:, :], in_=p[:, 0:512], func=AF.Silu), reads=[tp], writes=[t_o["zc"][ob]])
                kb.dma("pool", self.s_zc[t0:t0 + 128, :], o_zc[ob][:, :], reads=[t_o["zc"][ob]])
                p, tp = mm_tok(hTt, t_hTt, j, 5376, 5392)
                kb.op("dve", lambda e: e.tensor_tensor(out=dtt[:, :], in0=p[:, 0:16], in1=dtb[:, :], op=ALU.add), reads=[tp, t_small], writes=[t_dtt])
                kb.op("act", lambda e: e.activation(out=dtt[:, :], in_=dtt[:, :], func=AF.Exp), reads=[t_dtt], writes=[t_dtt])
                kb.op("act", lambda e: e.activation(out=o_dt[ob][:, :], in_=dtt[:, :], func=AF.Ln, bias=self.ones_f[:, 0:1], scale=1.0),
                      reads=[t_dtt, self.t_const], writes=[t_o["dt"][ob]])
                kb.dma("pool", self.s_dt[t0:t0 + 128, :], o_dt[ob][:, :], reads=[t_o["dt"][ob]])
            for cb in range(12):
                if cb < 4:
                    c0 = 3840 + cb * 128
                elif cb < 8:
                    c0 = 4864 + (cb - 4) * 128
                else:
                    c0 = 768 + (cb - 8) * 128
                p, tp = next_pmm()
                fns = [lambda e, k=k: e.matmul(p[:, :], lhsT=W[:, k, c0:c0 + 128], rhs=hTt[:, k, :], start=(k == 0), stop=(k == 7))
                       for k in range(8)]
                kb.mm_group(fns, reads=[t_hTt] + t_W, writes=[tp])
                ob = cb % NO
                if cb < 8:
                    kb.op("act", lambda e: e.activation(out=o_xbc[ob][:, :], in_=p[:, :], func=AF.Copy), reads=[tp], writes=[t_o["xbc"][ob]])
                    kb.dma("pool", self.s_xbc[cb * 128:(cb + 1) * 128, T * 512:(T + 1) * 512], o_xbc[ob][:, :], reads=[t_o["xbc"][ob]])
                else:
                    kb.op("act", lambda e: e.activation(out=o_xbc[ob][:, :], in_=p[:, :], func=AF.Silu), reads=[tp], writes=[t_o["xbc"][ob]])
                    kb.dma("pool", self.s_gaT[(cb - 8) * 128:(cb - 7) * 128, T * 512:(T + 1) * 512], o_xbc[ob][:, :], reads=[t_o["xbc"][ob]])


    def phase2a(self, l):
        kb, nc = self.kb, self.nc
        S = self.S
        NKT = S // 128
        NQB = S // 512
        kT2 = [kb.sb([128, S], BF16, "kT2") for _ in range(2)]
        v1 = kb.sb([128, NKT, 2, 65], BF16, "v1")
        kall = kb.sb([128, NKT, 128], BF16, "kall")
        kd = [kb.sb([128, 2, 2, 64], BF16, "kd") for _ in range(2)]
        t_kT, t_v1, t_kall = Trk(), Trk(), Trk()
        t_kd = [Trk(), Trk()]
        ptp = [kb.ps([128, 4, 128], BF16, "ptp") for _ in range(1)]
        t_ptp = [PTrk()]
        kb.op("pool", lambda e: e.memset(v1[:, :, :, :], 1.0), writes=[t_v1])
        CH = 8
        for kv in range(2):
            src = self.s_va[:, kv * 64:(kv + 1) * 64].rearrange("(n p) e -> p n e", p=128)
            for n0 in range(0, NKT, CH):
                kb.dma("sp", v1[:, n0:n0 + CH, kv, 0:64], src[:, n0:n0 + CH, :], writes=[t_v1])
        srck = self.s_ka.rearrange("(n p) c -> p n c", p=128)
        for n0 in range(0, NKT, CH):
            kb.dma("sp", kall[:, n0:n0 + CH, :], srck[:, n0:n0 + CH, :], writes=[t_kall])
        for n in range(NKT):
            b = n % 2
            kb.op("dve", lambda e: e.tensor_copy(out=kd[b][:, :, :, :],
                                                 in_=_bc(kall[:, n, :].rearrange("p (k e) -> p k e", e=64).unsqueeze(2), [128, 2, 2, 64])),
                  reads=[t_kall], writes=[t_kd[b]])
            fns = [lambda e, kv=kv: e.transpose(ptp[0][:, kv, :], kd[b][:, kv, :, :].rearrange("p a e -> p (a e)"), self.ident_bf[:, :])
                   for kv in range(2)]
            kb.mm_group(fns, reads=[t_kd[b], self.t_const], writes=[t_ptp[0]])
            for kv in range(2):
                kb.op("act", lambda e, kv=kv: e.activation(out=kT2[kv][:, n * 128:(n + 1) * 128], in_=ptp[0][:, kv, :], func=AF.Copy),
                      reads=[t_ptp[0]], writes=[t_kT])
        qsb = [kb.sb([128, 4, 512], BF16, "qsb") for _ in range(2)]
        t_qsb = [Trk(), Trk()]
        qT = [kb.sb([128, 4, 512], BF16, "qT") for _ in range(2)]
        t_qT = [Trk(), Trk()]
        gT = [kb.sb([64, 8, 512], BF16, "gT") for _ in range(2)]
        t_gT = [Trk(), Trk()]
        NP = 2
        psT = [[kb.ps([128, 512], F32, "psT") for _ in range(2)] for _ in range(NP)]
        t_psT = [[PTrk(), PTrk()] for _ in range(NP)]
        pT = [kb.sb([128, 2, 512], BF16, "pT") for _ in range(3)]
        t_pT = [[Trk(), Trk()] for _ in range(3)]
        pacc = [kb.ps([128, 512], F32, "pacc") for _ in range(2)]
        t_acc = [PTrk(), PTrk()]
        pbc = kb.ps([128, 512], F32, "pbc")
        t_bc = PTrk()
        rec = kb.sb([128, 2, 512], F32, "rec")
        t_rec = [Trk(), Trk()]
        t1 = [kb.sb([64, 512], F32, "t1") for _ in range(2)]
        t_t1 = [Trk(), Trk()]
        yT = [kb.sb([64, 512], BF16, "yT") for _ in range(2)]
        t_yT = [Trk(), Trk()]

        def load_q(qb):
            g = qb % 2
            kb.dma("sp", qsb[g][:, :, :], self.s_qa[qb * 512:(qb + 1) * 512, :].rearrange("(j p) c -> p j c", p=128), writes=[t_qsb[g]])
            kb.dma("sp", gT[g][:, :, :], self.s_gaT[:, qb * 512:(qb + 1) * 512].rearrange("(h e) t -> e h t", e=64), writes=[t_gT[g]])

        def prep_q(qb):
            g = qb % 2
            for j in range(4):
                fns = [lambda e, hp=hp: e.transpose(ptp[0][:, hp, :], qsb[g][:, j, hp * 128:(hp + 1) * 128], self.ident_bf[:, :])
                       for hp in range(4)]
                kb.mm_group(fns, reads=[t_qsb[g], self.t_const], writes=[t_ptp[0]])
                kb.op("dve", lambda e: e.tensor_copy(out=qT[g][:, :, j * 128:(j + 1) * 128], in_=ptp[0][:, :, :]),
                      reads=[t_ptp[0]], writes=[t_qT[g]])

        seq = [(qb, hp, kt) for qb in range(NQB) for hp in range(4) for kt in range(NKT)]

        def qk(i):
            qb, hp, kt = seq[i]
            g = qb % 2
            kv = hp // 2
            b = i % NP
            for half in range(2):
                pr = slice(half * 64, half * 64 + 64)
                kb.op("pe", lambda e: e.matmul(psT[b][half][:, :], lhsT=kT2[kv][pr, kt * 128:(kt + 1) * 128], rhs=qT[g][pr, hp, :], start=True, stop=True),
                      reads=[t_kT, t_qT[g]], writes=[t_psT[b][half]])

        pending = []

        def epilogue_a(qb, hp):
            for half in range(2):
                kb.op("dve", lambda e: e.reciprocal(out=rec[64:65, half, :], in_=pacc[half][64:65, :]), reads=[t_acc[half]], writes=[t_rec[half]])
                kb.op("dve", lambda e: e.tensor_tensor(out=t1[half][:, :], in0=pacc[half][0:64, :], in1=gT[qb % 2][:, 2 * hp + half, :], op=ALU.mult),
                      reads=[t_acc[half], t_gT[qb % 2]], writes=[t_t1[half]])

        def epilogue_b(qb, hp):
            for half in range(2):
                h = 2 * hp + half
                kb.op("pe", lambda e: e.matmul(pbc[0:64, :], lhsT=self.ones_f[64:65, 0:64], rhs=rec[64:65, half, :], start=True, stop=True),
                      reads=[t_rec[half], self.t_const], writes=[t_bc])
                kb.op("dve", lambda e: e.tensor_tensor(out=yT[half][:, :], in0=t1[half][:, :], in1=pbc[0:64, :], op=ALU.mult),
                      reads=[t_t1[half], t_bc], writes=[t_yT[half]])
                kb.dma("pool", self.s_yaT[h * 64:(h + 1) * 64, qb * 512:(qb + 1) * 512], yT[half][:, :], reads=[t_yT[half]])

        load_q(0)
        if NQB > 1:
            load_q(1)
        prep_q(0)
        N = len(seq)
        qk(0)
        for i in range(N):
            qb, hp, kt = seq[i]
            if i + 1 < N:
                qb2, hp2, kt2 = seq[i + 1]
                if qb2 != qb and hp2 == 0 and kt2 == 0:
                    prep_q(qb2)
                qk(i + 1)
            b = i % NP
            c = i % 3
            kv = hp // 2
            for half in range(2):
                kb.op("act", lambda e: e.activation(out=pT[c][:, half, :], in_=psT[b][half][:, :], func=AF.Exp, scale=0.125),
                      reads=[t_psT[b][half]], writes=[t_pT[c][half]])
            for half in range(2):
                kb.op("pe", lambda e: e.matmul(pacc[half][0:65, :], lhsT=v1[:, kt, kv, :], rhs=pT[c][:, half, :], start=(kt == 0), stop=(kt == NKT - 1)),
                      reads=[t_v1, t_pT[c][half]], writes=[t_acc[half]])
            if kt == NKT - 1:
                epilogue_a(qb, hp)
                pending.append((i + 2, qb, hp))
                if hp == 3 and qb + 2 < NQB:
                    load_q(qb + 2)
            while pending and (pending[0][0] <= i or i == N - 1):
                _, pq, ph = pending.pop(0)
                epilogue_b(pq, ph)

    def phase2b(self, l):
        kb, nc = self.kb, self.nc
        S = self.S
        E = kb.sb([128, 36, 128], F32, "E2b")
        msk = kb.sb([128, 3, 128], F32, "msk2b")
        t_E, t_msk = Trk(), Trk()
        for c in range(4):
            kb.dma("sp", E[:, c * 9:(c + 1) * 9, :], self.bias_tab[:, c * 9:(c + 1) * 9, :], writes=[t_E])
        kb.dma("sp", msk[:, :, :], self.mask_tab[:, :, :], writes=[t_msk])
        kb.op("act", lambda e: e.activation(out=E[:, :, :], in_=E[:, :, :], func=AF.Exp), reads=[t_E], writes=[t_E])
        for i in range(12):
            kb.op("dve", lambda e: e.tensor_tensor(out=E[:, i * 3:(i + 1) * 3, :], in0=E[:, i * 3:(i + 1) * 3, :], in1=msk[:, :, :], op=ALU.mult),
                  reads=[t_E, t_msk], writes=[t_E])
        MMAX = S
        NTMAX = MMAX // 128
        qT = kb.sb([128, 2, MMAX], BF16, "qT2b")
        kT = kb.sb([128, 2, MMAX], BF16, "kT2b")
        v1 = kb.sb([128, NTMAX, 4, 65], BF16, "v12b")
        t_qT, t_kT = Trk(), Trk()
        CH = 4
        t_v1 = [[Trk() for _ in range(4)] for _ in range(NTMAX // CH + 1)]
        t_v1_all = [t for lst in t_v1 for t in lst]
        qch = [kb.sb([128, CH, 256], BF16, "qch") for _ in range(2)]
        kch = [kb.sb([128, CH, 256], BF16, "kch") for _ in range(2)]
        t_qch, t_kch = [Trk(), Trk()], [Trk(), Trk()]
        ptp = [kb.ps([128, 4, 128], BF16, "ptp2b") for _ in range(2)]
        t_ptp = [PTrk(), PTrk()]
        psT = [[kb.ps([128, 2, 128], F32, "psT2b") for _ in range(2)] for _ in range(2)]
        t_psT = [[PTrk(), PTrk()], [PTrk(), PTrk()]]
        pacc = [kb.ps([128, 4, 65], F32, "pacc2b") for _ in range(2)]
        t_acc = [PTrk(), PTrk()]
        pe32 = [kb.sb([128, 4, 128], F32, "pe32") for _ in range(2)]
        t_pe32 = [Trk(), Trk()]
        pT = [kb.sb([128, 4, 128], BF16, "pT2b") for _ in range(2)]
        t_pT = [Trk(), Trk()]
        osb = [kb.sb([128, 4, 65], F32, "osb2b") for _ in range(2)]
        t_osb = [Trk(), Trk()]
        kb.op("pool", lambda e: e.memset(v1[:, :, :, :], 1.0), writes=t_v1_all)
        ci = 0
        si = 0
        ti = 0
        for g, d in enumerate(B_DIL):
            M = S // d
            NT = M // 128
            cols = slice(g * 256, (g + 1) * 256)
            for r in range(d):
                qv = self.s_qb.rearrange("(m d) c -> d m c", d=d)[r][:, cols].rearrange("(n p) c -> p n c", p=128)
                kv_ = self.s_kb.rearrange("(m d) c -> d m c", d=d)[r][:, cols].rearrange("(n p) c -> p n c", p=128)
                vv = self.s_vb.rearrange("(m d) c -> d m c", d=d)[r][:, cols].rearrange("(n p) (h e) -> p n h e", p=128, e=64)
                nv = self.s_nb[g].rearrange("(m d) c -> d m c", d=d)[r].rearrange("(n p) c -> p n c", p=128)
                for n0 in range(0, NT, CH):
                    n1 = min(NT, n0 + CH)
                    for hs in range(4):
                        kb.dma("sp", v1[:, n0:n1, hs, 0:64], vv[:, n0:n1, hs, :], writes=[t_v1[n0 // CH][hs]])
                    b = ci % 2
                    ci += 1
                    kb.dma("sp", qch[b][:, 0:n1 - n0, :], qv[:, n0:n1, :], writes=[t_qch[b]])
                    kb.dma("sp", kch[b][:, 0:n1 - n0, :], kv_[:, n0:n1, :], writes=[t_kch[b]])
                    for n in range(n0, n1):
                        pb = n % 2
                        fns = [lambda e, pr=pr: e.transpose(ptp[pb][:, pr, :], qch[b][:, n - n0, pr * 128:(pr + 1) * 128], self.ident_bf[:, :])
                               for pr in range(2)]
                        fns += [lambda e, pr=pr: e.transpose(ptp[pb][:, 2 + pr, :], kch[b][:, n - n0, pr * 128:(pr + 1) * 128], self.ident_bf[:, :])
                                for pr in range(2)]
                        kb.mm_group(fns, reads=[t_qch[b], t_kch[b], self.t_const], writes=[t_ptp[pb]])
                        kb.op("act", lambda e: e.activation(out=qT[:, :, n * 128:(n + 1) * 128], in_=ptp[pb][:, 0:2, :], func=AF.Copy),
                              reads=[t_ptp[pb]], writes=[t_qT])
                        kb.op("dve", lambda e: e.tensor_copy(out=kT[:, :, n * 128:(n + 1) * 128], in_=ptp[pb][:, 2:4, :]),
                              reads=[t_ptp[pb]], writes=[t_kT])
                for mt in range(NT):
                    a = ti % 2
                    ti += 1
                    offs = [o for o in (-1, 0, 1) if 0 <= mt + o < NT]
                    for oi, o in enumerate(offs):
                        kt = mt + o
                        b = si % 2
                        si += 1
                        fns = []
                        for hs in range(4):
                            pr, half = hs // 2, hs % 2
                            ps_ = slice(half * 64, half * 64 + 64)
                            fns.append(lambda e, hs=hs, pr=pr, ps_=ps_, half=half: e.matmul(
                                psT[b][half][:, pr, :], lhsT=kT[ps_, pr, kt * 128:(kt + 1) * 128],
                                rhs=qT[ps_, pr, mt * 128:(mt + 1) * 128], start=True, stop=True))
                        kb.mm_group(fns, reads=[t_qT, t_kT], writes=[t_psT[b][0], t_psT[b][1]])
                        for half in range(2):
                            kb.op("act", lambda e, half=half: e.activation(out=pe32[b][:, half:4:2, :], in_=psT[b][half][:, :, :], func=AF.Exp, scale=0.125),
                                  reads=[t_psT[b][half]], writes=[t_pe32[b]])
                        kb.op("dve", lambda e: e.tensor_tensor(out=pT[b][:, :, :], in0=pe32[b][:, :, :],
                                                               in1=E[:, g * 12 + (o + 1):g * 12 + 12:3, :], op=ALU.mult),
                              reads=[t_pe32[b], t_E], writes=[t_pT[b]])
                        fns = []
                        for hs in range(4):
                            fns.append(lambda e, hs=hs: e.matmul(pacc[a][:, hs, :], lhsT=pT[b][:, hs, :], rhs=v1[:, kt, hs, :],
                                                                 start=(oi == 0 and hs == 0), stop=(oi == len(offs) - 1),
                                                                 skip_group_check=True))
                        kb.mm_group(fns, reads=[t_pT[b]] + t_v1[kt // CH], writes=[t_acc[a]])
                    kb.op("act", lambda e: e.activation(out=osb[a][:, :, :], in_=pacc[a][:, :, :], func=AF.Copy), reads=[t_acc[a]], writes=[t_osb[a]])
                    kb.dma("pool", nv[:, mt, :], osb[a][:, :, :].rearrange("p h e -> p (h e)"), reads=[t_osb[a]])
        kb.barrier()
        nb3 = [kb.sb([128, 3, 260], F32, "nb3") for _ in range(2)]
        gt = [kb.sb([128, 256], BF16, "gt2b") for _ in range(2)]
        t_nb3 = [Trk(), Trk()]
        sm = kb.sb([128, 4, 65], F32, "sm2b")
        rc = kb.sb([128, 4], F32, "rc2b")
        yb32 = kb.sb([128, 4, 64], F32, "yb32")
        ybb = kb.sb([128, 256], BF16, "ybb")
        t_sm = Trk()
        t_ybb = Trk()
        ybo = [kb.sb([128, 2, 128], BF16, "ybo") for _ in range(2)]
        t_ybo = [Trk(), Trk()]
        for T in range(S // 128):
            b = T % 2
            t0 = T * 128
            for g in range(3):
                kb.dma("sp", nb3[b][:, g, :], self.s_nb[g][t0:t0 + 128, :], writes=[t_nb3[b]])
            kb.dma("sp", gt[b][:, :], self.s_gb[t0:t0 + 128, :], writes=[t_nb3[b]])
            smf = sm[:, :, :].rearrange("p h e -> p (h e)")
            kb.op("dve", lambda e: e.tensor_tensor(out=smf, in0=nb3[b][:, 0, :], in1=nb3[b][:, 1, :], op=ALU.add), reads=[t_nb3[b]], writes=[t_sm])
            kb.op("dve", lambda e: e.tensor_tensor(out=smf, in0=smf, in1=nb3[b][:, 2, :], op=ALU.add), reads=[t_nb3[b], t_sm], writes=[t_sm])
            kb.op("dve", lambda e: e.reciprocal(out=rc[:, :], in_=sm[:, :, 64]), reads=[t_sm], writes=[t_sm])
            kb.op("dve", lambda e: e.tensor_tensor(out=yb32[:, :, :], in0=sm[:, :, 0:64], in1=_bc(rc[:, :].unsqueeze(2), [128, 4, 64]), op=ALU.mult),
                  reads=[t_sm], writes=[t_sm])
            kb.op("dve", lambda e: e.tensor_tensor(out=ybb[:, :], in0=yb32[:, :, :].rearrange("p h e -> p (h e)"), in1=gt[b][:, :], op=ALU.mult),
                  reads=[t_sm, t_nb3[b]], writes=[t_ybb])
            fns = [lambda e, pr=pr: e.transpose(ptp[0][:, pr, :], ybb[:, pr * 128:(pr + 1) * 128], self.ident_bf[:, :]) for pr in range(2)]
            kb.mm_group(fns, reads=[t_ybb, self.t_const], writes=[t_ptp[0]])
            kb.op("act", lambda e: e.activation(out=ybo[b][:, :, :], in_=ptp[0][:, 0:2, :], func=AF.Copy), reads=[t_ptp[0]], writes=[t_ybo[b]])
            kb.dma("pool", self.s_ybT[:, t0:t0 + 128].rearrange("(c p) t -> p c t", p=128), ybo[b][:, :, :], reads=[t_ybo[b]])

    def phase2c(self, l):
        kb, nc = self.kb, self.nc
        S = self.S
        NC = S // 128
        UT = [kb.sb([128, 128], F32, "UTf"), kb.sb([128, 128], F32, "UTb")]
        SM = [kb.sb([128, 128], F32, "Sf"), kb.sb([128, 128], F32, "Sb")]
        t_msk = Trk()
        for (tile_, pat, cm, cmp_) in ((UT[0], 1, -1, ALU.is_ge), (UT[1], -1, 1, ALU.is_ge), (SM[0], -1, 1, ALU.is_gt), (SM[1], 1, -1, ALU.is_gt)):
            kb.op("pool", lambda e: e.affine_select(out=tile_[:, :], in_=self.ones_f[:, :], pattern=[[pat, 128]], compare_op=cmp_, fill=0.0,
                                                    base=0, channel_multiplier=cm), reads=[self.t_const], writes=[t_msk])
        cw = kb.sb([128, 8, 5], F32, "cw")
        cb_ = kb.sb([128, 8], F32, "cb")
        Abc = kb.sb([128, 16], F32, "Abc")
        dsk = kb.sb([128, 8], F32, "dsk")
        nwc = kb.sb([128, 512], F32, "nwc")
        eps_t = kb.sb([128, 1], F32, "eps2c")
        t_par = Trk()
        kb.dma("sp", cw[:, :, :], self.conv_w[l], writes=[t_par])
        kb.dma("sp", cb_[:, :], self.conv_b[l], writes=[t_par])
        kb.dma("sp", Abc[:, :], self.a_log[l:l + 1, :].partition_broadcast(128), writes=[t_par])
        kb.dma("sp", dsk[:, :], self.d_skip[l:l + 1, :].partition_broadcast(128), writes=[t_par])
        kb.dma("sp", nwc[:, :], self.ssm_nw[l:l + 1, :].partition_broadcast(128), writes=[t_par])
        kb.op("act", lambda e: e.activation(out=Abc[:, :], in_=Abc[:, :], func=AF.Exp), reads=[t_par], writes=[t_par])
        kb.op("dve", lambda e: e.tensor_scalar(out=Abc[:, :], in0=Abc[:, :], scalar1=-1.0, scalar2=None, op0=ALU.mult), reads=[t_par], writes=[t_par])
        kb.op("dve", lambda e: e.memset(eps_t[:, :], EPS), writes=[t_par])
        TC = min(2048, S)
        with ExitStack() as es2:
            old_es = kb.es
            kb.es = es2
            xin = [kb.sb([128, S + 4], BF16, "xin") for _ in range(2)]
            t_xin = [Trk(), Trk()]
            cacc = kb.sb([128, TC], F32, "cacc")
            t_cacc = Trk()
            cout = [kb.sb([128, TC], BF16, "cout") for _ in range(2)]
            t_cout = [Trk(), Trk()]
            for b in range(2):
                kb.op("dve", lambda e: e.memset(xin[b][:, 0:2], 0.0), writes=[t_xin[b]])
                kb.op("dve", lambda e: e.memset(xin[b][:, S + 2:S + 4], 0.0), writes=[t_xin[b]])
            i = 0
            for cb in range(8):
                b = cb % 2
                for c0 in range(0, S, 2048):
                    c1 = min(S, c0 + 2048)
                    kb.dma("sp", xin[b][:, 2 + c0:2 + c1], self.s_xbc[cb * 128:(cb + 1) * 128, c0:c1], writes=[t_xin[b]])
                for c0 in range(0, S, TC):
                    o = i % 2
                    i += 1
                    kb.op("dve", lambda e: e.tensor_scalar(out=cacc[:, :], in0=xin[b][:, c0:c0 + TC], scalar1=cw[:, cb, 0:1], scalar2=None, op0=ALU.mult),
                          reads=[t_xin[b], t_par], writes=[t_cacc])
                    for j in range(1, 5):
                        kb.op("dve", lambda e: e.scalar_tensor_tensor(out=cacc[:, :], in0=xin[b][:, c0 + j:c0 + j + TC], scalar=cw[:, cb, j:j + 1], in1=cacc[:, :],
                                                                      op0=ALU.mult, op1=ALU.add),
                              reads=[t_xin[b], t_par, t_cacc], writes=[t_cacc])
                    kb.op("act", lambda e: e.activation(out=cout[o][:, :], in_=cacc[:, :], func=AF.Silu, bias=cb_[:, cb:cb + 1], scale=1.0),
                          reads=[t_cacc, t_par], writes=[t_cout[o]])
                    kb.dma("pool", self.s_xcv[cb * 128:(cb + 1) * 128, c0:c0 + TC], cout[o][:, :], reads=[t_cout[o]])
            kb.es = old_es
        kb.barrier()
        xT = [kb.sb([128, 4, 128], BF16, "xT2c") for _ in range(2)]
        BT = [kb.sb([128, 2, 128], BF16, "BT2c") for _ in range(2)]
        CT = [kb.sb([128, 2, 128], BF16, "CT2c") for _ in range(2)]
        dtt = [kb.sb([128, 16], F32, "dt2c") for _ in range(2)]
        zt = [kb.sb([128, 512], BF16, "z2c") for _ in range(2)]
        ybl = [kb.sb([128, 512], F32, "ybl2c") for _ in range(2)]
        t_ld = [[Trk() for _ in range(6)], [Trk() for _ in range(6)]]
        ptxb = kb.ps([128, 6, 128], BF16, "ptxb")
        t_ptxb = PTrk()
        pcum = kb.ps([128, 16], F32, "pcum")
        t_pcum = PTrk()
        pGT = kb.ps([128, 2, 128], F32, "pGT")
        t_pGT = PTrk()
        pseg = [kb.ps([128, 4, 128], F32, "pseg") for _ in range(2)]
        t_pseg = [PTrk(), PTrk()]
        pd = kb.ps([128, 8, 64], F32, "pd")
        po = kb.ps([128, 8, 64], F32, "po")
        pst = kb.ps([128, 8, 64], F32, "pst")
        t_pd, t_po, t_pst = PTrk(), PTrk(), PTrk()
        Btok = kb.sb([128, 2, 128], BF16, "Btok")
        t_Btok = Trk()
        xdt = kb.sb([128, 8, 64], BF16, "xdt")
        xdd = kb.sb([128, 8, 64], BF16, "xdd")
        t_xdt, t_xdd = Trk(), Trk()
        a_t = kb.sb([128, 8], F32, "a_t")
        t_a = Trk()
        ecs = kb.sb([128, 16], F32, "ecs")
        t_ecs = Trk()
        Amat = [kb.sb([128, 128], F32, "Amat") for _ in range(4)]
        t_Amat = [Trk() for _ in range(4)]
        LT = [kb.sb([128, 4, 128], F32, "LT") for _ in range(2)]
        t_LT = [Trk(), Trk()]
        GTm = kb.sb([128, 2, 128], F32, "GTm")
        t_GTm = Trk()
        MT = [kb.sb([128, 4, 128], BF16, "MT") for _ in range(2)]
        t_MT = [Trk(), Trk()]
        S32 = kb.sb([128, 8, 64], F32, "S32")
        Sst = kb.sb([128, 8, 64], BF16, "Sst")
        t_S32, t_Sst = Trk(), Trk()
        ytmp = kb.sb([128, 8, 64], F32, "ytmp")
        yacc = [kb.sb([128, 8, 64], F32, "yacc") for _ in range(2)]
        t_ytmp = Trk()
        t_yacc = [Trk(), Trk()]
        u32 = kb.sb([128, 512], F32, "u32")
        junk = kb.sb([128, 512], BF16, "junk2c")
        stt = kb.sb([128, 4], F32, "stt2c")
        t_u, t_junk, t_stt = Trk(), Trk(), Trk()
        ycb = kb.sb([128, 512], BF16, "ycb")
        t_ycb = Trk()
        yco = [kb.sb([128, 4, 128], BF16, "yco") for _ in range(2)]
        t_yco = [Trk(), Trk()]

        for dr in (1, 0):
            final = (dr == 0)
            order = list(range(NC)) if dr == 0 else list(range(NC - 1, -1, -1))
            kb.op("dve", lambda e: e.memset(S32[:, :, :], 0.0), writes=[t_S32])
            kb.op("dve", lambda e: e.memset(Sst[:, :, :], 0.0), writes=[t_Sst])
            dcol = 127 if dr == 0 else 0

            def load(ci):
                c = order[ci]
                b = ci % 2
                t0 = c * 128
                kb.dma("sp", xT[b][:, :, :], self.s_xcv[0:512, t0:t0 + 128].rearrange("(c p) t -> p c t", p=128), writes=[t_ld[b][0]])
                kb.dma("sp", BT[b][:, :, :], self.s_xcv[512:768, t0:t0 + 128].rearrange("(c p) t -> p c t", p=128), writes=[t_ld[b][1]])
                kb.dma("sp", CT[b][:, :, :], self.s_xcv[768:1024, t0:t0 + 128].rearrange("(c p) t -> p c t", p=128), writes=[t_ld[b][2]])
                kb.dma("sp", dtt[b][:, :], self.s_dt[t0:t0 + 128, :], writes=[t_ld[b][3]])
                if final:
                    kb.dma("sp", zt[b][:, :], self.s_zc[t0:t0 + 128, :], writes=[t_ld[b][4]])
                    kb.dma("sp", ybl[b][:, :], self.s_yb32[t0:t0 + 128, :], writes=[t_ld[b][5]])

            load(0)
            for ci in range(NC):
                c = order[ci]
                b = ci % 2
                t0 = c * 128
                if ci + 1 < NC:
                    load(ci + 1)
                fns = [lambda e, q=q: e.transpose(ptxb[:, q, :], xT[b][:, q, :], self.ident_bf[:, :]) for q in range(4)]
                fns += [lambda e, q=q: e.transpose(ptxb[:, 4 + q, :], BT[b][:, q, :], self.ident_bf[:, :]) for q in range(2)]
                kb.mm_group(fns, reads=[*t_ld[b], self.t_const], writes=[t_ptxb])
                xtok = ptxb[:, 0:4, :].rearrange("p c (h e) -> p (c h) e", e=64)
                kb.op("dve", lambda e: e.tensor_tensor(out=xdt[:, :, :], in0=xtok, in1=_bc(dtt[b][:, dr * 8:dr * 8 + 8].unsqueeze(2), [128, 8, 64]), op=ALU.mult),
                      reads=[t_ptxb, *t_ld[b]], writes=[t_xdt])
                kb.op("act", lambda e: e.activation(out=Btok[:, :, :], in_=ptxb[:, 4:6, :], func=AF.Copy), reads=[t_ptxb], writes=[t_Btok])
                kb.op("dve", lambda e: e.tensor_tensor(out=a_t[:, :], in0=dtt[b][:, dr * 8:dr * 8 + 8], in1=Abc[:, dr * 8:dr * 8 + 8], op=ALU.mult),
                      reads=[*t_ld[b], t_par], writes=[t_a])
                kb.mm_group([lambda e: e.matmul(pcum[:, 0:8], lhsT=UT[dr][:, :], rhs=a_t[:, :], start=True, stop=True),
                             lambda e: e.matmul(pcum[:, 8:16], lhsT=self.ones_f[:, :], rhs=a_t[:, :], start=True, stop=True)],
                            reads=[t_a, t_msk, self.t_const], writes=[t_pcum])
                kb.op("act", lambda e: e.activation(out=ecs[:, :], in_=pcum[:, :], func=AF.Exp), reads=[t_pcum], writes=[t_ecs])
                kb.mm_group([lambda e, gq=gq: e.matmul(pGT[:, gq, :], lhsT=BT[b][:, gq, :], rhs=CT[b][:, gq, :], start=True, stop=True) for gq in range(2)],
                            reads=[*t_ld[b]], writes=[t_pGT])
                kb.op("dve", lambda e: e.tensor_tensor(out=GTm[:, :, :], in0=pGT[:, :, :], in1=_bc(UT[dr][:, :].unsqueeze(1), [128, 2, 128]), op=ALU.mult),
                      reads=[t_pGT, t_msk], writes=[t_GTm])
                for hg in range(2):
                    for hh in range(4):
                        h = hg * 4 + hh
                        kb.op("dve", lambda e: e.tensor_scalar(out=Amat[hh][:, :], in0=SM[dr][:, :], scalar1=a_t[:, h:h + 1], scalar2=None, op0=ALU.mult),
                              reads=[t_a, t_msk], writes=[t_Amat[hh]])
                        kb.op("pe", lambda e: e.matmul(pseg[hg][:, hh, :], lhsT=Amat[hh][:, :], rhs=UT[dr][:, :], start=True, stop=True),
                              reads=[t_Amat[hh], t_msk], writes=[t_pseg[hg]])
                    kb.op("act", lambda e: e.activation(out=LT[hg][:, :, :], in_=pseg[hg][:, :, :], func=AF.Exp), reads=[t_pseg[hg]], writes=[t_LT[hg]])
                    kb.op("dve", lambda e: e.tensor_tensor(out=MT[hg][:, :, :], in0=LT[hg][:, :, :], in1=_bc(GTm[:, hg:hg + 1, :], [128, 4, 128]), op=ALU.mult),
                          reads=[t_LT[hg], t_GTm], writes=[t_MT[hg]])
                    kb.op("dve", lambda e: e.tensor_tensor(out=xdd[:, hg * 4:hg * 4 + 4, :], in0=xdt[:, hg * 4:hg * 4 + 4, :],
                                                           in1=_bc(LT[hg][:, :, dcol:dcol + 1], [128, 4, 64]), op=ALU.mult),
                          reads=[t_xdt, t_LT[hg]], writes=[t_xdd])
                kb.mm_group([lambda e, h=h: e.matmul(pd[:, h, :], lhsT=MT[h // 4][:, h % 4, :], rhs=xdt[:, h, :], start=True, stop=True) for h in range(8)],
                            reads=[t_MT[0], t_MT[1], t_xdt], writes=[t_pd])
                kb.mm_group([lambda e, h=h: e.matmul(po[:, h, :], lhsT=CT[b][:, h // 4, :], rhs=Sst[:, h, :], start=True, stop=True) for h in range(8)],
                            reads=[*t_ld[b], t_Sst], writes=[t_po])
                kb.mm_group([lambda e, h=h: e.matmul(pst[:, h, :], lhsT=Btok[:, h // 4, :], rhs=xdd[:, h, :], start=True, stop=True) for h in range(8)],
                            reads=[t_Btok, t_xdd], writes=[t_pst])
                ya = yacc[ci % 2]
                t_ya = t_yacc[ci % 2]
                kb.op("dve", lambda e: e.tensor_tensor(out=ytmp[:, :, :], in0=po[:, :, :], in1=_bc(ecs[:, 0:8].unsqueeze(2), [128, 8, 64]), op=ALU.mult),
                      reads=[t_po, t_ecs], writes=[t_ytmp])
                kb.op("dve", lambda e: e.tensor_tensor(out=ya[:, :, :], in0=ytmp[:, :, :], in1=pd[:, :, :], op=ALU.add),
                      reads=[t_pd, t_ytmp], writes=[t_ya])
                kb.op("dve", lambda e: e.tensor_tensor(out=S32[:, :, :], in0=S32[:, :, :], in1=_bc(ecs[:, 8:16].unsqueeze(2), [128, 8, 64]), op=ALU.mult),
                      reads=[t_ecs, t_S32], writes=[t_S32])
                kb.op("dve", lambda e: e.tensor_tensor(out=S32[:, :, :], in0=S32[:, :, :], in1=pst[:, :, :], op=ALU.add),
                      reads=[t_pst, t_S32], writes=[t_S32])
                kb.op("act", lambda e: e.activation(out=Sst[:, :, :], in_=S32[:, :, :], func=AF.Copy), reads=[t_S32], writes=[t_Sst])
                yaf = ya[:, :, :].rearrange("p h e -> p (h e)")
                if not final:
                    kb.dma("pool", self.s_yb32[t0:t0 + 128, :], yaf, reads=[t_ya])
                    continue
                kb.op("dve", lambda e: e.tensor_tensor(out=ytmp[:, :, :], in0=xtok, in1=_bc(dsk[:, :].unsqueeze(2), [128, 8, 64]), op=ALU.mult),
                      reads=[t_ptxb, t_par], writes=[t_ytmp])
                kb.op("dve", lambda e: e.tensor_tensor(out=yaf, in0=yaf, in1=ytmp[:, :, :].rearrange("p h e -> p (h e)"), op=ALU.add),
                      reads=[t_ytmp, t_ya], writes=[t_ya])
                kb.op("dve", lambda e: e.tensor_tensor(out=yaf, in0=yaf, in1=ybl[b][:, :], op=ALU.add), reads=[*t_ld[b], t_ya], writes=[t_ya])
                kb.op("dve", lambda e: e.tensor_tensor(out=u32[:, :], in0=yaf, in1=zt[b][:, :], op=ALU.mult), reads=[*t_ld[b], t_ya], writes=[t_u])
                kb.op("act", lambda e: e.activation(out=junk[:, :], in_=u32[:, :], func=AF.Square, accum_out=stt[:, 0:1]), reads=[t_u], writes=[t_junk, t_stt])
                kb.op("act", lambda e: e.activation(out=stt[:, 1:2], in_=stt[:, 0:1], func=AF.Sqrt, scale=1.0 / 512, bias=eps_t[:, 0:1]),
                      reads=[t_stt, t_par], writes=[t_stt])
                kb.op("dve", lambda e: e.reciprocal(out=stt[:, 2:3], in_=stt[:, 1:2]), reads=[t_stt], writes=[t_stt])
                kb.op("dve", lambda e: e.scalar_tensor_tensor(out=ycb[:, :], in0=u32[:, :], scalar=stt[:, 2:3], in1=nwc[:, :], op0=ALU.mult, op1=ALU.mult),
                      reads=[t_u, t_stt, t_par], writes=[t_ycb])
                fns = [lambda e, q=q: e.transpose(ptxb[:, q, :], ycb[:, q * 128:(q + 1) * 128], self.ident_bf[:, :]) for q in range(4)]
                kb.mm_group(fns, reads=[t_ycb, self.t_const], writes=[t_ptxb])
                kb.op("act", lambda e: e.activation(out=yco[b][:, :, :], in_=ptxb[:, 0:4, :], func=AF.Copy), reads=[t_ptxb], writes=[t_yco[b]])
                kb.dma("pool", self.s_ycT[:, t0:t0 + 128].rearrange("(c p) t -> p c t", p=128), yco[b][:, :, :], reads=[t_yco[b]])
            kb.barrier()

    def phase3(self, l, x_src):
        kb, nc = self.kb, self.nc
        S = self.S
        NT = S // 128
        br = [b for b in ("2a", "2b", "2c") if b in self.phases]
        Wg = kb.sb([128, 8, 3 * D], BF16, "Wg")
        Wo = kb.sb([128, 8, D], BF16, "Wo")
        Wa = kb.sb([128, 4, D], BF16, "Wa")
        Wb = kb.sb([128, 2, D], BF16, "Wb")
        Wc = kb.sb([128, 4, D], BF16, "Wc")
        bg = kb.sb([1, 3 * D], BF16, "bg")
        ones_b = kb.sb([1, 128], BF16, "ones_b")
        t_W = Trk()
        wv = self.w_in[l].rearrange("(k p) n -> p k n", p=128)
        for k in range(8):
            kb.dma("pool", Wg[:, k, :], wv[:, k, NCOL1:D_IN], writes=[t_W])
        kb.dma("pool", Wo[:, :, :], self.w_out[l].rearrange("(k p) n -> p k n", p=128), writes=[t_W])
        kb.dma("pool", Wa[:, :, :], self.w_pa[l].rearrange("(k p) n -> p k n", p=128), writes=[t_W])
        kb.dma("pool", Wb[:, :, :], self.w_pb[l].rearrange("(k p) n -> p k n", p=128), writes=[t_W])
        kb.dma("pool", Wc[:, :, :], self.w_pc[l].rearrange("(k p) n -> p k n", p=128), writes=[t_W])
        kb.dma("pool", bg[:, :], self.b_gate[l:l + 1, :], writes=[t_W])
        kb.op("dve", lambda e: e.memset(ones_b[:, :], 1.0), writes=[t_W])
        eps_t = kb.sb([128, 1], F32, "eps3")
        kb.op("dve", lambda e: e.memset(eps_t[:, :], EPS), writes=[t_W])

        NB = 2
        xt = [kb.sb([128, D], F32, "xt3") for _ in range(NB)]
        t_xt = [Trk() for _ in range(NB)]
        yaT = [kb.sb([128, 4, 128], BF16, "yaT3") for _ in range(NB)]
        ybT = [kb.sb([128, 2, 128], BF16, "ybT3") for _ in range(NB)]
        ycT = [kb.sb([128, 4, 128], BF16, "ycT3") for _ in range(NB)]
        t_y = [[Trk() for _ in range(3)] for _ in range(NB)]
        junk = kb.sb([128, D], BF16, "junk3")
        t_junk = Trk()
        st = kb.sb([128, 8], F32, "st3")
        t_st = Trk()
        h32 = kb.sb([128, D], F32, "h32_3")
        hb = kb.sb([128, D], BF16, "hb3")
        t_h = Trk()
        hT = kb.sb([128, 8, 128], BF16, "hT3")
        t_hT = Trk()
        ptp = [kb.ps([128, 4, 128], BF16, "ptp3") for _ in range(2)]
        t_ptp = [PTrk(), PTrk()]
        pg = [kb.ps([128, 512], F32, "pg3") for _ in range(2)]
        t_pg = [PTrk(), PTrk()]
        pp = [kb.ps([128, 512], F32, "pp3") for _ in range(2)]
        t_pp = [PTrk(), PTrk()]
        po = [kb.ps([128, 512], F32, "po3") for _ in range(2)]
        t_po = [PTrk(), PTrk()]
        sg = [kb.sb([128, 512], F32, "sg3") for _ in range(2)]
        t_sg = [Trk(), Trk()]
        merged = kb.sb([128, D], F32, "merged3")
        tmpm = kb.sb([128, 512], F32, "tmpm3")
        t_tmpm = Trk()
        t_merged = [Trk(), Trk()]
        mb = kb.sb([128, D], BF16, "mb3")
        t_mb = Trk()
        mT = kb.sb([128, 8, 128], BF16, "mT3")
        t_mT = Trk()
        xn = [kb.sb([128, D], F32, "xn3") for _ in range(2)]
        t_xn = [Trk(), Trk()]
        ci = 0
        for T in range(NT):
            g = T % NB
            t0 = T * 128
            kb.dma("sp", xt[g][:, :], x_src[t0:t0 + 128, :], writes=[t_xt[g]])
            if "2a" in br:
                kb.dma("sp", yaT[g][:, :, :], self.s_yaT[:, t0:t0 + 128].rearrange("(c p) t -> p c t", p=128), writes=[t_y[g][0]])
            if "2b" in br:
                kb.dma("sp", ybT[g][:, :, :], self.s_ybT[:, t0:t0 + 128].rearrange("(c p) t -> p c t", p=128), writes=[t_y[g][1]])
            if "2c" in br:
                kb.dma("sp", ycT[g][:, :, :], self.s_ycT[:, t0:t0 + 128].rearrange("(c p) t -> p c t", p=128), writes=[t_y[g][2]])
            kb.op("act", lambda e: e.activation(out=junk[:, :], in_=xt[g][:, :], func=AF.Square, accum_out=st[:, 0:1]),
                  reads=[t_xt[g]], writes=[t_junk, t_st])
            kb.op("act", lambda e: e.activation(out=st[:, 1:2], in_=st[:, 0:1], func=AF.Sqrt, scale=1.0 / D, bias=eps_t[:, 0:1]),
                  reads=[t_st, t_W], writes=[t_st])
            kb.op("dve", lambda e: e.reciprocal(out=st[:, 2:3], in_=st[:, 1:2]), reads=[t_st], writes=[t_st])
            kb.op("dve", lambda e: e.scalar_tensor_tensor(out=h32[:, :], in0=xt[g][:, :], scalar=st[:, 2:3], in1=self.A_t[:, :],
                                                          op0=ALU.mult, op1=ALU.mult),
                  reads=[t_xt[g], t_st, self.t_mod], writes=[t_h])
            kb.op("dve", lambda e: e.tensor_tensor(out=hb[:, :], in0=h32[:, :], in1=self.Sh_t[:, :], op=ALU.add),
                  reads=[t_h, self.t_mod], writes=[t_h])
            for half in range(2):
                pq, tpq = ptp[half], t_ptp[half]
                fns = [lambda e, q=q: e.transpose(pq[:, q, :], hb[:, (half * 4 + q) * 128:(half * 4 + q + 1) * 128], self.ident_bf[:, :])
                       for q in range(4)]
                kb.mm_group(fns, reads=[t_h, self.t_const], writes=[tpq])
                kb.op("act", lambda e: e.activation(out=hT[:, half * 4:half * 4 + 4, :], in_=pq[:, :, :], func=AF.Copy),
                      reads=[tpq], writes=[t_hT])
            for half in range(2):
                cs = slice(half * 512, (half + 1) * 512)
                first = True
                for (bn, gi, Wx, yx, nk) in (("2a", 0, Wa, yaT, 4), ("2b", 1, Wb, ybT, 2), ("2c", 2, Wc, ycT, 4)):
                    if bn not in br:
                        continue
                    b = ci % 2
                    ci += 1
                    fns = [lambda e, k=k: e.matmul(pg[b][:, :], lhsT=hT[:, k, :], rhs=Wg[:, k, gi * D + half * 512:gi * D + (half + 1) * 512],
                                                   start=(k == 0), stop=False) for k in range(8)]
                    fns.append(lambda e: e.matmul(pg[b][:, :], lhsT=ones_b[0:1, :], rhs=bg[0:1, gi * D + half * 512:gi * D + (half + 1) * 512],
                                                  start=False, stop=True))
                    kb.mm_group(fns, reads=[t_hT, t_W], writes=[t_pg[b]])
                    kb.op("act", lambda e: e.activation(out=sg[b][:, :], in_=pg[b][:, :], func=AF.Sigmoid), reads=[t_pg[b]], writes=[t_sg[b]])
                    fns = [lambda e, k=k: e.matmul(pp[b][:, :], lhsT=yx[g][:, k, :], rhs=Wx[:, k, cs], start=(k == 0), stop=(k == nk - 1))
                           for k in range(nk)]
                    kb.mm_group(fns, reads=[*t_y[g], t_W], writes=[t_pp[b]])
                    if first:
                        kb.op("dve", lambda e: e.tensor_tensor(out=merged[:, cs], in0=pp[b][:, :], in1=sg[b][:, :], op=ALU.mult),
                              reads=[t_pp[b], t_sg[b]], writes=[t_merged[half]])
                        first = False
                    else:
                        kb.op("dve", lambda e: e.tensor_tensor(out=tmpm[:, :], in0=pp[b][:, :], in1=sg[b][:, :], op=ALU.mult),
                              reads=[t_pp[b], t_sg[b]], writes=[t_tmpm])
                        kb.op("dve", lambda e: e.tensor_tensor(out=merged[:, cs], in0=merged[:, cs], in1=tmpm[:, :], op=ALU.add),
                              reads=[t_tmpm, t_merged[half]], writes=[t_merged[half]])
                kb.op("act", lambda e: e.activation(out=mb[:, cs], in_=merged[:, cs], func=AF.Copy), reads=[t_merged[half]], writes=[t_mb])
                pq, tpq = ptp[half], t_ptp[half]
                fns = [lambda e, q=q: e.transpose(pq[:, q, :], mb[:, (half * 4 + q) * 128:(half * 4 + q + 1) * 128], self.ident_bf[:, :])
                       for q in range(4)]
                kb.mm_group(fns, reads=[t_mb, self.t_const], writes=[tpq])
                kb.op("act", lambda e: e.activation(out=mT[:, half * 4:half * 4 + 4, :], in_=pq[:, :, :], func=AF.Copy),
                      reads=[tpq], writes=[t_mT])
            o = T % 2
            for half in range(2):
                cs = slice(half * 512, (half + 1) * 512)
                fns = [lambda e, k=k: e.matmul(po[half][:, :], lhsT=mT[:, k, :], rhs=Wo[:, k, cs], start=(k == 0), stop=(k == 7)) for k in range(8)]
                kb.mm_group(fns, reads=[t_mT, t_W], writes=[t_po[half]])
                kb.op("dve", lambda e: e.tensor_tensor(out=xn[o][:, cs], in0=po[half][:, :], in1=self.G_t[:, cs], op=ALU.mult),
                      reads=[t_po[half], self.t_mod], writes=[t_xn[o]])
            kb.op("dve", lambda e: e.tensor_tensor(out=xn[o][:, :], in0=xn[o][:, :], in1=xt[g][:, :], op=ALU.add),
                  reads=[t_xt[g], t_xn[o]], writes=[t_xn[o]])
            kb.dma("pool", self.out[t0:t0 + 128, :], xn[o][:, :], reads=[t_xn[o]])

def rope_tables(S):
    t = np.arange(S)
    row = (t // GRID_W).astype(np.float32)
    col = (t % GRID_W).astype(np.float32)
    quarter = 16
    freqs = (10000.0 ** (-np.arange(quarter, dtype=np.float32) / quarter)).astype(np.float32)
    C = np.zeros((S, 64), np.float32)
    Sg = np.zeros((S, 64), np.float32)
    for hi, pos in enumerate((row, col)):
        ang = (pos[:, None] * freqs[None, :]).astype(np.float32)
        c, s = np.cos(ang), np.sin(ang)
        C[:, hi * 32:hi * 32 + 16] = c
        C[:, hi * 32 + 16:hi * 32 + 32] = c
        Sg[:, hi * 32:hi * 32 + 16] = -s
        Sg[:, hi * 32 + 16:hi * 32 + 32] = s
    return C, Sg


def t5_bucket_np(rel):
    nb = 16
    max_exact = 8
    ret = np.where(rel > 0, nb, 0)
    n = np.abs(rel)
    nf = np.maximum(n, 1).astype(np.float32)
    large = max_exact + (np.log(nf / np.float32(max_exact)) / np.float32(math.log(1024 / max_exact))
                         * np.float32(nb - max_exact)).astype(np.int32)
    large = np.minimum(large, nb - 1)
    return ret + np.where(n < max_exact, n, large)


def b_tables(rel_bias):
    rel_bias = np.asarray(rel_bias, np.float32)
    k = np.arange(128)[:, None]
    q = np.arange(128)[None, :]
    bias = np.zeros((3, 4, 3, 128, 128), np.float32)
    mask = np.zeros((3, 128, 128), np.float32)
    for oi, o in enumerate((-1, 0, 1)):
        relp = 128 * o + k - q
        mask[oi] = (np.abs(relp) <= 64).astype(np.float32)
        for g, d in enumerate(B_DIL):
            bk = t5_bucket_np(np.clip(relp, -64, 64) * d)
            for hs in range(4):
                bias[g, hs, oi] = rel_bias[bk, g * 4 + hs]
    bias_t = np.ascontiguousarray(bias.reshape(36, 128, 128).transpose(1, 0, 2))
    mask_t = np.ascontiguousarray(mask.transpose(1, 0, 2))
    return bias_t, mask_t


def make_inputs(inp, b, S, L):
    f = lambda a: np.ascontiguousarray(np.asarray(a), dtype=np.float32)
    C, Sg = rope_tables(S)
    BT, MT = b_tables(inp["rel_bias"])
    m = {
        "x": f(inp["x"][b][:S]),
        "c": f(np.asarray(inp["c"][b]).reshape(8, 128).T),
        "norm_w": f(inp["norm_w"][:L]),
        "w_ada": f(inp["w_ada"][:L]),
        "b_ada": f(inp["b_ada"][:L]),
        "w_in": f(inp["w_in"][:L]),
        "b_gate": f(inp["b_gate"][:L]),
        "qk_w": f(np.stack([inp["q_norm_a"][:L], inp["k_norm_a"][:L], inp["q_norm_b"][:L], inp["k_norm_b"][:L]], axis=1)),
        "rope_c": C, "rope_s": Sg,
        "dt_bias": f(np.asarray(inp["dt_bias"][:L]).reshape(L, 16)),
        "bias_tab": BT, "mask_tab": MT,
        "conv_w": f(np.asarray(inp["conv_w"][:L]).reshape(L, 5, 8, 128).transpose(0, 3, 2, 1)),
        "conv_b": f(np.asarray(inp["conv_b"][:L]).reshape(L, 8, 128).transpose(0, 2, 1)),
        "a_log": f(np.asarray(inp["a_log"][:L]).reshape(L, 16)),
        "d_skip": f(inp["d_skip"][:L]), "ssm_norm_w": f(inp["ssm_norm_w"][:L]),
        "w_out": f(inp["w_out"][:L]), "w_proj_a": f(inp["w_proj_a"][:L]),
        "w_proj_b": f(inp["w_proj_b"][:L]), "w_proj_c": f(inp["w_proj_c"][:L]),
    }
    return m


_PROG = {}


def kernel(**inputs):
    S, L = 8192, DEPTH
    key = (S, L)
    if key not in _PROG:
        _PROG[key] = Prog(S, L)
    prog = _PROG[key]
    in_maps = [make_inputs(inputs, b, S, L) for b in range(4)]
    res = run_bass_kernel_spmd(prog.nc, in_maps, core_ids=list(range(4)))
    return np.stack([np.asarray(r["out"]) for r in res.results], axis=0).astype(np.float32)
```

```python
from contextlib import ExitStack
import math
import numpy as np
import ml_dtypes
import concourse.bass as bass
import concourse.mybir as mybir
from concourse.bass_utils import run_bass_kernel_spmd

F32 = mybir.dt.float32
BF16 = mybir.dt.bfloat16
ALU = mybir.AluOpType
AF = mybir.ActivationFunctionType
AX = mybir.AxisListType

D = 1024
DEPTH = 2
EPS = 1e-6
NCOL1 = 5392
D_IN = 8464
GRID_W = 64
B_DIL = (1, 4, 16)


class Trk:
    __slots__ = ("w", "r")

    def __init__(self):
        self.w = []
        self.r = []


class PTrk(Trk):
    __slots__ = ()


class KB:
    def __init__(self, nc, es):
        self.nc = nc
        self.es = es
        self.eng = {"pe": nc.tensor, "act": nc.scalar, "dve": nc.vector, "pool": nc.gpsimd, "sp": nc.sync}
        self.semh = {}
        self.cnt = {}
        self.seen = {e: {} for e in self.eng}
        for e in ("pe", "act", "dve", "pool"):
            self.semh[e] = es.enter_context(nc.semaphore("s_" + e))
            self.cnt[e] = 0
        self.dq = {}
        for q, n in (("sp", 12), ("pool", 8), ("act", 4)):
            names = []
            for i in range(n):
                nm = "d_%s%d" % (q, i)
                self.semh[nm] = es.enter_context(nc.semaphore(nm))
                self.cnt[nm] = 0
                names.append(nm)
            self.dq[q] = [names, 0]
        self.uid = 0

    def sb(self, shape, dtype, name=None):
        self.uid += 1
        return self.es.enter_context(self.nc.sbuf_tensor("%s_%d" % (name or "t", self.uid), list(shape), dtype))

    def ps(self, shape, dtype, name=None):
        self.uid += 1
        esz = 4 if dtype == F32 else 2
        full = self.es.enter_context(self.nc.psum_tensor("%s_%d" % (name or "p", self.uid), [128, 2048 // esz], dtype))
        n = 1
        for d_ in shape[1:]:
            n *= d_
        assert n * esz <= 2048 and shape[0] == 128
        v = full[:, 0:n]
        if len(shape) == 3:
            v = v.rearrange("p (a b) -> p a b", b=shape[2])
        return v

    def ps2(self, dtype, name=None):
        self.uid += 1
        full = self.es.enter_context(self.nc.psum_tensor("%s_%d" % (name or "p2", self.uid), [128, 1024], dtype))
        return full[:, :].rearrange("p (a b) -> p a b", b=512)

    def _wait(self, e, tickets):
        need = {}
        for (s, v) in tickets:
            if v > need.get(s, 0):
                need[s] = v
        seen = self.seen[e]
        for s, v in need.items():
            if seen.get(s, 0) >= v:
                continue
            if s == e and e == "pe":
                continue
            self.eng[e].wait_ge(self.semh[s], v)
            seen[s] = v

    @staticmethod
    def _addr(lst, tk):
        for i, (s, v) in enumerate(lst):
            if s == tk[0]:
                if tk[1] > v:
                    lst[i] = tk
                return
        lst.append(tk)

    def _deps(self, reads, writes):
        tickets = []
        for t in reads:
            tickets += t.w
            if isinstance(t, PTrk):
                tickets += t.r
        for t in writes:
            tickets += t.w
            tickets += t.r
        return tickets

    def _mark(self, tk, reads, writes):
        for t in reads:
            if isinstance(t, PTrk):
                t.r = [tk]
            else:
                self._addr(t.r, tk)
        for t in writes:
            t.w = [tk]
            t.r = []

    def op(self, e, fn, reads=(), writes=()):
        self._wait(e, self._deps(reads, writes))
        ins = fn(self.eng[e])
        self.cnt[e] += 1
        ins.then_inc(self.semh[e], 1)
        tk = (e, self.cnt[e])
        self._mark(tk, reads, writes)
        return tk

    def mm_group(self, fns, reads=(), writes=()):
        self._wait("pe", self._deps(reads, writes))
        ins = None
        for fn in fns:
            ins = fn(self.eng["pe"])
        self.cnt["pe"] += 1
        ins.then_inc(self.semh["pe"], 1)
        tk = ("pe", self.cnt["pe"])
        self._mark(tk, reads, writes)
        return tk

    def dma(self, q, out, in_, reads=(), writes=(), **kw):
        names, i = self.dq[q]
        nm = names[i % len(names)]
        self.dq[q][1] = i + 1
        tickets = self._deps(reads, writes)
        tickets.append((nm, self.cnt[nm]))
        self._wait(q, tickets)
        self.eng[q].dma_start(out=out, in_=in_, **kw).then_inc(self.semh[nm], 16)
        self.cnt[nm] += 16
        tk = (nm, self.cnt[nm])
        self._mark(tk, reads, writes)
        return tk

    def barrier(self):
        allt = [(s, c) for s, c in self.cnt.items() if c > 0]
        for e in self.eng:
            self._wait(e, allt)


def _bc(ap, shape):
    return ap.to_broadcast(list(shape))


class Prog:
    def __init__(self, S=8192, layers=2, phases=("0", "1", "2a", "2b", "2c", "3"), debug=False, ext_scratch=True):
        self.ext_scratch = ext_scratch
        self.S = S
        self.L = layers
        self.phases = phases
        self.debug = debug
        self.nc = bass.Bass("TRN2", target_bir_lowering=False)
        self.build()

    def build(self):
        nc = self.nc
        S = self.S
        L = self.L
        dt = nc.dram_tensor

        def din(name, shape, dtype=F32):
            return dt(name, list(shape), dtype, kind="ExternalInput").ap()

        def dscr(name, shape, dtype):
            return dt(name, list(shape), dtype, kind="ExternalOutput" if (self.debug or self.ext_scratch) else "Internal").ap()

        self.x_in = din("x", [S, D])
        self.c_in = din("c", [128, 8])
        self.norm_w = din("norm_w", [L, D])
        self.w_ada = din("w_ada", [L, D, 3 * D])
        self.b_ada = din("b_ada", [L, 3 * D])
        self.w_in = din("w_in", [L, D, D_IN])
        self.b_gate = din("b_gate", [L, 3 * D])
        self.qk_w = din("qk_w", [L, 4, 64])
        self.rope_c = din("rope_c", [S, 64])
        self.rope_s = din("rope_s", [S, 64])
        self.dt_bias = din("dt_bias", [L, 16])
        self.w_out = din("w_out", [L, D, D])
        self.w_pa = din("w_proj_a", [L, 512, D])
        self.w_pb = din("w_proj_b", [L, 256, D])
        self.w_pc = din("w_proj_c", [L, 512, D])
        self.out = dt("out", [S, D], F32, kind="ExternalOutput").ap()
        self.s_qa = dscr("s_qa", [S, 512], BF16)
        self.s_ka = dscr("s_ka", [S, 128], BF16)
        self.s_va = dscr("s_va", [S, 128], BF16)
        self.s_gaT = dscr("s_gaT", [512, S], BF16)
        self.s_yaT = dscr("s_yaT", [512, S], BF16)
        self.s_qb = dscr("s_qb", [S, 768], BF16)
        self.s_kb = dscr("s_kb", [S, 768], BF16)
        self.s_vb = dscr("s_vb", [S, 768], BF16)
        self.s_gb = dscr("s_gb", [S, 256], BF16)
        self.s_zc = dscr("s_zc", [S, 512], BF16)
        self.s_xbc = dscr("s_xbc", [1024, S], BF16)
        self.s_dt = dscr("s_dt", [S, 16], F32)
        self.s_ybT = dscr("s_ybT", [256, S], BF16)
        self.s_nb = [dscr("s_nb%d" % g, [S, 260], F32) for g in range(3)]
        self.bias_tab = din("bias_tab", [128, 36, 128])
        self.mask_tab = din("mask_tab", [128, 3, 128])
        self.s_ycT = dscr("s_ycT", [512, S], BF16)
        self.s_xcv = dscr("s_xcv", [1024, S], BF16)
        self.s_yb32 = dscr("s_yb32", [S, 512], F32)
        self.conv_w = din("conv_w", [L, 128, 8, 5])
        self.conv_b = din("conv_b", [L, 128, 8])
        self.a_log = din("a_log", [L, 16])
        self.d_skip = din("d_skip", [L, 8])
        self.ssm_nw = din("ssm_norm_w", [L, 512])
        if self.debug:
            self.dbg_mod = dt("dbg_mod", [128, 3 * D], F32, kind="ExternalOutput").ap()

        with ExitStack() as es_top:
            kb = KB(nc, es_top)
            self.kb = kb
            self.ident_bf = kb.sb([128, 128], BF16, "identb")
            self.ident_f = kb.sb([128, 128], F32, "identf")
            self.ones_f = kb.sb([128, 128], F32, "onesf")
            self.t_const = Trk()
            kb.op("pool", lambda e: e.memset(self.ones_f[:, :], 1.0), writes=[self.t_const])
            kb.op("pool", lambda e: e.memset(self.ident_f[:, :], 0.0), writes=[self.t_const])
            kb.op("pool", lambda e: e.affine_select(out=self.ident_f[:, :], in_=self.ident_f[:, :],
                                                    pattern=[[-1, 128]], compare_op=ALU.not_equal, fill=1.0,
                                                    base=0, channel_multiplier=1),
                  reads=[self.t_const], writes=[self.t_const])
            kb.op("dve", lambda e: e.tensor_copy(out=self.ident_bf[:, :], in_=self.ident_f[:, :]),
                  reads=[self.t_const], writes=[self.t_const])
            self.A_t = kb.sb([128, D], F32, "A_t")
            self.Sh_t = kb.sb([128, D], F32, "Sh_t")
            self.G_t = kb.sb([128, D], F32, "G_t")
            self.t_mod = Trk()

            for l in range(L):
                x_src = self.x_in if l == 0 else self.out
                if "0" in self.phases:
                    with ExitStack() as es:
                        kb.es = es
                        self.phase0(l)
                        kb.barrier()
                if "1" in self.phases:
                    with ExitStack() as es:
                        kb.es = es
                        self.phase1(l, x_src)
                        kb.barrier()
                for ph, fn in (("2a", self.phase2a), ("2b", self.phase2b), ("2c", self.phase2c)):
                    if ph in self.phases:
                        with ExitStack() as es:
                            kb.es = es
                            fn(l)
                            kb.barrier()
                if "3" in self.phases:
                    with ExitStack() as es:
                        kb.es = es
                        self.phase3(l, x_src)
                        kb.barrier()
            kb.es = es_top
            kb.barrier()

    def phase0(self, l):
        kb, nc = self.kb, self.nc
        c_sb = kb.sb([128, 8], F32, "c_sb")
        c_act = kb.sb([128, 8], F32, "c_act")
        cl = kb.sb([128, 8, 128], F32, "cl")
        wa = kb.sb([128, 8, 3 * D], F32, "wa")
        bb = kb.sb([128, 3 * D], F32, "bb")
        nw = kb.sb([128, D], F32, "nw")
        mod = kb.sb([128, 3 * D], F32, "mod")
        t_c, t_wa, t_bb, t_cl, t_mod = Trk(), [Trk() for _ in range(8)], Trk(), Trk(), Trk()
        kb.dma("sp", c_sb[:, :], self.c_in[:, :], writes=[t_c])
        wv = self.w_ada[l].rearrange("(k p) n -> p k n", p=128)
        for k in range(8):
            kb.dma("sp", wa[:, k, :], wv[:, k, :], writes=[t_wa[k]])
        kb.dma("sp", bb[:, :], self.b_ada[l:l + 1, :].partition_broadcast(128), writes=[t_bb])
        kb.dma("sp", nw[:, :], self.norm_w[l:l + 1, :].partition_broadcast(128), writes=[t_bb])
        kb.op("act", lambda e: e.activation(out=c_act[:, :], in_=c_sb[:, :], func=AF.Silu), reads=[t_c], writes=[t_c])
        for k in range(8):
            kb.op("dve", lambda e: e.tensor_scalar(out=cl[:, k, :], in0=self.ones_f[:, :], scalar1=c_act[:, k:k + 1],
                                                   scalar2=None, op0=ALU.mult),
                  reads=[t_c, self.t_const], writes=[t_cl])
        pm = [kb.ps([128, 512], F32, "pm") for _ in range(2)]
        t_pm = [PTrk(), PTrk()]
        for n in range(6):
            b = n % 2
            fns = []
            for k in range(8):
                fns.append(lambda e, k=k: e.matmul(pm[b][:, :], lhsT=cl[:, k, :], rhs=wa[:, k, n * 512:(n + 1) * 512],
                                                   start=(k == 0), stop=(k == 7)))
            kb.mm_group(fns, reads=[t_cl] + t_wa, writes=[t_pm[b]])
            kb.op("dve", lambda e: e.tensor_tensor(out=mod[:, n * 512:(n + 1) * 512], in0=pm[b][:, :],
                                                   in1=bb[:, n * 512:(n + 1) * 512], op=ALU.add),
                  reads=[t_pm[b], t_bb], writes=[t_mod])
        kb.op("dve", lambda e: e.tensor_copy(out=self.Sh_t[:, :], in_=mod[:, 0:D]), reads=[t_mod], writes=[self.t_mod])
        kb.op("dve", lambda e: e.scalar_tensor_tensor(out=self.A_t[:, :], in0=mod[:, D:2 * D], scalar=1.0, in1=nw[:, :],
                                                      op0=ALU.add, op1=ALU.mult),
              reads=[t_mod, t_bb], writes=[self.t_mod])
        kb.op("dve", lambda e: e.tensor_copy(out=self.G_t[:, :], in_=mod[:, 2 * D:3 * D]), reads=[t_mod], writes=[self.t_mod])
        if self.debug and l == 0:
            kb.dma("pool", self.dbg_mod[:, :], mod[:, :], reads=[t_mod])

    def phase1(self, l, x_src):
        kb, nc = self.kb, self.nc
        S = self.S
        NT = S // 512
        W = kb.sb([128, 8, NCOL1], BF16, "W1")
        t_W = [Trk() for _ in range(8)]
        wv = self.w_in[l].rearrange("(k p) n -> p k n", p=128)
        for k in range(8):
            kb.dma("pool", W[:, k, :], wv[:, k, 0:NCOL1], writes=[t_W[k]])
        qkw = kb.sb([128, 4, 64], F32, "qkw")
        dtb = kb.sb([128, 16], F32, "dtb")
        t_small = Trk()
        kb.dma("sp", qkw[:, :, :].rearrange("p a e -> p (a e)"),
               self.qk_w[l:l + 1].rearrange("o a e -> o (a e)").partition_broadcast(128), writes=[t_small])
        kb.dma("sp", dtb[:, :], self.dt_bias[l:l + 1, :].partition_broadcast(128), writes=[t_small])

        NB = 2
        xt = [kb.sb([128, D], F32, "xt") for _ in range(NB)]
        t_xt = [Trk() for _ in range(NB)]
        junk = kb.sb([128, D], BF16, "junk")
        t_junk = Trk()
        h32 = kb.sb([128, D], F32, "h32")
        hb = kb.sb([128, D], BF16, "hb")
        t_h = Trk()
        st = [kb.sb([128, 8], F32, "st") for _ in range(NB)]
        t_st = [Trk() for _ in range(NB)]
        hT = [kb.sb([128, 8, 512], BF16, "hT") for _ in range(2)]
        t_hT = [Trk(), Trk()]
        ptp = [kb.ps([128, 4, 128], BF16, "ptp") for _ in range(2)]
        t_ptp = [PTrk(), PTrk()]
        pmm = [kb.ps([128, 512], F32, "pmm") for _ in range(6)]
        t_pmm = [PTrk() for _ in range(6)]
        mmi = [0]
        NO = 2
        o_qa = [kb.sb([128, 512], BF16, "o_qa") for _ in range(NO)]
        o_ka = [kb.sb([128, 128], BF16, "o_ka") for _ in range(NO)]
        o_va = [kb.sb([128, 128], BF16, "o_va") for _ in range(NO)]
        o_ga = [kb.sb([128, 512], BF16, "o_ga") for _ in range(NO)]
        o_qb = [kb.sb([128, 768], BF16, "o_qb") for _ in range(NO)]
        o_kb = [kb.sb([128, 768], BF16, "o_kb") for _ in range(NO)]
        o_vb = [kb.sb([128, 768], BF16, "o_vb") for _ in range(NO)]
        o_gb = [kb.sb([128, 256], BF16, "o_gb") for _ in range(NO)]
        o_zc = [kb.sb([128, 512], BF16, "o_zc") for _ in range(NO)]
        o_dt = [kb.sb([128, 16], F32, "o_dt") for _ in range(NO)]
        o_xbc = [kb.sb([128, 512], BF16, "o_xbc") for _ in range(NO)]
        t_o = {n: [Trk() for _ in range(NO)] for n in ("qa", "ka", "va", "ga", "qb", "kb", "vb", "gb", "zc", "dt", "xbc")}
        sq = kb.sb([128, 768], F32, "sq")
        t_sq = Trk()
        ssh = kb.sb([128, 12], F32, "ssh")
        rsh = kb.sb([128, 12], F32, "rsh")
        t_ssh = Trk()
        qn = kb.sb([128, 768], F32, "qn")
        qr = kb.sb([128, 512], F32, "qr")
        t_qn = Trk()
        t_qr = Trk()
        rc = [kb.sb([128, 64], F32, "rc") for _ in range(NB)]
        rs = [kb.sb([128, 64], F32, "rs") for _ in range(NB)]
        t_rope = [Trk() for _ in range(NB)]
        dtt = kb.sb([128, 16], F32, "dtt")
        t_dtt = Trk()

        def next_pmm():
            i = mmi[0] % 6
            mmi[0] += 1
            return pmm[i], t_pmm[i]

        def mm_tok(hTt, t_hTt, j, c0, c1):
            p, tp = next_pmm()
            n = c1 - c0
            fns = [lambda e, k=k: e.matmul(p[:, 0:n], lhsT=hTt[:, k, j * 128:(j + 1) * 128], rhs=W[:, k, c0:c1],
                                           start=(k == 0), stop=(k == 7)) for k in range(8)]
            kb.mm_group(fns, reads=[t_hTt] + t_W, writes=[tp])
            return p, tp

        def qk_norm(p, tp, nh, widx, dst, t_dst, rope, g, ob):
            n = nh * 64
            kb.op("act", lambda e: e.activation(out=sq[:, 0:n], in_=p[:, 0:n], func=AF.Square), reads=[tp], writes=[t_sq])
            kb.op("dve", lambda e: e.tensor_reduce(out=ssh[:, 0:nh], in_=sq[:, 0:n].rearrange("p (h e) -> p h e", e=64),
                                                   axis=AX.X, op=ALU.add), reads=[t_sq], writes=[t_ssh])
            kb.op("act", lambda e: e.activation(out=ssh[:, 0:nh], in_=ssh[:, 0:nh], func=AF.Sqrt, scale=1.0 / 64, bias=self.eps_t[:, 0:1]),
                  reads=[t_ssh, self.t_const], writes=[t_ssh])
            kb.op("dve", lambda e: e.reciprocal(out=rsh[:, 0:nh], in_=ssh[:, 0:nh]), reads=[t_ssh], writes=[t_ssh])
            p3 = p[:, 0:n].rearrange("p (h e) -> p h e", e=64)
            q3 = qn[:, 0:n].rearrange("p (h e) -> p h e", e=64)
            kb.op("dve", lambda e: e.tensor_tensor(out=q3, in0=p3, in1=_bc(rsh[:, 0:nh].unsqueeze(2), [128, nh, 64]), op=ALU.mult),
                  reads=[tp, t_ssh], writes=[t_qn])
            wb = _bc(qkw[:, widx:widx + 1, :], [128, nh, 64])
            if not rope:
                d3 = dst[:, 0:n].rearrange("p (h e) -> p h e", e=64)
                kb.op("dve", lambda e: e.tensor_tensor(out=d3, in0=q3, in1=wb, op=ALU.mult),
                      reads=[t_qn, t_small], writes=[t_dst])
                return
            kb.op("dve", lambda e: e.tensor_tensor(out=q3, in0=q3, in1=wb, op=ALU.mult), reads=[t_qn, t_small], writes=[t_qn])
            r3 = qr[:, 0:n].rearrange("p (h e) -> p h e", e=64)
            kb.op("dve", lambda e: e.tensor_tensor(out=r3, in0=q3, in1=_bc(rc[g][:, :].unsqueeze(1), [128, nh, 64]), op=ALU.mult),
                  reads=[t_qn, t_rope[g]], writes=[t_qr])
            q5 = qn[:, 0:n].rearrange("p (h a b i) -> p h a b i", a=2, b=2, i=16)
            s5 = rs[g][:, :].rearrange("p (a b i) -> p a b i", a=2, b=2, i=16)
            sw = kb.sb_sw
            w5 = sw[:, 0:n].rearrange("p (h a b i) -> p h a b i", a=2, b=2, i=16)
            for bsel in range(2):
                kb.op("dve", lambda e, bsel=bsel: e.tensor_tensor(
                    out=w5[:, :, :, bsel, :], in0=q5[:, :, :, 1 - bsel, :],
                    in1=_bc(s5[:, :, bsel, :].unsqueeze(1), [128, nh, 2, 16]), op=ALU.mult),
                    reads=[t_qn, t_rope[g]], writes=[self.t_sw])
            kb.op("dve", lambda e: e.tensor_tensor(out=dst[:, 0:n], in0=qr[:, 0:n], in1=sw[:, 0:n], op=ALU.add),
                  reads=[t_qr, self.t_sw], writes=[t_dst])

        kb.sb_sw = kb.sb([128, 512], F32, "sw")
        self.t_sw = Trk()
        self.eps_t = kb.sb([128, 1], F32, "eps_t")
        kb.op("pool", lambda e: e.memset(self.eps_t[:, :], EPS), writes=[self.t_const])

        gi = 0
        for T in range(NT):
            hTt, t_hTt = hT[T % 2], t_hT[T % 2]
            for j in range(4):
                g = gi % NB
                gi += 1
                t0 = T * 512 + j * 128
                kb.dma("sp", xt[g][:, :], x_src[t0:t0 + 128, :], writes=[t_xt[g]])
                kb.dma("sp", rc[g][:, :], self.rope_c[t0:t0 + 128, :], writes=[t_rope[g]])
                kb.dma("sp", rs[g][:, :], self.rope_s[t0:t0 + 128, :], writes=[t_rope[g]])
                kb.op("act", lambda e: e.activation(out=junk[:, :], in_=xt[g][:, :], func=AF.Square, accum_out=st[g][:, 0:1]),
                      reads=[t_xt[g]], writes=[t_junk, t_st[g]])
                kb.op("act", lambda e: e.activation(out=st[g][:, 1:2], in_=st[g][:, 0:1], func=AF.Sqrt, scale=1.0 / D, bias=self.eps_t[:, 0:1]),
                      reads=[t_st[g], self.t_const], writes=[t_st[g]])
                kb.op("dve", lambda e: e.reciprocal(out=st[g][:, 2:3], in_=st[g][:, 1:2]), reads=[t_st[g]], writes=[t_st[g]])
                kb.op("dve", lambda e: e.scalar_tensor_tensor(out=h32[:, :], in0=xt[g][:, :], scalar=st[g][:, 2:3], in1=self.A_t[:, :],
                                                              op0=ALU.mult, op1=ALU.mult),
                      reads=[t_xt[g], t_st[g], self.t_mod], writes=[t_h])
                kb.op("dve", lambda e: e.tensor_tensor(out=hb[:, :], in0=h32[:, :], in1=self.Sh_t[:, :], op=ALU.add),
                      reads=[t_h, self.t_mod], writes=[t_h])
                for half in range(2):
                    pp, tpp = ptp[half], t_ptp[half]
                    fns = [lambda e, q=q: e.transpose(pp[:, q, :], hb[:, (half * 4 + q) * 128:(half * 4 + q + 1) * 128], self.ident_bf[:, :])
                           for q in range(4)]
                    kb.mm_group(fns, reads=[t_h, self.t_const], writes=[tpp])
                    kb.op("act", lambda e: e.activation(out=hTt[:, half * 4:half * 4 + 4, j * 128:(j + 1) * 128], in_=pp[:, :, :], func=AF.Copy),
                          reads=[tpp], writes=[t_hTt])
                ob = g % NO
                p, tp = mm_tok(hTt, t_hTt, j, 0, 512)
                qk_norm(p, tp, 8, 0, o_qa[ob], t_o["qa"][ob], True, g, ob)
                kb.dma("pool", self.s_qa[t0:t0 + 128, :], o_qa[ob][:, :], reads=[t_o["qa"][ob]])
                p, tp = mm_tok(hTt, t_hTt, j, 512, 768)
                qk_norm(p, tp, 2, 1, o_ka[ob], t_o["ka"][ob], True, g, ob)
                kb.dma("pool", self.s_ka[t0:t0 + 128, :], o_ka[ob][:, :], reads=[t_o["ka"][ob]])
                kb.op("act", lambda e: e.activation(out=o_va[ob][:, :], in_=p[:, 128:256], func=AF.Copy), reads=[tp], writes=[t_o["va"][ob]])
                kb.dma("pool", self.s_va[t0:t0 + 128, :], o_va[ob][:, :], reads=[t_o["va"][ob]])
                for (c0, c1, o0) in ((1280, 1792, 0), (1792, 2048, 512)):
                    p, tp = mm_tok(hTt, t_hTt, j, c0, c1)
                    nh = (c1 - c0) // 64
                    qk_norm(p, tp, nh, 2, o_qb[ob][:, o0:o0 + nh * 64], t_o["qb"][ob], False, g, ob)
                kb.dma("pool", self.s_qb[t0:t0 + 128, :], o_qb[ob][:, :], reads=[t_o["qb"][ob]])
                for (c0, c1, o0) in ((2048, 2560, 0), (2560, 2816, 512)):
                    p, tp = mm_tok(hTt, t_hTt, j, c0, c1)
                    nh = (c1 - c0) // 64
                    qk_norm(p, tp, nh, 3, o_kb[ob][:, o0:o0 + nh * 64], t_o["kb"][ob], False, g, ob)
                kb.dma("pool", self.s_kb[t0:t0 + 128, :], o_kb[ob][:, :], reads=[t_o["kb"][ob]])
                for (c0, c1, o0) in ((2816, 3328, 0), (3328, 3584, 512)):
                    p, tp = mm_tok(hTt, t_hTt, j, c0, c1)
                    n = c1 - c0
                    kb.op("act", lambda e: e.activation(out=o_vb[ob][:, o0:o0 + n], in_=p[:, 0:n], func=AF.Copy), reads=[tp], writes=[t_o["vb"][ob]])
                kb.dma("pool", self.s_vb[t0:t0 + 128, :], o_vb[ob][:, :], reads=[t_o["vb"][ob]])
                p, tp = mm_tok(hTt, t_hTt, j, 3584, 3840)
                kb.op("act", lambda e: e.activation(out=o_gb[ob][:, :], in_=p[:, 0:256], func=AF.Silu), reads=[tp], writes=[t_o["gb"][ob]])
                kb.dma("pool", self.s_gb[t0:t0 + 128, :], o_gb[ob][:, :], reads=[t_o["gb"][ob]])
                p, tp = mm_tok(hTt, t_hTt, j, 4352, 4864)
                kb.op("act", lambda e: e.activation(out=o_zc[ob][:, :], in_=p[:, 0:512], func=AF.Silu), reads=[tp], writes=[t_o["zc"][ob]])
                kb.dma("pool", self.s_zc[t0:t0 + 128, :], o_zc[ob][:, :], reads=[t_o["zc"][ob]])
                p, tp = mm_tok(hTt, t_hTt, j, 5376, 5392)
                kb.op("dve", lambda e: e.tensor_tensor(out=dtt[:, :], in0=p[:, 0:16], in1=dtb[:, :], op=ALU.add), reads=[tp, t_small], writes=[t_dtt])
                kb.op("act", lambda e: e.activation(out=dtt[:, :], in_=dtt[:, :], func=AF.Exp), reads=[t_dtt], writes=[t_dtt])
                kb.op("act", lambda e: e.activation(out=o_dt[ob][:, :], in_=dtt[:, :], func=AF.Ln, bias=self.ones_f[:, 0:1], scale=1.0),
                      reads=[t_dtt, self.t_const], writes=[t_o["dt"][ob]])
                kb.dma("pool", self.s_dt[t0:t0 + 128, :], o_dt[ob][:, :], reads=[t_o["dt"][ob]])
            for cb in range(12):
                if cb < 4:
                    c0 = 3840 + cb * 128
                elif cb < 8:
                    c0 = 4864 + (cb - 4) * 128
                else:
                    c0 = 768 + (cb - 8) * 128
                p, tp = next_pmm()
                fns = [lambda e, k=k: e.matmul(p[:, :], lhsT=W[:, k, c0:c0 + 128], rhs=hTt[:, k, :], start=(k == 0), stop=(k == 7))
                       for k in range(8)]
                kb.mm_group(fns, reads=[t_hTt] + t_W, writes=[tp])
                ob = cb % NO
                if cb < 8:
                    kb.op("act", lambda e: e.activation(out=o_xbc[ob][:, :], in_=p[:, :], func=AF.Copy), reads=[tp], writes=[t_o["xbc"][ob]])
                    kb.dma("pool", self.s_xbc[cb * 128:(cb + 1) * 128, T * 512:(T + 1) * 512], o_xbc[ob][:, :], reads=[t_o["xbc"][ob]])
                else:
                    kb.op("act", lambda e: e.activation(out=o_xbc[ob][:, :], in_=p[:, :], func=AF.Silu), reads=[tp], writes=[t_o["xbc"][ob]])
                    kb.dma("pool", self.s_gaT[(cb - 8) * 128:(cb - 7) * 128, T * 512:(T + 1) * 512], o_xbc[ob][:, :], reads=[t_o["xbc"][ob]])


    def phase2a(self, l):
        kb, nc = self.kb, self.nc
        S = self.S
        NKT = S // 128
        NQB = S // 512
        kT2 = [kb.sb([128, S], BF16, "kT2") for _ in range(2)]
        v1 = kb.sb([128, NKT, 2, 65], BF16, "v1")
        kall = kb.sb([128, NKT, 128], BF16, "kall")
        kd = [kb.sb([128, 2, 2, 64], BF16, "kd") for _ in range(2)]
        t_kT, t_v1, t_kall = Trk(), Trk(), Trk()
        t_kd = [Trk(), Trk()]
        ptp = [kb.ps([128, 4, 128], BF16, "ptp") for _ in range(1)]
        t_ptp = [PTrk()]
        kb.op("pool", lambda e: e.memset(v1[:, :, :, :], 1.0), writes=[t_v1])
        CH = 8
        for kv in range(2):
            src = self.s_va[:, kv * 64:(kv + 1) * 64].rearrange("(n p) e -> p n e", p=128)
            for n0 in range(0, NKT, CH):
                kb.dma("sp", v1[:, n0:n0 + CH, kv, 0:64], src[:, n0:n0 + CH, :], writes=[t_v1])
        srck = self.s_ka.rearrange("(n p) c -> p n c", p=128)
        for n0 in range(0, NKT, CH):
            kb.dma("sp", kall[:, n0:n0 + CH, :], srck[:, n0:n0 + CH, :], writes=[t_kall])
        for n in range(NKT):
            b = n % 2
            kb.op("dve", lambda e: e.tensor_copy(out=kd[b][:, :, :, :],
                                                 in_=_bc(kall[:, n, :].rearrange("p (k e) -> p k e", e=64).unsqueeze(2), [128, 2, 2, 64])),
                  reads=[t_kall], writes=[t_kd[b]])
            fns = [lambda e, kv=kv: e.transpose(ptp[0][:, kv, :], kd[b][:, kv, :, :].rearrange("p a e -> p (a e)"), self.ident_bf[:, :])
                   for kv in range(2)]
            kb.mm_group(fns, reads=[t_kd[b], self.t_const], writes=[t_ptp[0]])
            for kv in range(2):
                kb.op("act", lambda e, kv=kv: e.activation(out=kT2[kv][:, n * 128:(n + 1) * 128], in_=ptp[0][:, kv, :], func=AF.Copy),
                      reads=[t_ptp[0]], writes=[t_kT])
        qsb = [kb.sb([128, 4, 512], BF16, "qsb") for _ in range(2)]
        t_qsb = [Trk(), Trk()]
        qT = [kb.sb([128, 4, 512], BF16, "qT") for _ in range(2)]
        t_qT = [Trk(), Trk()]
        gT = [kb.sb([64, 8, 512], BF16, "gT") for _ in range(2)]
        t_gT = [Trk(), Trk()]
        NP = 2
        psT = [kb.ps2(F32, "psT") for _ in range(NP)]
        t_psT = [PTrk() for _ in range(NP)]
        pT = [kb.sb([128, 2, 512], BF16, "pT") for _ in range(3)]
        t_pT = [[Trk(), Trk()] for _ in range(3)]
        pacc = [kb.ps([128, 512], F32, "pacc") for _ in range(2)]
        t_acc = [PTrk(), PTrk()]
        pbc = kb.ps([128, 512], F32, "pbc")
        t_bc = PTrk()
        rec = kb.sb([128, 2, 512], F32, "rec")
        t_rec = [Trk(), Trk()]
        t1 = [kb.sb([64, 512], F32, "t1") for _ in range(2)]
        t_t1 = [Trk(), Trk()]
        yT = [kb.sb([64, 512], BF16, "yT") for _ in range(2)]
        t_yT = [Trk(), Trk()]

        def load_q(qb):
            g = qb % 2
            kb.dma("sp", qsb[g][:, :, :], self.s_qa[qb * 512:(qb + 1) * 512, :].rearrange("(j p) c -> p j c", p=128), writes=[t_qsb[g]])
            kb.dma("sp", gT[g][:, :, :], self.s_gaT[:, qb * 512:(qb + 1) * 512].rearrange("(h e) t -> e h t", e=64), writes=[t_gT[g]])

        def prep_q(qb):
            g = qb % 2
            for j in range(4):
                fns = [lambda e, hp=hp: e.transpose(ptp[0][:, hp, :], qsb[g][:, j, hp * 128:(hp + 1) * 128], self.ident_bf[:, :])
                       for hp in range(4)]
                kb.mm_group(fns, reads=[t_qsb[g], self.t_const], writes=[t_ptp[0]])
                kb.op("dve", lambda e: e.tensor_copy(out=qT[g][:, :, j * 128:(j + 1) * 128], in_=ptp[0][:, :, :]),
                      reads=[t_ptp[0]], writes=[t_qT[g]])

        seq = [(qb, hp, kt) for qb in range(NQB) for hp in range(4) for kt in range(NKT)]

        def qk(i):
            qb, hp, kt = seq[i]
            g = qb % 2
            kv = hp // 2
            b = i % NP
            for half in range(2):
                pr = slice(half * 64, half * 64 + 64)
                kb.op("pe", lambda e: e.matmul(psT[b][:, half, :], lhsT=kT2[kv][pr, kt * 128:(kt + 1) * 128], rhs=qT[g][pr, hp, :], start=True, stop=True),
                      reads=[t_kT, t_qT[g]], writes=[t_psT[b]])

        pending = []

        def epilogue_a(qb, hp):
            for half in range(2):
                kb.op("dve", lambda e: e.reciprocal(out=rec[64:65, half, :], in_=pacc[half][64:65, :]), reads=[t_acc[half]], writes=[t_rec[half]])
                kb.op("dve", lambda e: e.tensor_tensor(out=t1[half][:, :], in0=pacc[half][0:64, :], in1=gT[qb % 2][:, 2 * hp + half, :], op=ALU.mult),
                      reads=[t_acc[half], t_gT[qb % 2]], writes=[t_t1[half]])

        def epilogue_b(qb, hp):
            for half in range(2):
                h = 2 * hp + half
                kb.op("pe", lambda e: e.matmul(pbc[0:64, :], lhsT=self.ones_f[64:65, 0:64], rhs=rec[64:65, half, :], start=True, stop=True),
                      reads=[t_rec[half], self.t_const], writes=[t_bc])
                kb.op("dve", lambda e: e.tensor_tensor(out=yT[half][:, :], in0=t1[half][:, :], in1=pbc[0:64, :], op=ALU.mult),
                      reads=[t_t1[half], t_bc], writes=[t_yT[half]])
                kb.dma("pool", self.s_yaT[h * 64:(h + 1) * 64, qb * 512:(qb + 1) * 512], yT[half][:, :], reads=[t_yT[half]])

        load_q(0)
        if NQB > 1:
            load_q(1)
        prep_q(0)
        N = len(seq)
        qk(0)
        for i in range(N):
            qb, hp, kt = seq[i]
            if i + 1 < N:
                qb2, hp2, kt2 = seq[i + 1]
                if qb2 != qb and hp2 == 0 and kt2 == 0:
                    prep_q(qb2)
                qk(i + 1)
            b = i % NP
            c = i % 3
            kv = hp // 2
            kb.op("act", lambda e: e.activation(out=pT[c][:, :, :], in_=psT[b][:, :, :], func=AF.Exp, scale=0.125),
                  reads=[t_psT[b]], writes=[t_pT[c][0], t_pT[c][1]])
            for half in range(2):
                kb.op("pe", lambda e: e.matmul(pacc[half][0:65, :], lhsT=v1[:, kt, kv, :], rhs=pT[c][:, half, :], start=(kt == 0), stop=(kt == NKT - 1)),
                      reads=[t_v1, t_pT[c][half]], writes=[t_acc[half]])
            if kt == NKT - 1:
                epilogue_a(qb, hp)
                pending.append((i + 2, qb, hp))
                if hp == 3 and qb + 2 < NQB:
                    load_q(qb + 2)
            while pending and (pending[0][0] <= i or i == N - 1):
                _, pq, ph = pending.pop(0)
                epilogue_b(pq, ph)

    def phase2b(self, l):
        kb, nc = self.kb, self.nc
        S = self.S
        E = kb.sb([128, 36, 128], F32, "E2b")
        msk = kb.sb([128, 3, 128], F32, "msk2b")
        t_E, t_msk = Trk(), Trk()
        for c in range(4):
            kb.dma("sp", E[:, c * 9:(c + 1) * 9, :], self.bias_tab[:, c * 9:(c + 1) * 9, :], writes=[t_E])
        kb.dma("sp", msk[:, :, :], self.mask_tab[:, :, :], writes=[t_msk])
        kb.op("act", lambda e: e.activation(out=E[:, :, :], in_=E[:, :, :], func=AF.Exp), reads=[t_E], writes=[t_E])
        for i in range(12):
            kb.op("dve", lambda e: e.tensor_tensor(out=E[:, i * 3:(i + 1) * 3, :], in0=E[:, i * 3:(i + 1) * 3, :], in1=msk[:, :, :], op=ALU.mult),
                  reads=[t_E, t_msk], writes=[t_E])
        MMAX = S
        NTMAX = MMAX // 128
        qT = kb.sb([128, 2, MMAX], BF16, "qT2b")
        kT = kb.sb([128, 2, MMAX], BF16, "kT2b")
        v1 = kb.sb([128, NTMAX, 4, 65], BF16, "v12b")
        t_qT, t_kT = Trk(), Trk()
        CH = 4
        t_v1 = [[Trk() for _ in range(4)] for _ in range(NTMAX // CH + 1)]
        t_v1_all = [t for lst in t_v1 for t in lst]
        qch = [kb.sb([128, CH, 256], BF16, "qch") for _ in range(2)]
        kch = [kb.sb([128, CH, 256], BF16, "kch") for _ in range(2)]
        t_qch, t_kch = [Trk(), Trk()], [Trk(), Trk()]
        ptp = [kb.ps([128, 4, 128], BF16, "ptp2b") for _ in range(2)]
        t_ptp = [PTrk(), PTrk()]
        psT = [[kb.ps([128, 2, 128], F32, "psT2b") for _ in range(2)] for _ in range(2)]
        t_psT = [[PTrk(), PTrk()], [PTrk(), PTrk()]]
        pacc = [kb.ps([128, 4, 65], F32, "pacc2b") for _ in range(2)]
        t_acc = [PTrk(), PTrk()]
        pe32 = [kb.sb([128, 4, 128], F32, "pe32") for _ in range(2)]
        t_pe32 = [Trk(), Trk()]
        pT = [kb.sb([128, 4, 128], BF16, "pT2b") for _ in range(2)]
        t_pT = [Trk(), Trk()]
        osb = [kb.sb([128, 4, 65], F32, "osb2b") for _ in range(2)]
        t_osb = [Trk(), Trk()]
        kb.op("pool", lambda e: e.memset(v1[:, :, :, :], 1.0), writes=t_v1_all)
        ci = 0
        si = 0
        ti = 0
        for g, d in enumerate(B_DIL):
            M = S // d
            NT = M // 128
            cols = slice(g * 256, (g + 1) * 256)
            for r in range(d):
                qv = self.s_qb.rearrange("(m d) c -> d m c", d=d)[r][:, cols].rearrange("(n p) c -> p n c", p=128)
                kv_ = self.s_kb.rearrange("(m d) c -> d m c", d=d)[r][:, cols].rearrange("(n p) c -> p n c", p=128)
                vv = self.s_vb.rearrange("(m d) c -> d m c", d=d)[r][:, cols].rearrange("(n p) (h e) -> p n h e", p=128, e=64)
                nv = self.s_nb[g].rearrange("(m d) c -> d m c", d=d)[r].rearrange("(n p) c -> p n c", p=128)
                for n0 in range(0, NT, CH):
                    n1 = min(NT, n0 + CH)
                    for hs in range(4):
                        kb.dma("sp", v1[:, n0:n1, hs, 0:64], vv[:, n0:n1, hs, :], writes=[t_v1[n0 // CH][hs]])
                    b = ci % 2
                    ci += 1
                    kb.dma("sp", qch[b][:, 0:n1 - n0, :], qv[:, n0:n1, :], writes=[t_qch[b]])
                    kb.dma("sp", kch[b][:, 0:n1 - n0, :], kv_[:, n0:n1, :], writes=[t_kch[b]])
                    for n in range(n0, n1):
                        pb = n % 2
                        fns = [lambda e, pr=pr: e.transpose(ptp[pb][:, pr, :], qch[b][:, n - n0, pr * 128:(pr + 1) * 128], self.ident_bf[:, :])
                               for pr in range(2)]
                        fns += [lambda e, pr=pr: e.transpose(ptp[pb][:, 2 + pr, :], kch[b][:, n - n0, pr * 128:(pr + 1) * 128], self.ident_bf[:, :])
                                for pr in range(2)]
                        kb.mm_group(fns, reads=[t_qch[b], t_kch[b], self.t_const], writes=[t_ptp[pb]])
                        kb.op("act", lambda e: e.activation(out=qT[:, :, n * 128:(n + 1) * 128], in_=ptp[pb][:, 0:2, :], func=AF.Copy),
                              reads=[t_ptp[pb]], writes=[t_qT])
                        kb.op("dve", lambda e: e.tensor_copy(out=kT[:, :, n * 128:(n + 1) * 128], in_=ptp[pb][:, 2:4, :]),
                              reads=[t_ptp[pb]], writes=[t_kT])
                steps = [(mt, oi, o) for mt in range(NT) for oi, o in enumerate([o for o in (-1, 0, 1) if 0 <= mt + o < NT])]
                nsteps = len(steps)
                sbase = si
                tbase = ti

                def qk2b(idx):
                    mt, oi, o = steps[idx]
                    kt = mt + o
                    b = (sbase + idx) % 2
                    fns = []
                    for hs in range(4):
                        pr, half = hs // 2, hs % 2
                        ps_ = slice(half * 64, half * 64 + 64)
                        fns.append(lambda e, hs=hs, pr=pr, ps_=ps_, half=half: e.matmul(
                            psT[b][half][:, pr, :], lhsT=kT[ps_, pr, kt * 128:(kt + 1) * 128],
                            rhs=qT[ps_, pr, mt * 128:(mt + 1) * 128], start=True, stop=True))
                    kb.mm_group(fns, reads=[t_qT, t_kT], writes=[t_psT[b][0], t_psT[b][1]])

                qk2b(0)
                for idx in range(nsteps):
                    mt, oi, o = steps[idx]
                    kt = mt + o
                    noff = len([o_ for o_ in (-1, 0, 1) if 0 <= mt + o_ < NT])
                    a = (tbase + mt) % 2
                    b = (sbase + idx) % 2
                    if idx + 1 < nsteps:
                        qk2b(idx + 1)
                    for half in range(2):
                        kb.op("act", lambda e, half=half: e.activation(out=pe32[b][:, half:4:2, :], in_=psT[b][half][:, :, :], func=AF.Exp, scale=0.125),
                              reads=[t_psT[b][half]], writes=[t_pe32[b]])
                    kb.op("dve", lambda e: e.tensor_tensor(out=pT[b][:, :, :], in0=pe32[b][:, :, :],
                                                           in1=E[:, g * 12 + (o + 1):g * 12 + 12:3, :], op=ALU.mult),
                          reads=[t_pe32[b], t_E], writes=[t_pT[b]])
                    fns = []
                    for hs in range(4):
                        fns.append(lambda e, hs=hs: e.matmul(pacc[a][:, hs, :], lhsT=pT[b][:, hs, :], rhs=v1[:, kt, hs, :],
                                                             start=(oi == 0 and hs == 0), stop=(oi == noff - 1),
                                                             skip_group_check=True))
                    kb.mm_group(fns, reads=[t_pT[b]] + t_v1[kt // CH], writes=[t_acc[a]])
                    if oi == noff - 1:
                        kb.op("act", lambda e: e.activation(out=osb[a][:, :, :], in_=pacc[a][:, :, :], func=AF.Copy), reads=[t_acc[a]], writes=[t_osb[a]])
                        kb.dma("pool", nv[:, mt, :], osb[a][:, :, :].rearrange("p h e -> p (h e)"), reads=[t_osb[a]])
                si += nsteps
                ti += NT
        kb.barrier()
        nb3 = [kb.sb([128, 3, 260], F32, "nb3") for _ in range(2)]
        gt = [kb.sb([128, 256], BF16, "gt2b") for _ in range(2)]
        t_nb3 = [Trk(), Trk()]
        sm = kb.sb([128, 4, 65], F32, "sm2b")
        rc = kb.sb([128, 4], F32, "rc2b")
        yb32 = kb.sb([128, 4, 64], F32, "yb32")
        ybb = kb.sb([128, 256], BF16, "ybb")
        t_sm = Trk()
        t_ybb = Trk()
        ybo = [kb.sb([128, 2, 128], BF16, "ybo") for _ in range(2)]
        t_ybo = [Trk(), Trk()]
        for T in range(S // 128):
            b = T % 2
            t0 = T * 128
            for g in range(3):
                kb.dma("sp", nb3[b][:, g, :], self.s_nb[g][t0:t0 + 128, :], writes=[t_nb3[b]])
            kb.dma("sp", gt[b][:, :], self.s_gb[t0:t0 + 128, :], writes=[t_nb3[b]])
            smf = sm[:, :, :].rearrange("p h e -> p (h e)")
            kb.op("dve", lambda e: e.tensor_tensor(out=smf, in0=nb3[b][:, 0, :], in1=nb3[b][:, 1, :], op=ALU.add), reads=[t_nb3[b]], writes=[t_sm])
            kb.op("dve", lambda e: e.tensor_tensor(out=smf, in0=smf, in1=nb3[b][:, 2, :], op=ALU.add), reads=[t_nb3[b], t_sm], writes=[t_sm])
            kb.op("dve", lambda e: e.reciprocal(out=rc[:, :], in_=sm[:, :, 64]), reads=[t_sm], writes=[t_sm])
            kb.op("dve", lambda e: e.tensor_tensor(out=yb32[:, :, :], in0=sm[:, :, 0:64], in1=_bc(rc[:, :].unsqueeze(2), [128, 4, 64]), op=ALU.mult),
                  reads=[t_sm], writes=[t_sm])
            kb.op("dve", lambda e: e.tensor_tensor(out=ybb[:, :], in0=yb32[:, :, :].rearrange("p h e -> p (h e)"), in1=gt[b][:, :], op=ALU.mult),
                  reads=[t_sm, t_nb3[b]], writes=[t_ybb])
            fns = [lambda e, pr=pr: e.transpose(ptp[0][:, pr, :], ybb[:, pr * 128:(pr + 1) * 128], self.ident_bf[:, :]) for pr in range(2)]
            kb.mm_group(fns, reads=[t_ybb, self.t_const], writes=[t_ptp[0]])
            kb.op("act", lambda e: e.activation(out=ybo[b][:, :, :], in_=ptp[0][:, 0:2, :], func=AF.Copy), reads=[t_ptp[0]], writes=[t_ybo[b]])
            kb.dma("pool", self.s_ybT[:, t0:t0 + 128].rearrange("(c p) t -> p c t", p=128), ybo[b][:, :, :], reads=[t_ybo[b]])

    def phase2c(self, l):
        kb, nc = self.kb, self.nc
        S = self.S
        NC = S // 128
        UT = [kb.sb([128, 128], F32, "UTf"), kb.sb([128, 128], F32, "UTb")]
        SM = [kb.sb([128, 128], F32, "Sf"), kb.sb([128, 128], F32, "Sb")]
        t_msk = Trk()
        for (tile_, pat, cm, cmp_) in ((UT[0], 1, -1, ALU.is_ge), (UT[1], -1, 1, ALU.is_ge), (SM[0], -1, 1, ALU.is_gt), (SM[1], 1, -1, ALU.is_gt)):
            kb.op("pool", lambda e: e.affine_select(out=tile_[:, :], in_=self.ones_f[:, :], pattern=[[pat, 128]], compare_op=cmp_, fill=0.0,
                                                    base=0, channel_multiplier=cm), reads=[self.t_const], writes=[t_msk])
        cw = kb.sb([128, 8, 5], F32, "cw")
        cb_ = kb.sb([128, 8], F32, "cb")
        Abc = kb.sb([128, 16], F32, "Abc")
        dsk = kb.sb([128, 8], F32, "dsk")
        nwc = kb.sb([128, 512], F32, "nwc")
        eps_t = kb.sb([128, 1], F32, "eps2c")
        t_par = Trk()
        kb.dma("sp", cw[:, :, :], self.conv_w[l], writes=[t_par])
        kb.dma("sp", cb_[:, :], self.conv_b[l], writes=[t_par])
        kb.dma("sp", Abc[:, :], self.a_log[l:l + 1, :].partition_broadcast(128), writes=[t_par])
        kb.dma("sp", dsk[:, :], self.d_skip[l:l + 1, :].partition_broadcast(128), writes=[t_par])
        kb.dma("sp", nwc[:, :], self.ssm_nw[l:l + 1, :].partition_broadcast(128), writes=[t_par])
        kb.op("act", lambda e: e.activation(out=Abc[:, :], in_=Abc[:, :], func=AF.Exp), reads=[t_par], writes=[t_par])
        kb.op("dve", lambda e: e.tensor_scalar(out=Abc[:, :], in0=Abc[:, :], scalar1=-1.0, scalar2=None, op0=ALU.mult), reads=[t_par], writes=[t_par])
        kb.op("dve", lambda e: e.memset(eps_t[:, :], EPS), writes=[t_par])
        TC = min(2048, S)
        with ExitStack() as es2:
            old_es = kb.es
            kb.es = es2
            xin = [kb.sb([128, S + 4], BF16, "xin") for _ in range(2)]
            t_xin = [Trk(), Trk()]
            cacc = kb.sb([128, TC], F32, "cacc")
            t_cacc = Trk()
            cout = [kb.sb([128, TC], BF16, "cout") for _ in range(2)]
            t_cout = [Trk(), Trk()]
            for b in range(2):
                kb.op("dve", lambda e: e.memset(xin[b][:, 0:2], 0.0), writes=[t_xin[b]])
                kb.op("dve", lambda e: e.memset(xin[b][:, S + 2:S + 4], 0.0), writes=[t_xin[b]])
            i = 0
            for cb in range(8):
                b = cb % 2
                for c0 in range(0, S, 2048):
                    c1 = min(S, c0 + 2048)
                    kb.dma("sp", xin[b][:, 2 + c0:2 + c1], self.s_xbc[cb * 128:(cb + 1) * 128, c0:c1], writes=[t_xin[b]])
                for c0 in range(0, S, TC):
                    o = i % 2
                    i += 1
                    kb.op("dve", lambda e: e.tensor_scalar(out=cacc[:, :], in0=xin[b][:, c0:c0 + TC], scalar1=cw[:, cb, 0:1], scalar2=None, op0=ALU.mult),
                          reads=[t_xin[b], t_par], writes=[t_cacc])
                    for j in range(1, 5):
                        kb.op("dve", lambda e: e.scalar_tensor_tensor(out=cacc[:, :], in0=xin[b][:, c0 + j:c0 + j + TC], scalar=cw[:, cb, j:j + 1], in1=cacc[:, :],
                                                                      op0=ALU.mult, op1=ALU.add),
                              reads=[t_xin[b], t_par, t_cacc], writes=[t_cacc])
                    kb.op("act", lambda e: e.activation(out=cout[o][:, :], in_=cacc[:, :], func=AF.Silu, bias=cb_[:, cb:cb + 1], scale=1.0),
                          reads=[t_cacc, t_par], writes=[t_cout[o]])
                    kb.dma("pool", self.s_xcv[cb * 128:(cb + 1) * 128, c0:c0 + TC], cout[o][:, :], reads=[t_cout[o]])
            kb.es = old_es
        kb.barrier()
        xT = [kb.sb([128, 4, 128], BF16, "xT2c") for _ in range(2)]
        BT = [kb.sb([128, 2, 128], BF16, "BT2c") for _ in range(2)]
        CT = [kb.sb([128, 2, 128], BF16, "CT2c") for _ in range(2)]
        dtt = [kb.sb([128, 16], F32, "dt2c") for _ in range(2)]
        zt = [kb.sb([128, 512], BF16, "z2c") for _ in range(2)]
        ybl = [kb.sb([128, 512], F32, "ybl2c") for _ in range(2)]
        t_ld = [[Trk() for _ in range(6)], [Trk() for _ in range(6)]]
        ptxb = kb.ps([128, 6, 128], BF16, "ptxb")
        t_ptxb = PTrk()
        pcum = kb.ps([128, 16], F32, "pcum")
        t_pcum = PTrk()
        pGT = kb.ps([128, 2, 128], F32, "pGT")
        t_pGT = PTrk()
        pseg = [kb.ps([128, 4, 128], F32, "pseg") for _ in range(2)]
        t_pseg = [PTrk(), PTrk()]
        pd = kb.ps([128, 8, 64], F32, "pd")
        po = kb.ps([128, 8, 64], F32, "po")
        pst = kb.ps([128, 8, 64], F32, "pst")
        t_pd, t_po, t_pst = PTrk(), PTrk(), PTrk()
        Btok = kb.sb([128, 2, 128], BF16, "Btok")
        t_Btok = Trk()
        xdt = kb.sb([128, 8, 64], BF16, "xdt")
        xdd = kb.sb([128, 8, 64], BF16, "xdd")
        t_xdt, t_xdd = Trk(), Trk()
        a_t = kb.sb([128, 8], F32, "a_t")
        t_a = Trk()
        ecs = kb.sb([128, 16], F32, "ecs")
        t_ecs = Trk()
        Amat = [kb.sb([128, 128], F32, "Amat") for _ in range(4)]
        t_Amat = [Trk() for _ in range(4)]
        LT = [kb.sb([128, 4, 128], F32, "LT") for _ in range(2)]
        t_LT = [Trk(), Trk()]
        GTm = kb.sb([128, 2, 128], F32, "GTm")
        t_GTm = Trk()
        MT = [kb.sb([128, 4, 128], BF16, "MT") for _ in range(2)]
        t_MT = [Trk(), Trk()]
        S32 = kb.sb([128, 8, 64], F32, "S32")
        Sst = kb.sb([128, 8, 64], BF16, "Sst")
        t_S32, t_Sst = Trk(), Trk()
        ytmp = kb.sb([128, 8, 64], F32, "ytmp")
        yacc = [kb.sb([128, 8, 64], F32, "yacc") for _ in range(2)]
        t_ytmp = Trk()
        t_yacc = [Trk(), Trk()]
        u32 = kb.sb([128, 512], F32, "u32")
        junk = kb.sb([128, 512], BF16, "junk2c")
        stt = kb.sb([128, 4], F32, "stt2c")
        t_u, t_junk, t_stt = Trk(), Trk(), Trk()
        ycb = kb.sb([128, 512], BF16, "ycb")
        t_ycb = Trk()
        yco = [kb.sb([128, 4, 128], BF16, "yco") for _ in range(2)]
        t_yco = [Trk(), Trk()]

        for dr in (1, 0):
            final = (dr == 0)
            order = list(range(NC)) if dr == 0 else list(range(NC - 1, -1, -1))
            kb.op("dve", lambda e: e.memset(S32[:, :, :], 0.0), writes=[t_S32])
            kb.op("dve", lambda e: e.memset(Sst[:, :, :], 0.0), writes=[t_Sst])
            dcol = 127 if dr == 0 else 0

            def load(ci):
                c = order[ci]
                b = ci % 2
                t0 = c * 128
                kb.dma("sp", xT[b][:, :, :], self.s_xcv[0:512, t0:t0 + 128].rearrange("(c p) t -> p c t", p=128), writes=[t_ld[b][0]])
                kb.dma("sp", BT[b][:, :, :], self.s_xcv[512:768, t0:t0 + 128].rearrange("(c p) t -> p c t", p=128), writes=[t_ld[b][1]])
                kb.dma("sp", CT[b][:, :, :], self.s_xcv[768:1024, t0:t0 + 128].rearrange("(c p) t -> p c t", p=128), writes=[t_ld[b][2]])
                kb.dma("sp", dtt[b][:, :], self.s_dt[t0:t0 + 128, :], writes=[t_ld[b][3]])
                if final:
                    kb.dma("sp", zt[b][:, :], self.s_zc[t0:t0 + 128, :], writes=[t_ld[b][4]])
                    kb.dma("sp", ybl[b][:, :], self.s_yb32[t0:t0 + 128, :], writes=[t_ld[b][5]])

            load(0)
            for ci in range(NC):
                c = order[ci]
                b = ci % 2
                t0 = c * 128
                if ci + 1 < NC:
                    load(ci + 1)
                fns = [lambda e, q=q: e.transpose(ptxb[:, q, :], xT[b][:, q, :], self.ident_bf[:, :]) for q in range(4)]
                fns += [lambda e, q=q: e.transpose(ptxb[:, 4 + q, :], BT[b][:, q, :], self.ident_bf[:, :]) for q in range(2)]
                kb.mm_group(fns, reads=[*t_ld[b], self.t_const], writes=[t_ptxb])
                xtok = ptxb[:, 0:4, :].rearrange("p c (h e) -> p (c h) e", e=64)
                kb.op("dve", lambda e: e.tensor_tensor(out=xdt[:, :, :], in0=xtok, in1=_bc(dtt[b][:, dr * 8:dr * 8 + 8].unsqueeze(2), [128, 8, 64]), op=ALU.mult),
                      reads=[t_ptxb, *t_ld[b]], writes=[t_xdt])
                kb.op("act", lambda e: e.activation(out=Btok[:, :, :], in_=ptxb[:, 4:6, :], func=AF.Copy), reads=[t_ptxb], writes=[t_Btok])
                kb.op("dve", lambda e: e.tensor_tensor(out=a_t[:, :], in0=dtt[b][:, dr * 8:dr * 8 + 8], in1=Abc[:, dr * 8:dr * 8 + 8], op=ALU.mult),
                      reads=[*t_ld[b], t_par], writes=[t_a])
                kb.mm_group([lambda e: e.matmul(pcum[:, 0:8], lhsT=UT[dr][:, :], rhs=a_t[:, :], start=True, stop=True),
                             lambda e: e.matmul(pcum[:, 8:16], lhsT=self.ones_f[:, :], rhs=a_t[:, :], start=True, stop=True)],
                            reads=[t_a, t_msk, self.t_const], writes=[t_pcum])
                kb.op("act", lambda e: e.activation(out=ecs[:, :], in_=pcum[:, :], func=AF.Exp), reads=[t_pcum], writes=[t_ecs])
                kb.mm_group([lambda e, gq=gq: e.matmul(pGT[:, gq, :], lhsT=BT[b][:, gq, :], rhs=CT[b][:, gq, :], start=True, stop=True) for gq in range(2)],
                            reads=[*t_ld[b]], writes=[t_pGT])
                kb.op("dve", lambda e: e.tensor_tensor(out=GTm[:, :, :], in0=pGT[:, :, :], in1=_bc(UT[dr][:, :].unsqueeze(1), [128, 2, 128]), op=ALU.mult),
                      reads=[t_pGT, t_msk], writes=[t_GTm])
                for hg in range(2):
                    for hh in range(4):
                        h = hg * 4 + hh
                        kb.op("dve", lambda e: e.tensor_scalar(out=Amat[hh][:, :], in0=SM[dr][:, :], scalar1=a_t[:, h:h + 1], scalar2=None, op0=ALU.mult),
                              reads=[t_a, t_msk], writes=[t_Amat[hh]])
                        kb.op("pe", lambda e: e.matmul(pseg[hg][:, hh, :], lhsT=Amat[hh][:, :], rhs=UT[dr][:, :], start=True, stop=True),
                              reads=[t_Amat[hh], t_msk], writes=[t_pseg[hg]])
                    kb.op("act", lambda e: e.activation(out=LT[hg][:, :, :], in_=pseg[hg][:, :, :], func=AF.Exp), reads=[t_pseg[hg]], writes=[t_LT[hg]])
                    kb.op("dve", lambda e: e.tensor_tensor(out=MT[hg][:, :, :], in0=LT[hg][:, :, :], in1=_bc(GTm[:, hg:hg + 1, :], [128, 4, 128]), op=ALU.mult),
                          reads=[t_LT[hg], t_GTm], writes=[t_MT[hg]])
                    kb.op("dve", lambda e: e.tensor_tensor(out=xdd[:, hg * 4:hg * 4 + 4, :], in0=xdt[:, hg * 4:hg * 4 + 4, :],
                                                           in1=_bc(LT[hg][:, :, dcol:dcol + 1], [128, 4, 64]), op=ALU.mult),
                          reads=[t_xdt, t_LT[hg]], writes=[t_xdd])
                kb.mm_group([lambda e, h=h: e.matmul(pd[:, h, :], lhsT=MT[h // 4][:, h % 4, :], rhs=xdt[:, h, :], start=True, stop=True) for h in range(8)],
                            reads=[t_MT[0], t_MT[1], t_xdt], writes=[t_pd])
                kb.mm_group([lambda e, h=h: e.matmul(po[:, h, :], lhsT=CT[b][:, h // 4, :], rhs=Sst[:, h, :], start=True, stop=True) for h in range(8)],
                            reads=[*t_ld[b], t_Sst], writes=[t_po])
                kb.mm_group([lambda e, h=h: e.matmul(pst[:, h, :], lhsT=Btok[:, h // 4, :], rhs=xdd[:, h, :], start=True, stop=True) for h in range(8)],
                            reads=[t_Btok, t_xdd], writes=[t_pst])
                ya = yacc[ci % 2]
                t_ya = t_yacc[ci % 2]
                kb.op("dve", lambda e: e.tensor_tensor(out=ytmp[:, :, :], in0=po[:, :, :], in1=_bc(ecs[:, 0:8].unsqueeze(2), [128, 8, 64]), op=ALU.mult),
                      reads=[t_po, t_ecs], writes=[t_ytmp])
                kb.op("dve", lambda e: e.tensor_tensor(out=ya[:, :, :], in0=ytmp[:, :, :], in1=pd[:, :, :], op=ALU.add),
                      reads=[t_pd, t_ytmp], writes=[t_ya])
                kb.op("dve", lambda e: e.tensor_tensor(out=S32[:, :, :], in0=S32[:, :, :], in1=_bc(ecs[:, 8:16].unsqueeze(2), [128, 8, 64]), op=ALU.mult),
                      reads=[t_ecs, t_S32], writes=[t_S32])
                kb.op("dve", lambda e: e.tensor_tensor(out=S32[:, :, :], in0=S32[:, :, :], in1=pst[:, :, :], op=ALU.add),
                      reads=[t_pst, t_S32], writes=[t_S32])
                kb.op("act", lambda e: e.activation(out=Sst[:, :, :], in_=S32[:, :, :], func=AF.Copy), reads=[t_S32], writes=[t_Sst])
                yaf = ya[:, :, :].rearrange("p h e -> p (h e)")
                if not final:
                    kb.dma("pool", self.s_yb32[t0:t0 + 128, :], yaf, reads=[t_ya])
                    continue
                kb.op("dve", lambda e: e.tensor_tensor(out=ytmp[:, :, :], in0=xtok, in1=_bc(dsk[:, :].unsqueeze(2), [128, 8, 64]), op=ALU.mult),
                      reads=[t_ptxb, t_par], writes=[t_ytmp])
                kb.op("dve", lambda e: e.tensor_tensor(out=yaf, in0=yaf, in1=ytmp[:, :, :].rearrange("p h e -> p (h e)"), op=ALU.add),
                      reads=[t_ytmp, t_ya], writes=[t_ya])
                kb.op("dve", lambda e: e.tensor_tensor(out=yaf, in0=yaf, in1=ybl[b][:, :], op=ALU.add), reads=[*t_ld[b], t_ya], writes=[t_ya])
                kb.op("dve", lambda e: e.tensor_tensor(out=u32[:, :], in0=yaf, in1=zt[b][:, :], op=ALU.mult), reads=[*t_ld[b], t_ya], writes=[t_u])
                kb.op("act", lambda e: e.activation(out=junk[:, :], in_=u32[:, :], func=AF.Square, accum_out=stt[:, 0:1]), reads=[t_u], writes=[t_junk, t_stt])
                kb.op("act", lambda e: e.activation(out=stt[:, 1:2], in_=stt[:, 0:1], func=AF.Sqrt, scale=1.0 / 512, bias=eps_t[:, 0:1]),
                      reads=[t_stt, t_par], writes=[t_stt])
                kb.op("dve", lambda e: e.reciprocal(out=stt[:, 2:3], in_=stt[:, 1:2]), reads=[t_stt], writes=[t_stt])
                kb.op("dve", lambda e: e.scalar_tensor_tensor(out=ycb[:, :], in0=u32[:, :], scalar=stt[:, 2:3], in1=nwc[:, :], op0=ALU.mult, op1=ALU.mult),
                      reads=[t_u, t_stt, t_par], writes=[t_ycb])
                fns = [lambda e, q=q: e.transpose(ptxb[:, q, :], ycb[:, q * 128:(q + 1) * 128], self.ident_bf[:, :]) for q in range(4)]
                kb.mm_group(fns, reads=[t_ycb, self.t_const], writes=[t_ptxb])
                kb.op("act", lambda e: e.activation(out=yco[b][:, :, :], in_=ptxb[:, 0:4, :], func=AF.Copy), reads=[t_ptxb], writes=[t_yco[b]])
                kb.dma("pool", self.s_ycT[:, t0:t0 + 128].rearrange("(c p) t -> p c t", p=128), yco[b][:, :, :], reads=[t_yco[b]])
            kb.barrier()

    def phase3(self, l, x_src):
        kb, nc = self.kb, self.nc
        S = self.S
        NT = S // 128
        br = [b for b in ("2a", "2b", "2c") if b in self.phases]
        Wg = kb.sb([128, 8, 3 * D], BF16, "Wg")
        Wo = kb.sb([128, 8, D], BF16, "Wo")
        Wa = kb.sb([128, 4, D], BF16, "Wa")
        Wb = kb.sb([128, 2, D], BF16, "Wb")
        Wc = kb.sb([128, 4, D], BF16, "Wc")
        bg = kb.sb([1, 3 * D], BF16, "bg")
        ones_b = kb.sb([1, 128], BF16, "ones_b")
        t_W = Trk()
        wv = self.w_in[l].rearrange("(k p) n -> p k n", p=128)
        for k in range(8):
            kb.dma("pool", Wg[:, k, :], wv[:, k, NCOL1:D_IN], writes=[t_W])
        kb.dma("pool", Wo[:, :, :], self.w_out[l].rearrange("(k p) n -> p k n", p=128), writes=[t_W])
        kb.dma("pool", Wa[:, :, :], self.w_pa[l].rearrange("(k p) n -> p k n", p=128), writes=[t_W])
        kb.dma("pool", Wb[:, :, :], self.w_pb[l].rearrange("(k p) n -> p k n", p=128), writes=[t_W])
        kb.dma("pool", Wc[:, :, :], self.w_pc[l].rearrange("(k p) n -> p k n", p=128), writes=[t_W])
        kb.dma("pool", bg[:, :], self.b_gate[l:l + 1, :], writes=[t_W])
        kb.op("dve", lambda e: e.memset(ones_b[:, :], 1.0), writes=[t_W])
        eps_t = kb.sb([128, 1], F32, "eps3")
        kb.op("dve", lambda e: e.memset(eps_t[:, :], EPS), writes=[t_W])

        NB = 2
        xt = [kb.sb([128, D], F32, "xt3") for _ in range(NB)]
        t_xt = [Trk() for _ in range(NB)]
        yaT = [kb.sb([128, 4, 128], BF16, "yaT3") for _ in range(NB)]
        ybT = [kb.sb([128, 2, 128], BF16, "ybT3") for _ in range(NB)]
        ycT = [kb.sb([128, 4, 128], BF16, "ycT3") for _ in range(NB)]
        t_y = [[Trk() for _ in range(3)] for _ in range(NB)]
        junk = kb.sb([128, D], BF16, "junk3")
        t_junk = Trk()
        st = kb.sb([128, 8], F32, "st3")
        t_st = Trk()
        h32 = kb.sb([128, D], F32, "h32_3")
        hb = kb.sb([128, D], BF16, "hb3")
        t_h = Trk()
        hT = kb.sb([128, 8, 128], BF16, "hT3")
        t_hT = Trk()
        ptp = [kb.ps([128, 4, 128], BF16, "ptp3") for _ in range(2)]
        t_ptp = [PTrk(), PTrk()]
        pg = [kb.ps([128, 512], F32, "pg3") for _ in range(2)]
        t_pg = [PTrk(), PTrk()]
        pp = [kb.ps([128, 512], F32, "pp3") for _ in range(2)]
        t_pp = [PTrk(), PTrk()]
        po = [kb.ps([128, 512], F32, "po3") for _ in range(2)]
        t_po = [PTrk(), PTrk()]
        sg = [kb.sb([128, 512], F32, "sg3") for _ in range(2)]
        t_sg = [Trk(), Trk()]
        merged = kb.sb([128, D], F32, "merged3")
        tmpm = kb.sb([128, 512], F32, "tmpm3")
        t_tmpm = Trk()
        t_merged = [Trk(), Trk()]
        mb = kb.sb([128, D], BF16, "mb3")
        t_mb = Trk()
        mT = kb.sb([128, 8, 128], BF16, "mT3")
        t_mT = Trk()
        xn = [kb.sb([128, D], F32, "xn3") for _ in range(2)]
        t_xn = [Trk(), Trk()]
        ci = 0
        for T in range(NT):
            g = T % NB
            t0 = T * 128
            kb.dma("sp", xt[g][:, :], x_src[t0:t0 + 128, :], writes=[t_xt[g]])
            if "2a" in br:
                kb.dma("sp", yaT[g][:, :, :], self.s_yaT[:, t0:t0 + 128].rearrange("(c p) t -> p c t", p=128), writes=[t_y[g][0]])
            if "2b" in br:
                kb.dma("sp", ybT[g][:, :, :], self.s_ybT[:, t0:t0 + 128].rearrange("(c p) t -> p c t", p=128), writes=[t_y[g][1]])
            if "2c" in br:
                kb.dma("sp", ycT[g][:, :, :], self.s_ycT[:, t0:t0 + 128].rearrange("(c p) t -> p c t", p=128), writes=[t_y[g][2]])
            kb.op("act", lambda e: e.activation(out=junk[:, :], in_=xt[g][:, :], func=AF.Square, accum_out=st[:, 0:1]),
                  reads=[t_xt[g]], writes=[t_junk, t_st])
            kb.op("act", lambda e: e.activation(out=st[:, 1:2], in_=st[:, 0:1], func=AF.Sqrt, scale=1.0 / D, bias=eps_t[:, 0:1]),
                  reads=[t_st, t_W], writes=[t_st])
            kb.op("dve", lambda e: e.reciprocal(out=st[:, 2:3], in_=st[:, 1:2]), reads=[t_st], writes=[t_st])
            kb.op("dve", lambda e: e.scalar_tensor_tensor(out=h32[:, :], in0=xt[g][:, :], scalar=st[:, 2:3], in1=self.A_t[:, :],
                                                          op0=ALU.mult, op1=ALU.mult),
                  reads=[t_xt[g], t_st, self.t_mod], writes=[t_h])
            kb.op("dve", lambda e: e.tensor_tensor(out=hb[:, :], in0=h32[:, :], in1=self.Sh_t[:, :], op=ALU.add),
                  reads=[t_h, self.t_mod], writes=[t_h])
            for half in range(2):
                pq, tpq = ptp[half], t_ptp[half]
                fns = [lambda e, q=q: e.transpose(pq[:, q, :], hb[:, (half * 4 + q) * 128:(half * 4 + q + 1) * 128], self.ident_bf[:, :])
                       for q in range(4)]
                kb.mm_group(fns, reads=[t_h, self.t_const], writes=[tpq])
                kb.op("act", lambda e: e.activation(out=hT[:, half * 4:half * 4 + 4, :], in_=pq[:, :, :], func=AF.Copy),
                      reads=[tpq], writes=[t_hT])
            for half in range(2):
                cs = slice(half * 512, (half + 1) * 512)
                first = True
                for (bn, gi, Wx, yx, nk) in (("2a", 0, Wa, yaT, 4), ("2b", 1, Wb, ybT, 2), ("2c", 2, Wc, ycT, 4)):
                    if bn not in br:
                        continue
                    b = ci % 2
                    ci += 1
                    fns = [lambda e, k=k: e.matmul(pg[b][:, :], lhsT=hT[:, k, :], rhs=Wg[:, k, gi * D + half * 512:gi * D + (half + 1) * 512],
                                                   start=(k == 0), stop=False) for k in range(8)]
                    fns.append(lambda e: e.matmul(pg[b][:, :], lhsT=ones_b[0:1, :], rhs=bg[0:1, gi * D + half * 512:gi * D + (half + 1) * 512],
                                                  start=False, stop=True))
                    kb.mm_group(fns, reads=[t_hT, t_W], writes=[t_pg[b]])
                    kb.op("act", lambda e: e.activation(out=sg[b][:, :], in_=pg[b][:, :], func=AF.Sigmoid), reads=[t_pg[b]], writes=[t_sg[b]])
                    fns = [lambda e, k=k: e.matmul(pp[b][:, :], lhsT=yx[g][:, k, :], rhs=Wx[:, k, cs], start=(k == 0), stop=(k == nk - 1))
                           for k in range(nk)]
                    kb.mm_group(fns, reads=[*t_y[g], t_W], writes=[t_pp[b]])
                    if first:
                        kb.op("dve", lambda e: e.tensor_tensor(out=merged[:, cs], in0=pp[b][:, :], in1=sg[b][:, :], op=ALU.mult),
                              reads=[t_pp[b], t_sg[b]], writes=[t_merged[half]])
                        first = False
                    else:
                        kb.op("dve", lambda e: e.tensor_tensor(out=tmpm[:, :], in0=pp[b][:, :], in1=sg[b][:, :], op=ALU.mult),
                              reads=[t_pp[b], t_sg[b]], writes=[t_tmpm])
                        kb.op("dve", lambda e: e.tensor_tensor(out=merged[:, cs], in0=merged[:, cs], in1=tmpm[:, :], op=ALU.add),
                              reads=[t_tmpm, t_merged[half]], writes=[t_merged[half]])
                kb.op("act", lambda e: e.activation(out=mb[:, cs], in_=merged[:, cs], func=AF.Copy), reads=[t_merged[half]], writes=[t_mb])
                pq, tpq = ptp[half], t_ptp[half]
                fns = [lambda e, q=q: e.transpose(pq[:, q, :], mb[:, (half * 4 + q) * 128:(half * 4 + q + 1) * 128], self.ident_bf[:, :])
                       for q in range(4)]
                kb.mm_group(fns, reads=[t_mb, self.t_const], writes=[tpq])
                kb.op("act", lambda e: e.activation(out=mT[:, half * 4:half * 4 + 4, :], in_=pq[:, :, :], func=AF.Copy),
                      reads=[tpq], writes=[t_mT])
            o = T % 2
            for half in range(2):
                cs = slice(half * 512, (half + 1) * 512)
                fns = [lambda e, k=k: e.matmul(po[half][:, :], lhsT=mT[:, k, :], rhs=Wo[:, k, cs], start=(k == 0), stop=(k == 7)) for k in range(8)]
                kb.mm_group(fns, reads=[t_mT, t_W], writes=[t_po[half]])
                kb.op("dve", lambda e: e.tensor_tensor(out=xn[o][:, cs], in0=po[half][:, :], in1=self.G_t[:, cs], op=ALU.mult),
                      reads=[t_po[half], self.t_mod], writes=[t_xn[o]])
            kb.op("dve", lambda e: e.tensor_tensor(out=xn[o][:, :], in0=xn[o][:, :], in1=xt[g][:, :], op=ALU.add),
                  reads=[t_xt[g], t_xn[o]], writes=[t_xn[o]])
            kb.dma("pool", self.out[t0:t0 + 128, :], xn[o][:, :], reads=[t_xn[o]])

def rope_tables(S):
    t = np.arange(S)
    row = (t // GRID_W).astype(np.float32)
    col = (t % GRID_W).astype(np.float32)
    quarter = 16
    freqs = (10000.0 ** (-np.arange(quarter, dtype=np.float32) / quarter)).astype(np.float32)
    C = np.zeros((S, 64), np.float32)
    Sg = np.zeros((S, 64), np.float32)
    for hi, pos in enumerate((row, col)):
        ang = (pos[:, None] * freqs[None, :]).astype(np.float32)
        c, s = np.cos(ang), np.sin(ang)
        C[:, hi * 32:hi * 32 + 16] = c
        C[:, hi * 32 + 16:hi * 32 + 32] = c
        Sg[:, hi * 32:hi * 32 + 16] = -s
        Sg[:, hi * 32 + 16:hi * 32 + 32] = s
    return C, Sg


def t5_bucket_np(rel):
    nb = 16
    max_exact = 8
    ret = np.where(rel > 0, nb, 0)
    n = np.abs(rel)
    nf = np.maximum(n, 1).astype(np.float32)
    large = max_exact + (np.log(nf / np.float32(max_exact)) / np.float32(math.log(1024 / max_exact))
                         * np.float32(nb - max_exact)).astype(np.int32)
    large = np.minimum(large, nb - 1)
    return ret + np.where(n < max_exact, n, large)


def b_tables(rel_bias):
    rel_bias = np.asarray(rel_bias, np.float32)
    k = np.arange(128)[:, None]
    q = np.arange(128)[None, :]
    bias = np.zeros((3, 4, 3, 128, 128), np.float32)
    mask = np.zeros((3, 128, 128), np.float32)
    for oi, o in enumerate((-1, 0, 1)):
        relp = 128 * o + k - q
        mask[oi] = (np.abs(relp) <= 64).astype(np.float32)
        for g, d in enumerate(B_DIL):
            bk = t5_bucket_np(np.clip(relp, -64, 64) * d)
            for hs in range(4):
                bias[g, hs, oi] = rel_bias[bk, g * 4 + hs]
    bias_t = np.ascontiguousarray(bias.reshape(36, 128, 128).transpose(1, 0, 2))
    mask_t = np.ascontiguousarray(mask.transpose(1, 0, 2))
    return bias_t, mask_t


def make_inputs(inp, b, S, L):
    f = lambda a: np.ascontiguousarray(np.asarray(a), dtype=np.float32)
    C, Sg = rope_tables(S)
    BT, MT = b_tables(inp["rel_bias"])
    m = {
        "x": f(inp["x"][b][:S]),
        "c": f(np.asarray(inp["c"][b]).reshape(8, 128).T),
        "norm_w": f(inp["norm_w"][:L]),
        "w_ada": f(inp["w_ada"][:L]),
        "b_ada": f(inp["b_ada"][:L]),
        "w_in": f(inp["w_in"][:L]),
        "b_gate": f(inp["b_gate"][:L]),
        "qk_w": f(np.stack([inp["q_norm_a"][:L], inp["k_norm_a"][:L], inp["q_norm_b"][:L], inp["k_norm_b"][:L]], axis=1)),
        "rope_c": C, "rope_s": Sg,
        "dt_bias": f(np.asarray(inp["dt_bias"][:L]).reshape(L, 16)),
        "bias_tab": BT, "mask_tab": MT,
        "conv_w": f(np.asarray(inp["conv_w"][:L]).reshape(L, 5, 8, 128).transpose(0, 3, 2, 1)),
        "conv_b": f(np.asarray(inp["conv_b"][:L]).reshape(L, 8, 128).transpose(0, 2, 1)),
        "a_log": f(np.asarray(inp["a_log"][:L]).reshape(L, 16)),
        "d_skip": f(inp["d_skip"][:L]), "ssm_norm_w": f(inp["ssm_norm_w"][:L]),
        "w_out": f(inp["w_out"][:L]), "w_proj_a": f(inp["w_proj_a"][:L]),
        "w_proj_b": f(inp["w_proj_b"][:L]), "w_proj_c": f(inp["w_proj_c"][:L]),
    }
    return m


_PROG = {}


def kernel(**inputs):
    S, L = 8192, DEPTH
    key = (S, L)
    if key not in _PROG:
        _PROG[key] = Prog(S, L)
    prog = _PROG[key]
    in_maps = [make_inputs(inputs, b, S, L) for b in range(4)]
    res = run_bass_kernel_spmd(prog.nc, in_maps, core_ids=list(range(4)))
    return np.stack([np.asarray(r["out"]) for r in res.results], axis=0).astype(np.float32)
```
